# Optimizing a Trainium2 kernel written in Bass

```python
import jax, jax.numpy as jnp
from jax import lax
import numpy as np

D_MODEL = 2048
BATCH = 1
SEQ = 8192
DEPTH = 4
DEC_BATCH = 4
DEC_SEQ = 4096
PAST_LEN = 128

N_META = 16
LEAD = 128
PAD = LEAD - N_META
D_MIX = D_MODEL
CONV_W = D_MIX // 4
ATT_HD = 128
ATT_HQ = 6
ATT_KV = 2
ATT_G = ATT_HQ // ATT_KV
ATT_Q = ATT_HQ * ATT_HD
ATT_KVW = ATT_KV * ATT_HD
WINDOW = 128
BLK = 128
DN_H = 6
DN_DK = 128
DN_DV = 128
DN_W = DN_H * DN_DV
DN_CHUNK = 64
EPS = 1e-6
SIZES = (CONV_W, CONV_W, CONV_W, CONV_W,
         ATT_Q, ATT_KVW, ATT_KVW, ATT_Q,
         DN_H * DN_DK, DN_H * DN_DK, DN_W, DN_W,
         2 * DN_H, 2 * DN_H)
N_PROJ = 4 * CONV_W + 2 * ATT_Q + 2 * ATT_KVW + 2 * DN_H * DN_DK + 2 * DN_W + 4 * DN_H

kernel_name = "hymba_conv_swa_gdn_bidir_encoder"


def _rmsnorm(x, w):
    xf = x.astype(jnp.float32)
    y = xf * lax.rsqrt(jnp.mean(xf * xf, axis=-1, keepdims=True) + EPS)
    return (y * w.astype(jnp.float32)).astype(x.dtype)


def _l2norm(x):
    return x * lax.rsqrt(jnp.sum(x * x, axis=-1, keepdims=True) + EPS)


def _conv3(x, w):
    xp = jnp.pad(x, ((0, 0), (1, 1), (0, 0)))
    return xp[:, :-2] * w[0] + xp[:, 1:-1] * w[1] + xp[:, 2:] * w[2]


def _window_attn(q, k, v, sink):
    B_, T = q.shape[0], q.shape[1]
    nb = T // BLK
    qb = q.reshape(B_, nb, BLK, ATT_KV, ATT_G, ATT_HD)

    def band(a):
        ab = a.reshape(B_, nb, BLK, ATT_KV, ATT_HD)
        ap = jnp.pad(ab, ((0, 0), (1, 1), (0, 0), (0, 0), (0, 0)))
        return jnp.concatenate([ap[:, :-2], ap[:, 1:-1], ap[:, 2:]], axis=2)

    kb, vb = band(k), band(v)
    s = jnp.einsum('bnqkgd,bnskd->bnkgqs', qb, kb).astype(jnp.float32) * (ATT_HD ** -0.5)
    qpos = jnp.arange(T).reshape(nb, BLK)
    kpos = (jnp.arange(nb)[:, None] - 1) * BLK + jnp.arange(3 * BLK)[None, :]
    dist = jnp.abs(qpos[:, :, None] - kpos[:, None, :])
    allowed = (dist <= WINDOW) & ((kpos >= PAD) & (kpos < T))[:, None, :]
    slopes = jnp.exp2(-8.0 * jnp.arange(1, ATT_HQ + 1, dtype=jnp.float32) / ATT_HQ).reshape(ATT_KV, ATT_G)
    s = s - slopes[None, None, :, :, None, None] * dist.astype(jnp.float32)[None, :, None, None]
    s = jnp.where(allowed[None, :, None, None], s, -jnp.inf)
    sk = sink.astype(jnp.float32).reshape(ATT_KV, ATT_G)[None, None, :, :, None, None]
    m = jnp.maximum(jnp.max(s, axis=-1, keepdims=True), sk)
    p = jnp.exp(s - m)
    p = p / (jnp.sum(p, axis=-1, keepdims=True) + jnp.exp(sk - m))
    o = jnp.einsum('bnkgqs,bnskd->bnqkgd', p.astype(v.dtype), vb)
    return o.reshape(B_, T, ATT_HQ, ATT_HD)


def _gated_delta_chunked(q, k, v, beta, g):
    B_, T, H, DK = q.shape
    DV = v.shape[-1]
    C = DN_CHUNK
    n = T // C

    def ch(a):
        return jnp.moveaxis(a.reshape(B_, n, C, H, -1), 3, 1)

    q = ch(q) * (DK ** -0.5)
    k = ch(k)
    v = ch(v)
    beta = jnp.moveaxis(beta.reshape(B_, n, C, H), 3, 1)
    gc = jnp.cumsum(jnp.moveaxis(g.reshape(B_, n, C, H), 3, 1), axis=-1)
    idx = jnp.arange(C)
    incl = idx[:, None] >= idx[None, :]
    strict = idx[:, None] > idx[None, :]
    decay = jnp.exp(jnp.where(incl, gc[..., :, None] - gc[..., None, :], -jnp.inf))
    kb = k * beta[..., None]
    a_mat = jnp.where(strict, jnp.einsum('bhncd,bhnsd->bhncs', kb, k) * decay, 0.0)
    lhs = a_mat + jnp.eye(C, dtype=jnp.float32)
    rhs = jnp.concatenate([v * beta[..., None], kb * jnp.exp(gc)[..., None]], axis=-1)
    sol = lax.linalg.triangular_solve(lhs, rhs, left_side=True, lower=True)
    u, w = sol[..., :DV], sol[..., DV:]
    qk = jnp.einsum('bhncd,bhnsd->bhncs', q, k) * decay
    q_dec = q * jnp.exp(gc)[..., None]
    g_last = gc[..., -1]
    k_tail = k * jnp.exp(g_last[..., None] - gc)[..., None]

    def step(S, xs):
        q_i, qk_i, u_i, w_i, kt_i, gl_i = xs
        v_new = u_i - jnp.einsum('bhcd,bhdv->bhcv', w_i, S)
        o_i = jnp.einsum('bhcd,bhdv->bhcv', q_i, S) + jnp.einsum('bhcs,bhsv->bhcv', qk_i, v_new)
        S = S * jnp.exp(gl_i)[..., None, None] + jnp.einsum('bhcd,bhcv->bhdv', kt_i, v_new)
        return S, o_i

    xs = tuple(jnp.moveaxis(a, 2, 0) for a in (q_dec, qk, u, w, k_tail, g_last))
    S0 = jnp.zeros((B_, H, DK, DV), jnp.float32)
    _, o = lax.scan(step, S0, xs)
    return jnp.transpose(o, (1, 0, 3, 2, 4)).reshape(B_, T, H, DV)


def _layer(h, valid, norm_w, w_in, conv_a_w, attn_sink, dn_conv_w, dn_a_log, dn_dt_bias, dn_norm_w, w_out):
    B_, T = h.shape[0], h.shape[1]
    xn = _rmsnorm(h, norm_w)
    proj = jnp.einsum('btd,dp->btp', xn, w_in)
    split_idx = [int(i) for i in np.cumsum(SIZES)[:-1]]
    cx, cb, cc, cz, aq, ak, av, az, dq, dk, dv, dz, dbeta, da = jnp.split(proj, split_idx, axis=-1)

    y_conv = cb * _conv3(cc * cx, conv_a_w) * jax.nn.silu(cz)

    y_att = _window_attn(aq.reshape(B_, T, ATT_HQ, ATT_HD), ak.reshape(B_, T, ATT_KV, ATT_HD),
                         av.reshape(B_, T, ATT_KV, ATT_HD), attn_sink)
    y_att = y_att.reshape(B_, T, ATT_Q) * jax.nn.silu(az)

    qkv = jax.nn.silu(_conv3(jnp.concatenate([dq, dk, dv], axis=-1), dn_conv_w)).astype(jnp.float32)
    q = _l2norm(qkv[..., :DN_H * DN_DK].reshape(B_, T, DN_H, DN_DK))
    k = _l2norm(qkv[..., DN_H * DN_DK:2 * DN_H * DN_DK].reshape(B_, T, DN_H, DN_DK))
    k = k * valid.astype(jnp.float32)[None, :, None, None]
    v = qkv[..., 2 * DN_H * DN_DK:].reshape(B_, T, DN_H, DN_DV)
    beta = jax.nn.sigmoid(dbeta.astype(jnp.float32)).reshape(B_, T, 2, DN_H)
    g = -jnp.exp(dn_a_log.astype(jnp.float32)) * jax.nn.softplus(
        da.astype(jnp.float32).reshape(B_, T, 2, DN_H) + dn_dt_bias.astype(jnp.float32))
    o_f = _gated_delta_chunked(q, k, v, beta[:, :, 0], g[:, :, 0])
    fl = lambda a: jnp.flip(a, axis=1)
    o_b = fl(_gated_delta_chunked(fl(q), fl(k), fl(v), fl(beta[:, :, 1]), fl(g[:, :, 1])))
    o_dn = _rmsnorm(o_f + o_b, dn_norm_w).reshape(B_, T, DN_W).astype(h.dtype)
    y_dn = o_dn * jax.nn.silu(dz)

    y = jnp.concatenate([y_conv, y_att, y_dn], axis=-1) * valid[None, :, None]
    return h + jnp.einsum('btm,md->btd', y, w_out)


def _trunk(x, meta_tokens, norm_w, w_in, conv_a_w, attn_sink, dn_conv_w, dn_a_log, dn_dt_bias,
           dn_norm_w, w_out, final_norm_w):
    B_ = x.shape[0]
    lead = jnp.concatenate([jnp.zeros((PAD, D_MODEL), x.dtype), meta_tokens.astype(x.dtype)], axis=0)
    h = jnp.concatenate([jnp.broadcast_to(lead[None], (B_, LEAD, D_MODEL)), x], axis=1)
    T = h.shape[1]
    valid = (jnp.arange(T) >= PAD).astype(x.dtype)
    for l in range(DEPTH):
        h = _layer(h, valid, norm_w[l], w_in[l], conv_a_w[l], attn_sink[l], dn_conv_w[l],
                   dn_a_log[l], dn_dt_bias[l], dn_norm_w[l], w_out[l])
    return _rmsnorm(h, final_norm_w)[:, LEAD:]


def setup_inputs(seed: int = 0) -> dict:
    key = jax.random.key(seed)
    ks = jax.random.split(key, 16)
    f32 = jnp.float32
    dt = jnp.exp(jax.random.uniform(ks[9], (DEPTH, 2, DN_H), f32, np.log(1e-3), np.log(1e-1)))
    return {
        "x_prompt": jax.random.normal(ks[0], (BATCH, SEQ, D_MODEL), f32),
        "x_sample": jax.random.normal(ks[1], (DEC_BATCH, DEC_SEQ, D_MODEL), f32),
        "meta_tokens": jax.random.normal(ks[2], (N_META, D_MODEL), f32),
        "norm_w": 1.0 + 0.02 * jax.random.normal(ks[3], (DEPTH, D_MODEL), f32),
        "w_in": jax.random.normal(ks[4], (DEPTH, D_MODEL, N_PROJ), f32) * (D_MODEL ** -0.5),
        "conv_a_w": jax.random.normal(ks[5], (DEPTH, 3, CONV_W), f32) * (3 ** -0.5),
        "attn_sink": 0.5 * jax.random.normal(ks[6], (DEPTH, ATT_HQ), f32),
        "dn_conv_w": jax.random.normal(ks[7], (DEPTH, 3, 2 * DN_H * DN_DK + DN_W), f32) * (3 ** -0.5),
        "dn_a_log": jnp.log(jax.random.uniform(ks[8], (DEPTH, 2, DN_H), f32, 1.0, 16.0)),
        "dn_dt_bias": dt + jnp.log(-jnp.expm1(-dt)),
        "dn_norm_w": 1.0 + 0.02 * jax.random.normal(ks[10], (DEPTH, DN_DV), f32),
        "w_out": jax.random.normal(ks[11], (DEPTH, D_MIX, D_MODEL), f32) * (0.5 * D_MIX ** -0.5),
        "final_norm_w": 1.0 + 0.02 * jax.random.normal(ks[12], (D_MODEL,), f32),
    }


def reference(x_prompt, x_sample, meta_tokens, norm_w, w_in, conv_a_w, attn_sink, dn_conv_w,
              dn_a_log, dn_dt_bias, dn_norm_w, w_out, final_norm_w):
    y_prompt = _trunk(x_prompt, meta_tokens, norm_w, w_in, conv_a_w, attn_sink, dn_conv_w,
                      dn_a_log, dn_dt_bias, dn_norm_w, w_out, final_norm_w)
    y_sample = _trunk(x_sample, meta_tokens, norm_w, w_in, conv_a_w, attn_sink, dn_conv_w,
                      dn_a_log, dn_dt_bias, dn_norm_w, w_out, final_norm_w)
    return (y_prompt, y_sample)
```

```python
import contextlib
import numpy as np
import concourse.bass as bass
import concourse.mybir as mybir
from concourse.bass_utils import run_bass_kernel_spmd

F32 = mybir.dt.float32
BF16 = mybir.dt.bfloat16
AF = mybir.ActivationFunctionType
ALU = mybir.AluOpType

import os as _os
BCUT = int(_os.environ.get('BCUT', '0'))
CCUT = int(_os.environ.get('CCUT', '0'))
D = 2048
KC = 16
NPROJ = 7192
LEAD = 128
PAD = 112
NEG = -1.0e9
EPS = 1e-6


class Sched:
    COMPUTE = ("pe", "act", "dve", "pool")
    ALLENG = ("pe", "act", "dve", "pool", "sp")
    NDMA = {"sp": 8, "pool": 4}

    def __init__(self, nc):
        self.nc = nc
        self.gstack = contextlib.ExitStack()
        self.sems = {}
        for e in self.COMPUTE:
            self.sems["e:" + e] = self.gstack.enter_context(nc.semaphore("s_" + e))
        for q, k in self.NDMA.items():
            for j in range(k):
                self.sems[f"d:{q}:{j}"] = self.gstack.enter_context(nc.semaphore(f"d_{q}{j}"))
        self.seq = {e: 0 for e in self.COMPUTE}
        self.ndma = {q: 0 for q in self.NDMA}
        self.last_tok = {}
        self.known = {e: {} for e in self.ALLENG}
        self.last_w = {}
        self.readers = {}
        self.ops = {e: [] for e in self.ALLENG}
        self.final_tokens = []
        self.barrier_tokens = []
        self.pstack = None
        self.nops = 0

    def _uid(self):
        self.uid = getattr(self, "uid", 0) + 1
        return f"_{self.uid}"

    def sb(self, name, shape, dtype, glob=False):
        st = self.gstack if (glob or self.pstack is None) else self.pstack
        return st.enter_context(self.nc.sbuf_tensor("s_" + name + self._uid(), list(shape), dtype))

    def ps(self, name, shape, dtype):
        return self.pstack.enter_context(self.nc.psum_tensor("p_" + name + self._uid(), list(shape), dtype))

    def _deps(self, reads, writes):
        deps = list(self.barrier_tokens)
        for k in reads:
            t = self.last_w.get(k)
            if t is not None:
                deps.append(t)
        for k in writes:
            t = self.last_w.get(k)
            if t is not None:
                deps.append(t)
            deps.extend(self.readers.get(k, ()))
        return deps

    def _commit(self, token, reads, writes):
        self.last_tok[token[0]] = token[1]
        for k in reads:
            self.readers.setdefault(k, []).append(token)
        for k in writes:
            self.last_w[k] = token
            self.readers[k] = []

    def _waits(self, issuer, deps, skip_key=None):
        kn = self.known[issuer]
        need = {}
        for (key, val) in deps:
            if key == skip_key:
                continue
            if kn.get(key, 0) < val and need.get(key, 0) < val:
                need[key] = val
        for key, val in need.items():
            kn[key] = val
        return list(need.items())

    def op(self, eng, fn, reads=(), writes=()):
        deps = self._deps(reads, writes)
        key = "e:" + eng
        waits = self._waits(eng, deps, skip_key=key if eng == "pe" else None)
        self.seq[eng] += 1
        token = (key, self.seq[eng])
        self.ops[eng].append((waits, fn, key, 1))
        self._commit(token, reads, writes)
        self.nops += 1
        return token

    def dma(self, q, out, in_, reads=(), writes=(), final=False):
        n = self.ndma[q]
        self.ndma[q] += 1
        K = self.NDMA[q]
        key = f"d:{q}:{n % K}"
        deps = self._deps(reads, writes)
        if n // K > 0:
            deps.append((key, 16 * (n // K)))
        waits = self._waits(q, deps)
        token = (key, 16 * (n // K + 1))
        self.ops[q].append((waits, lambda e: e.dma_start(out=out, in_=in_), key, 16))
        self._commit(token, reads, writes)
        if final:
            self.final_tokens.append(token)
        self.nops += 1
        return token

    def barrier(self):
        self.barrier_tokens = list(self.last_tok.items())

    @contextlib.contextmanager
    def phase(self, last=False):
        self.pstack = contextlib.ExitStack()
        self.barrier()
        yield self
        self._emit(last)
        self.pstack.close()
        self.pstack = None

    def _emit(self, last):
        nc = self.nc
        sems = self.sems
        fin = []
        if last:
            fin = self._waits("sp", self.final_tokens)
        ops = self.ops

        def run(engine, lst, tail=()):
            for (waits, fn, key, amt) in lst:
                for (k, v) in waits:
                    engine.wait_ge(sems[k], v)
                inst = fn(engine)
                inst.then_inc(sems[key], amt)
            for (k, v) in tail:
                engine.wait_ge(sems[k], v)

        with nc.Block() as block:
            @block.sync
            def _(e):
                run(e, ops["sp"], fin)

            @block.tensor
            def _(e):
                run(e, ops["pe"])

            @block.scalar
            def _(e):
                run(e, ops["act"])

            @block.vector
            def _(e):
                run(e, ops["dve"])

            @block.gpsimd
            def _(e):
                run(e, ops["pool"])
        self.ops = {e: [] for e in self.ALLENG}

    def close(self):
        self.gstack.close()


def build_nc(NB, L, NBOUT, G=4, stop=None):
    T = NB * 128
    nc = bass.Bass("TRN2", target_bir_lowering=False)
    dt_in = lambda n, s, d=F32: nc.dram_tensor(n, list(s), d, kind="ExternalInput").ap()
    dt_sc = lambda n, s, d=F32: nc.dram_tensor(n, list(s), d, kind="Internal").ap()

    h0 = dt_in("h0", [T, D])
    validT_d = dt_in("validT", [128, NB])
    validB_d = dt_in("validB", [128, T])
    w_in_d = dt_in("w_in", [L, D, NPROJ])
    w_out_d = dt_in("w_out", [L, D, D])
    nwB_d = dt_in("nwB", [128, L, D])
    fnwB_d = dt_in("fnwB", [128, D])
    caw_d = dt_in("caw", [128, L, 4, 3])
    dcw_d = dt_in("dcw", [128, L, 18, 3])
    sinkB_d = dt_in("sinkB", [128, L, 6])
    alogB_d = dt_in("alogB", [128, L, 12])
    dtbB_d = dt_in("dtbB", [128, L, 12])
    dnwB_d = dt_in("dnwB", [128, L, 128])
    c_identb_d = dt_in("c_identb", [128, 128])
    c_U_d = dt_in("c_U", [128, 2, 128])
    c_maskB_d = dt_in("c_maskB", [128, 2, 128])
    c_strict_d = dt_in("c_strict", [128, 2, 128])
    c_AL_d = dt_in("c_AL", [128, 3, 768])
    out_d = nc.dram_tensor("out", [NBOUT * 128, D], F32, kind="ExternalOutput").ap()

    Wb_in = dt_sc("Wb_in", [L, D, NPROJ], BF16)
    Wb_out = dt_sc("Wb_out", [L, D, D], BF16)
    H = dt_sc("H", [T, D])
    ZS = dt_sc("ZS", [D, T], BF16)
    UC = dt_sc("UC", [512, T])
    CB = dt_sc("CB", [512, T])
    QT = dt_sc("QT", [768, T], BF16)
    KT = dt_sc("KT", [256, T], BF16)
    VV = dt_sc("VV", [T, 256], BF16)
    DQKV = dt_sc("DQKV", [2304, T])
    BG = dt_sc("BG", [T, 24])
    KQT = dt_sc("KQT", [NB, 128, 6 * 2 * 128], BF16)
    KN = dt_sc("KN", [NB, 128, 768], BF16)
    VN = dt_sc("VN", [NB, 128, 768], BF16)
    OF = dt_sc("OF", [NB, 128, 768])
    YDN = dt_sc("YDN", [NB, 128, 768], BF16)

    S = Sched(nc)
    identb = S.sb("identb", [128, 128], BF16)
    identf = S.sb("identf", [128, 128], F32)
    Uf = S.sb("Uf", [128, 2, 128], F32)
    maskB = S.sb("maskB", [128, 2, 128], F32)
    strictM = S.sb("strictM", [128, 2, 128], F32)
    AL = S.sb("AL", [128, 3, 768], F32)
    onesf = S.sb("onesf", [128, 128], F32)
    onesb = S.sb("onesb", [128, 128], BF16)
    validT = S.sb("validT", [128, NB], F32)
    kbias = S.sb("kbias", [128, NB], F32)
    caw = S.sb("caw", [128, L, 4, 3], F32)
    dcw = S.sb("dcw", [128, L, 18, 3], F32)
    sinkB = S.sb("sinkB", [128, L, 6], F32)
    esink = S.sb("esink", [128, L, 6], F32)
    alogB = S.sb("alogB", [128, L, 12], F32)
    negA = S.sb("negA", [128, L, 12], F32)
    dtbB = S.sb("dtbB", [128, L, 12], F32)
    dnwB = S.sb("dnwB", [128, L, 128], F32)
    fnwB = S.sb("fnwB", [128, D], F32)
    nw = S.sb("nw", [128, D], F32)
    betaT = S.sb("betaT", [128, NB, 12], F32)
    nbetaT = S.sb("nbetaT", [128, NB, 12], F32)
    gT = S.sb("gT", [128, NB, 12], F32)
    epsc = S.sb("epsc", [128, 1], F32)
    onec = S.sb("onec", [128, 1], F32)

    with S.phase():
        S.dma("pool", identb[:], c_identb_d, writes=["identb"])
        S.dma("sp", identf[:], c_identb_d, writes=["identf"])
        S.dma("sp", Uf[:], c_U_d, writes=["Uf"])
        S.dma("sp", maskB[:], c_maskB_d, writes=["maskB"])
        S.dma("sp", strictM[:], c_strict_d, writes=["strictM"])
        S.dma("sp", AL[:], c_AL_d, writes=["AL"])
        S.dma("sp", validT[:], validT_d, writes=["validT"])
        S.dma("sp", caw[:], caw_d, writes=["caw"])
        S.dma("sp", dcw[:], dcw_d, writes=["dcw"])
        S.dma("sp", sinkB[:], sinkB_d, writes=["sinkB"])
        S.dma("sp", alogB[:], alogB_d, writes=["alogB"])
        S.dma("sp", dtbB[:], dtbB_d, writes=["dtbB"])
        S.dma("sp", dnwB[:], dnwB_d, writes=["dnwB"])
        S.dma("sp", fnwB[:], fnwB_d, writes=["fnwB"])
        S.op("pool", lambda e: e.memset(onesf[:], 1.0), writes=["onesf"])
        S.op("pool", lambda e: e.memset(onesb[:], 1.0), writes=["onesb"])
        S.op("pool", lambda e: e.memset(epsc[:], EPS), writes=["epsc"])
        S.op("pool", lambda e: e.memset(onec[:], 1.0), writes=["onec"])
        S.op("dve", lambda e: e.tensor_scalar(kbias[:], validT[:], -1.0, -NEG, ALU.add, ALU.mult),
             reads=["validT"], writes=["kbias"])
        S.op("act", lambda e: e.activation(esink[:], sinkB[:], AF.Exp), reads=["sinkB"], writes=["esink"])
        S.op("act", lambda e: e.activation(negA[:], alogB[:], AF.Exp), reads=["alogB"], writes=["negA"])
        S.op("dve", lambda e: e.tensor_scalar(negA[:], negA[:], -1.0, None, ALU.mult),
             reads=["negA"], writes=["negA"])
        for l in range(L):
            for kc in range(KC):
                S.dma("pool", Wb_in[l, kc * 128:(kc + 1) * 128, :], w_in_d[l, kc * 128:(kc + 1) * 128, :],
                      writes=[f"Wb_in{l}"])
            for kc in range(0, KC, 4):
                S.dma("pool", Wb_out[l, kc * 128:(kc + 4) * 128, :], w_out_d[l, kc * 128:(kc + 4) * 128, :],
                      writes=[f"Wb_out{l}"])

    if stop == '0':
        S.close()
        return nc
    for l in range(L):
        Hin = h0 if l == 0 else H
        hin_key = "h0" if l == 0 else "H"
        with S.phase():
            N = G * 128
            ht = [S.sb(f"ht{i}", [128, D], F32) for i in range(2)]
            junk = S.sb("junk", [128, D], BF16)
            xn = S.sb("xn", [128, D], BF16)
            xnT = S.sb("xnT", [128, KC, N], BF16)
            wt = [S.sb(f"wt{i}", [128, KC, 512], BF16) for i in range(2)]
            wlast = S.sb("wlast", [128, KC, 24], BF16)
            cxs = S.sb("cxs", [128, 4, N], F32)
            ef = [S.sb(f"ef{i}", [128, N], F32) for i in range(3)]
            eb = [S.sb(f"eb{i}", [128, N], BF16) for i in range(3)]
            vtok = S.sb("vtok", [128, 256], BF16)
            bgt = S.sb("bgt", [128, 24], F32)
            ss = S.sb("ss", [128, 2], F32)
            ptr = [S.ps(f"ptr{i}", [128, 4, 128], BF16) for i in range(2)]
            pa = [S.ps(f"pa{i}", [128, N], F32) for i in range(2)]
            pv = S.ps("pv", [128, 256], F32)
            pbg = S.ps("pbg", [128, 24], F32)

            S.dma("sp", nw[:], nwB_d[:, l, :], writes=["nw"])
            S.dma("sp", wlast[:], Wb_in[l, :, 7168:7192].rearrange("(kc p) c -> p kc c", p=128),
                  reads=[f"Wb_in{l}"], writes=["wlast"])
            ngroups = (NB + G - 1) // G
            ecnt = [0]
            for gi in range(ngroups):
                b0 = gi * G
                gb = min(G, NB - b0)
                n = gb * 128
                for bi in range(gb):
                    b = b0 + bi
                    hh = ht[b % 2]
                    hk = f"ht{b % 2}"
                    S.dma("sp", hh[:], Hin[b * 128:(b + 1) * 128, :], reads=[hin_key], writes=[hk])
                    S.op("act", lambda e, hh=hh: e.activation(junk[:], hh[:], AF.Square, accum_out=ss[:, 0:1]),
                         reads=[hk], writes=["junk", "ss0"])
                    S.op("act", lambda e: e.activation(ss[:, 1:2], ss[:, 0:1], AF.Sqrt, bias=epsc[:, 0:1], scale=1.0 / D),
                         reads=["ss0", "epsc"], writes=["ss1"])
                    S.op("dve", lambda e: e.reciprocal(ss[:, 1:2], ss[:, 1:2]),
                         reads=["ss1"], writes=["ss1"])
                    S.op("dve", lambda e, hh=hh: e.scalar_tensor_tensor(xn[:], hh[:], ss[:, 1:2], nw[:], ALU.mult, ALU.mult),
                         reads=[hk, "ss1", "nw"], writes=["xn"])
                    for q4 in range(4):
                        pt = ptr[q4 % 2]
                        pk = f"ptr{q4 % 2}"

                        def tr(e, pt=pt, q4=q4):
                            r = None
                            for j in range(4):
                                kc = q4 * 4 + j
                                r = e.transpose(pt[:, j, :], xn[:, kc * 128:(kc + 1) * 128], identb[:])
                            return r
                        S.op("pe", tr, reads=["xn", "identb"], writes=[pk])
                        eng = "act" if q4 % 2 == 0 else "dve"
                        if eng == "act":
                            S.op("act", lambda e, pt=pt, q4=q4, bi=bi: e.copy(xnT[:, q4 * 4:(q4 + 1) * 4, bi * 128:(bi + 1) * 128], pt[:]),
                                 reads=[pk], writes=[f"xnT{bi}"])
                        else:
                            S.op("dve", lambda e, pt=pt, q4=q4, bi=bi: e.tensor_copy(xnT[:, q4 * 4:(q4 + 1) * 4, bi * 128:(bi + 1) * 128], pt[:]),
                                 reads=[pk], writes=[f"xnT{bi}"])
                xk = [f"xnT{bi}" for bi in range(gb)]
                for u in range(14):
                    w = wt[u % 2]
                    wk = f"wt{u % 2}"
                    S.dma("sp", w[:], Wb_in[l, :, u * 512:(u + 1) * 512].rearrange("(kc p) c -> p kc c", p=128),
                          reads=[f"Wb_in{l}"], writes=[wk])
                    for c4 in range(4):
                        col = u * 512 + c4 * 128
                        ch = col // 128
                        if 3072 <= col < 3328:
                            continue
                        p = pa[ecnt[0] % 2]
                        pk = f"pa{ecnt[0] % 2}"
                        ecnt[0] += 1

                        def mm(e, p=p, w=w, c4=c4, n=n):
                            r = None
                            for kc in range(KC):
                                r = e.matmul(p[:, 0:n], w[:, kc, c4 * 128:(c4 + 1) * 128], xnT[:, kc, 0:n],
                                             start=(kc == 0), stop=(kc == KC - 1))
                            return r
                        S.op("pe", mm, reads=[wk] + xk, writes=[pk])
                        tcols = slice(b0 * 128, b0 * 128 + n)
                        i3 = ch % 3
                        if col < 512:
                            S.op("act", lambda e, p=p, ch=ch, n=n: e.copy(cxs[:, ch, 0:n], p[:, 0:n]),
                                 reads=[pk], writes=[f"cxs{ch}"])
                        elif col < 1024:
                            S.op("act", lambda e, p=p, i3=i3, n=n: e.copy(ef[i3][:, 0:n], p[:, 0:n]),
                                 reads=[pk], writes=[f"ef{i3}"])
                            S.dma("sp", CB[col - 512:col - 512 + 128, tcols], ef[i3][:, 0:n], reads=[f"ef{i3}"], writes=["CB"])
                        elif col < 1536:
                            cxi = (col - 1024) // 128
                            S.op("dve", lambda e, p=p, i3=i3, n=n, cxi=cxi: e.tensor_tensor(ef[i3][:, 0:n], p[:, 0:n], cxs[:, cxi, 0:n], ALU.mult),
                                 reads=[pk, f"cxs{cxi}"], writes=[f"ef{i3}"])
                            S.dma("sp", UC[col - 1024:col - 1024 + 128, tcols], ef[i3][:, 0:n], reads=[f"ef{i3}"], writes=["UC"])
                        elif col < 2048 or 3328 <= col < 4096 or 6400 <= col < 7168:
                            if col < 2048:
                                zr = col - 1536
                            elif col < 4096:
                                zr = 512 + col - 3328
                            else:
                                zr = 1280 + col - 6400
                            S.op("act", lambda e, p=p, i3=i3, n=n: e.activation(eb[i3][:, 0:n], p[:, 0:n], AF.Silu),
                                 reads=[pk], writes=[f"eb{i3}"])
                            S.dma("sp", ZS[zr:zr + 128, tcols], eb[i3][:, 0:n], reads=[f"eb{i3}"], writes=["ZS"])
                        elif col < 2816:
                            S.op("act", lambda e, p=p, i3=i3, n=n: e.activation(eb[i3][:, 0:n], p[:, 0:n], AF.Copy, scale=128.0 ** -0.5),
                                 reads=[pk], writes=[f"eb{i3}"])
                            S.dma("sp", QT[col - 2048:col - 2048 + 128, tcols], eb[i3][:, 0:n], reads=[f"eb{i3}"], writes=["QT"])
                        elif col < 3072:
                            S.op("dve", lambda e, p=p, i3=i3, n=n: e.tensor_copy(eb[i3][:, 0:n], p[:, 0:n]),
                                 reads=[pk], writes=[f"eb{i3}"])
                            S.dma("sp", KT[col - 2816:col - 2816 + 128, tcols], eb[i3][:, 0:n], reads=[f"eb{i3}"], writes=["KT"])
                        else:
                            r0 = col - 4096
                            S.op("dve", lambda e, p=p, i3=i3, n=n: e.tensor_copy(ef[i3][:, 0:n], p[:, 0:n]),
                                 reads=[pk], writes=[f"ef{i3}"])
                            S.dma("sp", DQKV[r0:r0 + 128, tcols], ef[i3][:, 0:n], reads=[f"ef{i3}"], writes=["DQKV"])
                    if u == 6:
                        for bi in range(gb):
                            b = b0 + bi

                            def mmv(e, w=w, bi=bi):
                                r = None
                                for kc in range(KC):
                                    r = e.matmul(pv[:], xnT[:, kc, bi * 128:(bi + 1) * 128], w[:, kc, 0:256],
                                                 start=(kc == 0), stop=(kc == KC - 1))
                                return r
                            S.op("pe", mmv, reads=[wk, f"xnT{bi}"], writes=["pv"])
                            S.op("act", lambda e: e.copy(vtok[:], pv[:]), reads=["pv"], writes=["vtok"])
                            S.dma("sp", VV[b * 128:(b + 1) * 128, :], vtok[:], reads=["vtok"], writes=["VV"])
                for bi in range(gb):
                    b = b0 + bi

                    def mmb(e, bi=bi):
                        r = None
                        for kc in range(KC):
                            r = e.matmul(pbg[:], xnT[:, kc, bi * 128:(bi + 1) * 128], wlast[:, kc, :],
                                         start=(kc == 0), stop=(kc == KC - 1))
                        return r
                    S.op("pe", mmb, reads=["wlast", f"xnT{bi}"], writes=["pbg"])
                    S.op("dve", lambda e: e.tensor_copy(bgt[:], pbg[:]), reads=["pbg"], writes=["bgt"])
                    S.dma("sp", BG[b * 128:(b + 1) * 128, :], bgt[:], reads=["bgt"], writes=["BG"])

        if stop == 'A':
            S.close()
            return nc
        with S.phase():
            raw = S.sb("raw", [128, 18, 130], F32)
            cv = S.sb("cv", [128, 18, 128], F32)
            ctmp = S.sb("ctmp", [128, 18, 128], F32)
            sq = S.sb("sq", [128, 12, 128], BF16)
            rs = S.sb("rs", [128, 12, 128], F32)
            vB = S.sb("vB", [128, 128], F32)
            kq = S.sb("kq", [128, 6, 2, 128], BF16)
            vT = S.sb("vT", [128, 6, 128], BF16)
            kv_tok = S.sb("kv_tok", [128, 12, 128], BF16)
            bgt2 = S.sb("bgt2", [128, 24], F32)
            spt = S.sb("spt", [128, 12], F32)
            pss = S.ps("pss", [128, 12, 128], F32)
            ptk = S.ps("ptk", [128, 12, 128], BF16)
            for b in range(NB):
                t0 = b * 128
                lo = 1 if b == 0 else 0
                hi = 129 if b == NB - 1 else 130
                if b == 0:
                    S.op("pool", lambda e: e.memset(raw[:, :, 0:1], 0.0), writes=["raw"])
                if b == NB - 1:
                    S.op("pool", lambda e: e.memset(raw[:, :, 129:130], 0.0), writes=["raw"])
                S.dma("sp", raw[:, :, lo:hi],
                      DQKV[:, t0 - 1 + lo:t0 - 1 + hi].rearrange("(c p) t -> p c t", p=128),
                      reads=["DQKV"], writes=["raw"])
                S.dma("sp", vB[:], validB_d[:, t0:t0 + 128], writes=["vB"])
                S.dma("sp", bgt2[:], BG[t0:t0 + 128, :], reads=["BG"], writes=["bgt2"])
                S.op("act", lambda e, b=b: e.activation(betaT[:, b, :], bgt2[:, 0:12], AF.Sigmoid),
                     reads=["bgt2"], writes=["betaT"])
                S.op("dve", lambda e, b=b: e.tensor_scalar(nbetaT[:, b, :], betaT[:, b, :], -1.0, None, ALU.mult),
                     reads=["betaT"], writes=["nbetaT"])
                S.op("dve", lambda e: e.tensor_tensor(spt[:], bgt2[:, 12:24], dtbB[:, l, :], ALU.add),
                     reads=["bgt2", "dtbB"], writes=["spt"])
                S.op("act", lambda e: e.activation(spt[:], spt[:], AF.Exp), reads=["spt"], writes=["spt"])
                S.op("act", lambda e: e.activation(spt[:], spt[:], AF.Ln, bias=onec[:, 0:1]), reads=["spt", "onec"], writes=["spt"])
                S.op("dve", lambda e, b=b: e.tensor_tensor(gT[:, b, :], spt[:], negA[:, l, :], ALU.mult),
                     reads=["spt", "negA"], writes=["gT"])
                for c in range(18):
                    if c % 2 == 1:
                        S.op("dve", lambda e, c=c: e.tensor_scalar(cv[:, c, :], raw[:, c, 1:129], dcw[:, l, c, 1:2], None, ALU.mult),
                             reads=["raw", "dcw"], writes=[f"cv{c}"])
                        S.op("dve", lambda e, c=c: e.scalar_tensor_tensor(cv[:, c, :], raw[:, c, 0:128], dcw[:, l, c, 0:1], cv[:, c, :], ALU.mult, ALU.add),
                             reads=["raw", f"cv{c}"], writes=[f"cv{c}"])
                        S.op("dve", lambda e, c=c: e.scalar_tensor_tensor(cv[:, c, :], raw[:, c, 2:130], dcw[:, l, c, 2:3], cv[:, c, :], ALU.mult, ALU.add),
                             reads=["raw", f"cv{c}"], writes=[f"cv{c}"])
                    else:
                        S.op("pool", lambda e, c=c: e.tensor_scalar(cv[:, c, :], raw[:, c, 1:129], dcw[:, l, c, 1:2], None, ALU.mult),
                             reads=["raw", "dcw"], writes=[f"cv{c}"])
                        for (j, sl) in ((0, slice(0, 128)), (2, slice(2, 130))):
                            S.op("pool", lambda e, c=c, j=j, sl=sl: e.tensor_scalar(ctmp[:, c, :], raw[:, c, sl], dcw[:, l, c, j:j + 1], None, ALU.mult),
                                 reads=["raw", "dcw"], writes=[f"ctmp{c}"])
                            S.op("pool", lambda e, c=c: e.tensor_tensor(cv[:, c, :], cv[:, c, :], ctmp[:, c, :], ALU.add),
                                 reads=[f"ctmp{c}", f"cv{c}"], writes=[f"cv{c}"])
                cvk = [f"cv{c}" for c in range(18)]
                S.op("act", lambda e: e.activation(cv[:], cv[:], AF.Silu), reads=cvk, writes=cvk)
                S.op("pool", lambda e: e.tensor_tensor(sq[:], cv[:, 0:12, :], cv[:, 0:12, :], ALU.mult),
                     reads=cvk, writes=["sq"])

                def mss(e):
                    r = None
                    for j in range(3):
                        r = e.matmul(pss[:, j * 4:(j + 1) * 4, :], onesb[:], sq[:, j * 4:(j + 1) * 4, :], start=True, stop=True)
                    return r
                S.op("pe", mss, reads=["sq", "onesb"], writes=["pss"])
                S.op("act", lambda e: e.activation(rs[:], pss[:], AF.Sqrt, bias=epsc[:, 0:1], scale=1.0),
                     reads=["pss", "epsc"], writes=["rs"])
                S.op("dve", lambda e: e.reciprocal(rs[:], rs[:]), reads=["rs"], writes=["rs"])
                S.op("pool", lambda e: e.tensor_tensor(rs[:, 6:12, :], rs[:, 6:12, :], vB[:].unsqueeze(1).broadcast_to([128, 6, 128]), ALU.mult),
                     reads=["rs", "vB"], writes=["rs"])
                S.op("dve", lambda e: e.scalar_tensor_tensor(kq[:, :, 1, :], cv[:, 0:6, :], 128.0 ** -0.5, rs[:, 0:6, :], ALU.mult, ALU.mult),
                     reads=cvk + ["rs"], writes=["kq"])
                S.op("pool", lambda e: e.tensor_tensor(kq[:, :, 0, :], cv[:, 6:12, :], rs[:, 6:12, :], ALU.mult),
                     reads=cvk + ["rs"], writes=["kq"])
                S.op("act", lambda e: e.copy(vT[:], cv[:, 12:18, :]), reads=cvk, writes=["vT"])

                def trk(e):
                    r = None
                    for h in range(6):
                        r = e.transpose(ptk[:, h, :], kq[:, h, 0, :], identb[:])
                    for h in range(6):
                        r = e.transpose(ptk[:, 6 + h, :], vT[:, h, :], identb[:])
                    return r
                S.op("pe", trk, reads=["kq", "vT", "identb"], writes=["ptk"])
                S.op("act", lambda e: e.copy(kv_tok[:], ptk[:]), reads=["ptk"], writes=["kv_tok"])
                S.dma("sp", KQT[b], kq[:].rearrange("p h two t -> p (h two t)"), reads=["kq"], writes=["KQT"])
                S.dma("sp", KN[b], kv_tok[:, 0:6, :].rearrange("p h d -> p (h d)"), reads=["kv_tok"], writes=["KN"])
                S.dma("sp", VN[b], kv_tok[:, 6:12, :].rearrange("p h d -> p (h d)"), reads=["kv_tok"], writes=["VN"])

        if stop == 'A2':
            S.close()
            return nc
        with S.phase():
            kqt = S.sb("kqt", [128, 6, 2, 128], BF16)
            knt = S.sb("knt", [128, 6, 128], BF16)
            vnt = S.sb("vnt", [128, 6, 128], BF16)
            gcs = S.sb("gcs", [128, 12], F32)
            ngc = S.sb("ngc", [128, 6], F32)
            egs = S.sb("egs", [128, 18], F32)
            gU = S.sb("gU", [128, 6, 128], F32)
            Et = S.sb("Et", [128, 6, 128], F32)
            GCs = S.sb("GCs", [128, 6, 128], F32)
            DT = S.sb("DT", [128, 6, 128], F32)
            EG = S.sb("EG", [128, 6, 128], F32)
            tt = S.sb("tt", [128, 6, 128], F32)
            qg = S.sb("qg", [128, 6, 128], BF16)
            MT = S.sb("MT", [128, 6, 128], BF16)
            X = [S.sb(f"X{i}", [128, 6, 2, 128], F32) for i in range(2)]
            XT = [S.sb(f"XT{i}", [128, 6, 128], F32) for i in range(2)]
            Pb = S.sb("Pb", [128, 6, 128], BF16)
            kg = S.sb("kg", [128, 6, 128], BF16)
            kt = S.sb("kt", [128, 6, 128], BF16)
            nWT = S.sb("nWT", [128, 6, 128], BF16)
            vnew = S.sb("vnew", [128, 6, 128], BF16)
            Sf = S.sb("Sf", [128, 6, 128], F32)
            Sb = S.sb("Sb", [128, 6, 128], BF16)
            of = S.sb("of", [128, 6, 128], F32)
            ofl = S.sb("ofl", [128, 6, 128], F32)
            on = S.sb("on", [128, 6, 128], BF16)
            zsd = S.sb("zsd", [128, 6, 128], BF16)
            ydn = S.sb("ydn", [128, 6, 128], BF16)
            rn = S.sb("rn", [128, 12], F32)
            P1 = S.ps("P1", [128, 6, 2, 128], F32)
            P2 = S.ps("P2", [128, 12, 128], F32)
            P3 = S.ps("P3", [128, 12], F32)
            PT_ = S.ps("PT_", [128, 6, 128], BF16)

            def scan_block(dirn, b):
                if True:
                    S.dma("sp", kqt[:].rearrange("p h two t -> p (h two t)"), KQT[b], reads=["KQT"], writes=["kqt"])
                    S.dma("sp", knt[:].rearrange("p h d -> p (h d)"), KN[b], reads=["KN"], writes=["knt"])
                    S.dma("sp", vnt[:].rearrange("p h d -> p (h d)"), VN[b], reads=["VN"], writes=["vnt"])
                    if BCUT == 1:
                        return
                    gsl = gT[:, b, dirn * 6:(dirn + 1) * 6]

                    def mgc(e, gsl=gsl):
                        e.matmul(P3[:, 0:6], Uf[:, dirn, :], gsl, start=True, stop=True)
                        return e.matmul(P3[:, 6:12], onesf[:], gsl, start=True, stop=True)
                    S.op("pe", mgc, reads=["Uf", "onesf", "gT"], writes=["P3"])
                    S.op("dve", lambda e: e.tensor_copy(gcs[:], P3[:]), reads=["P3"], writes=["gcs"])
                    S.op("dve", lambda e: e.tensor_scalar(ngc[:], gcs[:, 0:6], -1.0, None, ALU.mult), reads=["gcs"], writes=["ngc"])
                    S.op("dve", lambda e: e.tensor_tensor(egs[:, 6:12], gcs[:, 6:12], gcs[:, 0:6], ALU.subtract), reads=["gcs"], writes=["egs1"])
                    S.op("act", lambda e: e.activation(egs[:, 0:6], gcs[:, 0:6], AF.Exp), reads=["gcs"], writes=["egs0"])
                    S.op("act", lambda e: e.activation(egs[:, 6:12], egs[:, 6:12], AF.Exp), reads=["egs1"], writes=["egs1"])
                    S.op("act", lambda e: e.activation(egs[:, 12:18], gcs[:, 6:12], AF.Exp), reads=["gcs"], writes=["egs2"])
                    if BCUT == 2:
                        return
                    for h in range(6):
                        S.op("pool", lambda e, h=h, gsl=gsl: e.tensor_scalar(gU[:, h, :], Uf[:, dirn, :], gsl[:, h:h + 1], None, ALU.mult),
                             reads=["Uf", "gT"], writes=["gU"])

                    def mGC(e):
                        e.matmul(P2[:, 0:4, :], onesf[:], gU[:, 0:4, :], start=True, stop=True)
                        return e.matmul(P2[:, 4:6, :], onesf[:], gU[:, 4:6, :], start=True, stop=True)
                    S.op("pe", mGC, reads=["gU", "onesf"], writes=["P2a"])

                    def mKQ(e):
                        r = None
                        for h in range(6):
                            r = e.matmul(P1[:, h, :, :], kqt[:, h, 0, :], kqt[:, h, :, :], start=True, stop=True)
                        return r
                    S.op("pe", mKQ, reads=["kqt"], writes=["P1a", "P1b"])
                    if BCUT == 3:
                        return
                    S.op("dve", lambda e: e.tensor_copy(GCs[:], P2[:, 0:6, :]), reads=["P2a"], writes=["GCs"])
                    S.op("pool", lambda e: e.tensor_tensor(Et[:], GCs[:], maskB[:, dirn, :].unsqueeze(1).broadcast_to([128, 6, 128]), ALU.add),
                         reads=["GCs", "maskB"], writes=["Et"])
                    if BCUT == 31:
                        return
                    S.op("act", lambda e: e.activation(EG[:], GCs[:], AF.Exp), reads=["GCs"], writes=["EG"])
                    if BCUT == 32:
                        return
                    for h in range(6):
                        S.op("act", lambda e, h=h: e.activation(DT[:, h, :], Et[:, h, :], AF.Exp, bias=ngc[:, h:h + 1]),
                             reads=["Et", "ngc"], writes=["DT"])
                    if BCUT == 33:
                        return
                    S.op("pool", lambda e: e.tensor_tensor(qg[:], kqt[:, :, 1, :], EG[:], ALU.mult), reads=["kqt", "EG"], writes=["qg"])
                    if BCUT == 34:
                        return
                    S.op("dve", lambda e: e.tensor_tensor(tt[:], P1[:, :, 0, :], DT[:], ALU.mult), reads=["P1a", "DT"], writes=["tt"])
                    if BCUT == 35:
                        return
                    S.op("dve", lambda e: e.tensor_tensor(MT[:], P1[:, :, 1, :], DT[:], ALU.mult), reads=["P1b", "DT"], writes=["MT"])
                    if BCUT == 4:
                        return
                    nb_ = nbetaT[:, b, dirn * 6:(dirn + 1) * 6]
                    bt_ = betaT[:, b, dirn * 6:(dirn + 1) * 6]
                    S.op("pool", lambda e: e.tensor_tensor(tt[:], tt[:], strictM[:, dirn, :].unsqueeze(1).broadcast_to([128, 6, 128]), ALU.mult),
                         reads=["tt", "strictM"], writes=["tt"])
                    for h in range(6):
                        S.op("pool", lambda e, h=h, nb_=nb_: e.tensor_scalar(X[0][:, h, 0, :], tt[:, h, :], nb_[:, h:h + 1], None, ALU.mult),
                             reads=["tt", "nbetaT"], writes=["X0"])
                    S.op("pool", lambda e: e.tensor_tensor(X[0][:, :, 1, :], X[0][:, :, 0, :], identf[:].unsqueeze(1).broadcast_to([128, 6, 128]), ALU.add),
                         reads=["X0", "identf"], writes=["X0"])

                    def trN(e):
                        r = None
                        for h in range(6):
                            r = e.transpose(P2[:, 6 + h, :], X[0][:, h, 0, :], identf[:])
                        return r
                    S.op("pe", trN, reads=["X0", "identf"], writes=["P2b"])
                    S.op("dve", lambda e: e.tensor_copy(XT[0][:], P2[:, 6:12, :]), reads=["P2b"], writes=["XT0"])
                    if BCUT == 5:
                        return
                    cur = 0
                    for lev in range(0, 7):
                        nxt = 1 - cur
                        Xc, Xn_, XTc, XTn = X[cur], X[nxt], XT[cur], XT[nxt]
                        kc_, kn_ = f"X{cur}", f"X{nxt}"
                        tc_, tn_ = f"XT{cur}", f"XT{nxt}"
                        if lev == 0:
                            def mA(e, Xc=Xc, XTc=XTc):
                                r = None
                                for h in range(6):
                                    r = e.matmul(P1[:, h, 0, :], XTc[:, h, :], Xc[:, h, 0, :], start=True, stop=True)
                                return r
                            S.op("pe", mA, reads=[kc_, tc_], writes=["P1a"])
                        elif lev < 6:
                            def mA(e, Xc=Xc, XTc=XTc):
                                r = None
                                for h in range(6):
                                    r = e.matmul(P1[:, h, :, :], XTc[:, h, :], Xc[:, h, :, :], start=True, stop=True)
                                return r
                            S.op("pe", mA, reads=[kc_, tc_], writes=["P1a", "P1b"])
                        else:
                            def mA(e, Xc=Xc, XTc=XTc):
                                r = None
                                for h in range(6):
                                    r = e.matmul(P1[:, h, 1, :], XTc[:, h, :], Xc[:, h, 1, :], start=True, stop=True)
                                return r
                            S.op("pe", mA, reads=[kc_, tc_], writes=["P1b"])
                        if lev < 6:
                            def mB(e, Xc=Xc, XTc=XTc):
                                r = None
                                for h in range(6):
                                    r = e.matmul(P2[:, 6 + h, :], Xc[:, h, 0, :], XTc[:, h, :], start=True, stop=True)
                                return r
                            S.op("pe", mB, reads=[kc_, tc_], writes=["P2b"])
                            S.op("dve", lambda e, Xn_=Xn_: e.tensor_copy(Xn_[:, :, 0, :], P1[:, :, 0, :]), reads=["P1a"], writes=[kn_])
                            S.op("dve", lambda e, XTn=XTn: e.tensor_copy(XTn[:], P2[:, 6:12, :]), reads=["P2b"], writes=[tn_])
                        if lev == 0:
                            S.op("pool", lambda e, Xn_=Xn_, Xc=Xc: e.tensor_copy(Xn_[:, :, 1, :], Xc[:, :, 1, :]), reads=[kc_], writes=[kn_])
                        else:
                            S.op("dve", lambda e, Xn_=Xn_, Xc=Xc: e.tensor_tensor(Xn_[:, :, 1, :], P1[:, :, 1, :], Xc[:, :, 1, :], ALU.add),
                                 reads=["P1b", kc_], writes=[kn_])
                        cur = nxt
                    if BCUT == 6:
                        return
                    S.op("pool", lambda e, Xf=X[cur]: e.tensor_copy(Pb[:], Xf[:, :, 1, :]), reads=[f"X{cur}"], writes=["Pb"])
                    pk_ = "Pb"
                    for h in range(6):
                        S.op("pool", lambda e, h=h: e.tensor_scalar(kg[:, h, :], knt[:, h, :], egs[:, h:h + 1], None, ALU.mult),
                             reads=["knt", "egs0"], writes=["kg"])
                        S.op("pool", lambda e, h=h: e.tensor_scalar(kt[:, h, :], knt[:, h, :], egs[:, 6 + h:7 + h], None, ALU.mult),
                             reads=["knt", "egs1"], writes=["kt"])

                    def mW(e):
                        r = None
                        for h in range(6):
                            r = e.matmul(P2[:, 6 + h, :], kg[:, h, :], Pb[:, h, :], start=True, stop=True)
                        return r
                    S.op("pe", mW, reads=["kg", pk_], writes=["P2b"])
                    S.op("dve", lambda e: e.tensor_scalar(nWT[:], P2[:, 6:12, :], -1.0, None, ALU.mult), reads=["P2b"], writes=["nWT"])

                    if BCUT == 7:
                        return
                    def mV(e):
                        r = None
                        for h in range(6):
                            e.matmul(P1[:, h, 0, :], Pb[:, h, :], vnt[:, h, :], start=True, stop=False)
                            r = e.matmul(P1[:, h, 0, :], nWT[:, h, :], Sb[:, h, :], start=False, stop=True)
                        return r
                    S.op("pe", mV, reads=[pk_, "vnt", "nWT", "Sb"], writes=["P1a"])
                    for h in range(6):
                        S.op("dve", lambda e, h=h, bt_=bt_: e.tensor_scalar(vnew[:, h, :], P1[:, h, 0, :], bt_[:, h:h + 1], None, ALU.mult),
                             reads=["P1a", "betaT"], writes=["vnew"])

                    def mO(e):
                        r = None
                        for h in range(6):
                            e.matmul(P1[:, h, 1, :], qg[:, h, :], Sb[:, h, :], start=True, stop=False)
                            r = e.matmul(P1[:, h, 1, :], MT[:, h, :], vnew[:, h, :], start=False, stop=True)
                        return r
                    S.op("pe", mO, reads=["qg", "Sb", "MT", "vnew"], writes=["P1b"])

                    def mS(e):
                        r = None
                        for h in range(6):
                            r = e.matmul(P2[:, h, :], kt[:, h, :], vnew[:, h, :], start=True, stop=True)
                        return r
                    S.op("pe", mS, reads=["kt", "vnew"], writes=["P2a"])
                    for h in range(6):
                        S.op("dve", lambda e, h=h: e.scalar_tensor_tensor(Sf[:, h, :], Sf[:, h, :], egs[:, 12 + h:13 + h], P2[:, h, :], ALU.mult, ALU.add),
                             reads=["Sf", "egs2", "P2a"], writes=["Sf"])
                    S.op("pool", lambda e: e.tensor_copy(Sb[:], Sf[:]), reads=["Sf"], writes=["Sb"])
                    if BCUT == 8:
                        return
                    if dirn == 0:
                        S.op("dve", lambda e: e.tensor_copy(of[:], P1[:, :, 1, :]), reads=["P1b"], writes=["of"])
                        S.dma("sp", OF[b], of[:].rearrange("p h d -> p (h d)"), reads=["of"], writes=["OF"])
                    else:
                        S.dma("sp", ofl[:].rearrange("p h d -> p (h d)"), OF[b], reads=["OF"], writes=["ofl"])
                        S.dma("sp", zsd[:], ZS[1280:2048, b * 128:(b + 1) * 128].rearrange("(h p) t -> p h t", p=128),
                              reads=["ZS"], writes=["zsd"])
                        S.op("dve", lambda e: e.tensor_tensor(of[:], P1[:, :, 1, :], ofl[:], ALU.add), reads=["P1b", "ofl"], writes=["of"])
                        S.op("pool", lambda e: e.tensor_tensor(ofl[:], of[:], of[:], ALU.mult), reads=["of"], writes=["ofl"])
                        S.op("dve", lambda e: e.tensor_reduce(rn[:, 0:6], ofl[:], mybir.AxisListType.X, ALU.add), reads=["ofl"], writes=["rn0"])
                        S.op("act", lambda e: e.activation(rn[:, 6:12], rn[:, 0:6], AF.Sqrt, bias=epsc[:, 0:1], scale=1.0 / 128), reads=["rn0", "epsc"], writes=["rn1"])
                        S.op("dve", lambda e: e.reciprocal(rn[:, 6:12], rn[:, 6:12]), reads=["rn1"], writes=["rn1"])
                        for h in range(6):
                            S.op("pool", lambda e, h=h: e.tensor_scalar(ofl[:, h, :], of[:, h, :], rn[:, 6 + h:7 + h], None, ALU.mult),
                                 reads=["of", "rn1"], writes=["ofl"])
                        S.op("pool", lambda e: e.tensor_tensor(on[:], ofl[:], dnwB[:, l, :].unsqueeze(1).broadcast_to([128, 6, 128]), ALU.mult),
                             reads=["ofl", "dnwB"], writes=["on"])

                        def trO(e):
                            r = None
                            for h in range(6):
                                r = e.transpose(PT_[:, h, :], on[:, h, :], identb[:])
                            return r
                        S.op("pe", trO, reads=["on", "identb"], writes=["PT_"])
                        S.op("dve", lambda e: e.tensor_tensor(ydn[:], PT_[:], zsd[:], ALU.mult), reads=["PT_", "zsd"], writes=["ydn"])
                        S.dma("sp", YDN[b], ydn[:].rearrange("p h t -> p (h t)"), reads=["ydn"], writes=["YDN"])

            for dirn in range(2):
                S.op("pool", lambda e: e.memset(Sf[:], 0.0), writes=["Sf"])
                S.op("pool", lambda e: e.memset(Sb[:], 0.0), writes=["Sb"])
                order = range(NB) if dirn == 0 else range(NB - 1, -1, -1)
                for b in order:
                    scan_block(dirn, b)

        if stop == 'B':
            S.close()
            return nc
        with S.phase(last=(l == L - 1)):
            wout = S.sb("wout", [128, KC, D], BF16)
            hc = S.sb("hc", [128, D], F32)
            hn = S.sb("hn", [128, D], F32)
            yo = S.sb("yo", [128, D], F32)
            junk2 = S.sb("junk2", [128, D], BF16)
            ss2 = S.sb("ss2", [128, 2], F32)
            yT = S.sb("yT", [128, 16, 128], BF16)
            uc = S.sb("uc", [128, 4, 130], F32)
            cbt = S.sb("cbt", [128, 4, 128], F32)
            cvc = S.sb("cvc", [128, 4, 128], F32)
            cvt = S.sb("cvt", [128, 4, 128], F32)
            zst = S.sb("zst", [128, 10, 128], BF16)
            qt = S.sb("qt", [128, 6, 128], BF16)
            ktl = S.sb("ktl", [128, 2, 384], BF16)
            vl = S.sb("vl", [128, 3, 256], BF16)
            Ea = S.sb("Ea", [128, 384], F32)
            PTa = S.sb("PTa", [128, 384], BF16)
            den = S.sb("den", [128, 384], F32)
            oa = S.sb("oa", [128, 384], F32)
            pst = S.ps("pst", [128, 384], F32)
            pot = [S.ps(f"pot{g}", [128, 384], F32) for g in range(2)]
            pden = [S.ps(f"pden{g}", [128, 384], F32) for g in range(2)]
            po = [S.ps(f"po{i}", [128, 512], F32) for i in range(2)]

            S.dma("sp", wout[:], Wb_out[l].rearrange("(kc p) c -> p kc c", p=128), reads=[f"Wb_out{l}"], writes=["wout"])
            for b in range(NB):
                t0 = b * 128
                lo = 1 if b == 0 else 0
                hi = 129 if b == NB - 1 else 130
                if b == 0:
                    S.op("pool", lambda e: e.memset(uc[:, :, 0:1], 0.0), writes=["uc"])
                if b == NB - 1:
                    S.op("pool", lambda e: e.memset(uc[:, :, 129:130], 0.0), writes=["uc"])
                S.dma("sp", uc[:, :, lo:hi], UC[:, t0 - 1 + lo:t0 - 1 + hi].rearrange("(c p) t -> p c t", p=128),
                      reads=["UC"], writes=["uc"])
                S.dma("sp", cbt[:], CB[:, t0:t0 + 128].rearrange("(c p) t -> p c t", p=128), reads=["CB"], writes=["cbt"])
                S.dma("sp", zst[:], ZS[0:1280, t0:t0 + 128].rearrange("(c p) t -> p c t", p=128), reads=["ZS"], writes=["zst"])
                S.dma("sp", hc[:], Hin[t0:t0 + 128, :], reads=[hin_key], writes=["hc"])
                if CCUT == 1:
                    continue
                for c in range(4):
                    S.op("pool", lambda e, c=c: e.tensor_scalar(cvc[:, c, :], uc[:, c, 1:129], caw[:, l, c, 1:2], None, ALU.mult),
                         reads=["uc", "caw"], writes=[f"cvc{c}"])
                    for (j, sl) in ((0, slice(0, 128)), (2, slice(2, 130))):
                        S.op("pool", lambda e, c=c, j=j, sl=sl: e.tensor_scalar(cvt[:, c, :], uc[:, c, sl], caw[:, l, c, j:j + 1], None, ALU.mult),
                             reads=["uc", "caw"], writes=[f"cvt{c}"])
                        S.op("pool", lambda e, c=c: e.tensor_tensor(cvc[:, c, :], cvc[:, c, :], cvt[:, c, :], ALU.add),
                             reads=[f"cvt{c}", f"cvc{c}"], writes=[f"cvc{c}"])
                cvck = [f"cvc{c}" for c in range(4)]
                S.op("pool", lambda e: e.tensor_tensor(cvc[:], cvc[:], cbt[:], ALU.mult), reads=cvck + ["cbt"], writes=cvck)
                S.op("pool", lambda e: e.tensor_tensor(yT[:, 0:4, :], cvc[:], zst[:, 0:4, :], ALU.mult), reads=cvck + ["zst"], writes=["yT_c"])
                if CCUT == 2:
                    continue
                kbs = [kb for kb in (b - 1, b, b + 1) if 0 <= kb < NB]
                k0, k1 = kbs[0], kbs[-1]
                S.dma("sp", qt[:], QT[:, t0:t0 + 128].rearrange("(h p) t -> p h t", p=128), reads=["QT"], writes=["qt"])
                S.dma("sp", ktl[:, :, 0:(k1 - k0 + 1) * 128], KT[:, k0 * 128:(k1 + 1) * 128].rearrange("(g p) t -> p g t", p=128),
                      reads=["KT"], writes=["ktl"])
                S.dma("sp", vl[:, 0:(k1 - k0 + 1), :], VV[k0 * 128:(k1 + 1) * 128, :].rearrange("(j p) c -> p j c", p=128),
                      reads=["VV"], writes=["vl"])
                for g in range(2):
                    for ji, kb in enumerate(kbs):
                        off = kb - b + 1
                        j = kb - k0
                        first = (ji == 0)
                        lastk = (ji == len(kbs) - 1)
                        S.op("pe", lambda e, g=g, j=j: e.matmul(pst[:], ktl[:, g, j * 128:(j + 1) * 128], qt[:, 3 * g:3 * g + 3, :], start=True, stop=True),
                             reads=["ktl", "qt"], writes=["pst"])
                        S.op("dve", lambda e, g=g, off=off: e.tensor_tensor(Ea[:], pst[:], AL[:, off, g * 384:(g + 1) * 384], ALU.add),
                             reads=["pst", "AL"], writes=["Ea"])
                        S.op("act", lambda e, kb=kb: e.activation(PTa[:], Ea[:], AF.Exp, bias=kbias[:, kb:kb + 1]),
                             reads=["Ea", "kbias"], writes=["PTa"])

                        def mpv(e, g=g, j=j, first=first, lastk=lastk):
                            e.matmul(pot[g][:], vl[:, j, g * 128:(g + 1) * 128], PTa[:], start=first, stop=lastk)
                            return e.matmul(pden[g][:], onesb[:], PTa[:], start=first, stop=lastk)
                        S.op("pe", mpv, reads=["vl", "PTa", "onesb"], writes=[f"pot{g}", f"pden{g}"])
                    for hh in range(3):
                        h = 3 * g + hh
                        S.op("dve", lambda e, g=g, hh=hh, h=h: e.tensor_scalar(den[:, hh * 128:(hh + 1) * 128], pden[g][:, hh * 128:(hh + 1) * 128], esink[:, l, h:h + 1], None, ALU.add),
                             reads=[f"pden{g}", "esink"], writes=["den"])
                    S.op("dve", lambda e: e.reciprocal(den[:], den[:]), reads=["den"], writes=["den"])
                    S.op("dve", lambda e, g=g: e.tensor_tensor(oa[:], pot[g][:], den[:], ALU.mult), reads=[f"pot{g}", "den"], writes=["oa"])
                    S.op("pool", lambda e, g=g: e.tensor_tensor(yT[:, 4 + 3 * g:7 + 3 * g, :], oa[:].rearrange("p (h t) -> p h t", h=3), zst[:, 4 + 3 * g:7 + 3 * g, :], ALU.mult),
                         reads=["oa", "zst"], writes=[f"yT_a{g}"])
                if CCUT == 3:
                    continue
                S.dma("sp", yT[:, 10:16, :].rearrange("p h t -> p (h t)"), YDN[b], reads=["YDN"], writes=["yT_d"])
                if CCUT == 4:
                    continue
                ykeys = ["yT_c", "yT_a0", "yT_a1", "yT_d"]
                for n4 in range(4):
                    p = po[n4 % 2]
                    pk = f"po{n4 % 2}"

                    def mo(e, p=p, n4=n4):
                        r = None
                        for mc in range(16):
                            r = e.matmul(p[:], yT[:, mc, :], wout[:, mc, n4 * 512:(n4 + 1) * 512], start=(mc == 0), stop=(mc == 15))
                        return r
                    S.op("pe", mo, reads=ykeys + ["wout"], writes=[pk])
                    S.op("dve", lambda e, p=p, n4=n4, b=b: e.scalar_tensor_tensor(hn[:, n4 * 512:(n4 + 1) * 512], p[:], validT[:, b:b + 1], hc[:, n4 * 512:(n4 + 1) * 512], ALU.mult, ALU.add),
                         reads=[pk, "validT", "hc"], writes=[f"hn{n4}"])
                if CCUT == 5:
                    continue
                hnk = [f"hn{n4}" for n4 in range(4)]
                if l < L - 1:
                    S.dma("sp", H[t0:t0 + 128, :], hn[:], reads=hnk, writes=["H"])
                elif 1 <= b <= NBOUT:
                    S.op("act", lambda e: e.activation(junk2[:], hn[:], AF.Square, accum_out=ss2[:, 0:1]), reads=hnk, writes=["junk2", "ss2a"])
                    if CCUT == 6:
                        continue
                    S.op("act", lambda e: e.activation(ss2[:, 1:2], ss2[:, 0:1], AF.Sqrt, bias=epsc[:, 0:1], scale=1.0 / D), reads=["ss2a", "epsc"], writes=["ss2b"])
                    if CCUT == 7:
                        continue
                    S.op("dve", lambda e: e.reciprocal(ss2[:, 1:2], ss2[:, 1:2]), reads=["ss2b"], writes=["ss2b"])
                    if CCUT == 8:
                        continue
                    S.op("dve", lambda e: e.scalar_tensor_tensor(yo[:], hn[:], ss2[:, 1:2], fnwB[:], ALU.mult, ALU.mult),
                         reads=hnk + ["ss2b", "fnwB"], writes=["yo"])
                    if CCUT == 9:
                        continue
                    S.dma("sp", out_d[(b - 1) * 128:b * 128, :], yo[:], reads=["yo"], writes=["out"], final=True)
    S.close()
    return nc


def make_consts():
    import ml_dtypes
    i = np.arange(128)
    c = {}
    c["c_identb"] = np.eye(128, dtype=np.float32)
    U = np.zeros((128, 2, 128), np.float32)
    U[:, 0, :] = (i[:, None] <= i[None, :])
    U[:, 1, :] = (i[:, None] >= i[None, :])
    c["c_U"] = U
    mB = np.zeros((128, 2, 128), np.float32)
    mB[:, 0, :] = np.where(i[None, :] < i[:, None], NEG, 0.0)
    mB[:, 1, :] = np.where(i[None, :] > i[:, None], NEG, 0.0)
    c["c_maskB"] = mB
    st = np.zeros((128, 2, 128), np.float32)
    st[:, 0, :] = (i[None, :] > i[:, None])
    st[:, 1, :] = (i[None, :] < i[:, None])
    c["c_strict"] = st
    slopes = np.exp2(-8.0 * np.arange(1, 7, dtype=np.float32) / 6).astype(np.float32)
    AL = np.zeros((128, 3, 6, 128), np.float32)
    s = i[:, None].astype(np.float32)
    q = i[None, :].astype(np.float32)
    for off in range(3):
        dist = np.abs(q - (s + (off - 1) * 128))
        for h in range(6):
            AL[:, off, h, :] = np.where(dist <= 128, -slopes[h] * dist, NEG)
    c["c_AL"] = AL.reshape(128, 3, 768)
    return c


def prep_weights(norm_w, w_in, conv_a_w, attn_sink, dn_conv_w, dn_a_log, dn_dt_bias, dn_norm_w, w_out, final_norm_w):
    L = w_in.shape[0]
    rep = lambda a: np.ascontiguousarray(np.broadcast_to(a[None], (128,) + a.shape)).astype(np.float32)
    m = {}
    m["w_in"] = np.ascontiguousarray(w_in, dtype=np.float32)
    m["w_out"] = np.ascontiguousarray(w_out, dtype=np.float32)
    m["nwB"] = rep(np.asarray(norm_w, np.float32))
    m["fnwB"] = rep(np.asarray(final_norm_w, np.float32))
    m["caw"] = np.ascontiguousarray(np.asarray(conv_a_w, np.float32).reshape(L, 3, 4, 128).transpose(3, 0, 2, 1))
    m["dcw"] = np.ascontiguousarray(np.asarray(dn_conv_w, np.float32).reshape(L, 3, 18, 128).transpose(3, 0, 2, 1))
    m["sinkB"] = rep(np.asarray(attn_sink, np.float32))
    m["alogB"] = rep(np.asarray(dn_a_log, np.float32).reshape(L, 12))
    m["dtbB"] = rep(np.asarray(dn_dt_bias, np.float32).reshape(L, 12))
    m["dnwB"] = rep(np.asarray(dn_norm_w, np.float32))
    return m


def core_inputs(x_seq, meta_tokens, NB):
    T = NB * 128
    h0 = np.zeros((T, D), np.float32)
    valid = np.zeros((T,), np.float32)
    if x_seq is not None:
        Sx = x_seq.shape[0]
        h0[PAD:LEAD] = meta_tokens
        h0[LEAD:LEAD + Sx] = x_seq
        valid[PAD:LEAD + Sx] = 1.0
    return {
        "h0": h0,
        "validT": np.ascontiguousarray(valid.reshape(NB, 128).T),
        "validB": np.ascontiguousarray(np.broadcast_to(valid[None], (128, T))),
    }


def kernel(x_prompt, x_sample, meta_tokens, norm_w, w_in, conv_a_w, attn_sink, dn_conv_w,
           dn_a_log, dn_dt_bias, dn_norm_w, w_out, final_norm_w):
    x_prompt = np.asarray(x_prompt, np.float32)
    x_sample = np.asarray(x_sample, np.float32)
    meta_tokens = np.asarray(meta_tokens, np.float32)
    L = w_in.shape[0]
    Sp = x_prompt.shape[1]
    Ss = x_sample.shape[1]
    NB = (LEAD + max(Sp, Ss)) // 128
    NBOUT = NB - 1
    shared = prep_weights(norm_w, np.asarray(w_in), conv_a_w, attn_sink, dn_conv_w, dn_a_log, dn_dt_bias,
                          dn_norm_w, np.asarray(w_out), final_norm_w)
    shared.update(make_consts())
    seqs = [x_prompt[i] for i in range(x_prompt.shape[0])] + [x_sample[i] for i in range(x_sample.shape[0])]
    assert len(seqs) <= 8
    in_maps = []
    for c in range(8):
        m = dict(shared)
        m.update(core_inputs(seqs[c] if c < len(seqs) else None, meta_tokens, NB))
        in_maps.append(m)
    nc = build_nc(NB, L, NBOUT)
    res = run_bass_kernel_spmd(nc, in_maps, core_ids=list(range(8)))
    outs = [np.asarray(r["out"]) for r in res.results]
    nP = x_prompt.shape[0]
    y_prompt = np.stack([outs[i][:Sp] for i in range(nP)], axis=0).astype(np.float32)
    y_sample = np.stack([outs[nP + i][:Ss] for i in range(x_sample.shape[0])], axis=0).astype(np.float32)
    return (y_prompt, y_sample)
```

```python
import contextlib
import numpy as np
import concourse.bass as bass
import concourse.mybir as mybir
from concourse.bass_utils import run_bass_kernel_spmd

F32 = mybir.dt.float32
BF16 = mybir.dt.bfloat16
AF = mybir.ActivationFunctionType
ALU = mybir.AluOpType

import os as _os
BCUT = int(_os.environ.get('BCUT', '0'))
CCUT = int(_os.environ.get('CCUT', '0'))
D = 2048
KC = 16
NPROJ = 7192
LEAD = 128
PAD = 112
NEG = -1.0e9
EPS = 1e-6


class Sched:
    COMPUTE = ("pe", "act", "dve", "pool")
    ALLENG = ("pe", "act", "dve", "pool", "sp")
    NDMA = {"sp": 8, "pool": 4}

    def __init__(self, nc):
        self.nc = nc
        self.gstack = contextlib.ExitStack()
        self.sems = {}
        for e in self.COMPUTE:
            self.sems["e:" + e] = self.gstack.enter_context(nc.semaphore("s_" + e))
        for q, k in self.NDMA.items():
            for j in range(k):
                self.sems[f"d:{q}:{j}"] = self.gstack.enter_context(nc.semaphore(f"d_{q}{j}"))
        self.seq = {e: 0 for e in self.COMPUTE}
        self.ndma = {q: 0 for q in self.NDMA}
        self.last_tok = {}
        self.known = {e: {} for e in self.ALLENG}
        self.last_w = {}
        self.readers = {}
        self.ops = {e: [] for e in self.ALLENG}
        self.final_tokens = []
        self.barrier_tokens = []
        self.pstack = None
        self.nops = 0
        self.bankmap = {}
        self.bank_last = {}

    def _uid(self):
        self.uid = getattr(self, "uid", 0) + 1
        return f"_{self.uid}"

    def sb(self, name, shape, dtype, glob=False):
        st = self.gstack if (glob or self.pstack is None) else self.pstack
        return st.enter_context(self.nc.sbuf_tensor("s_" + name + self._uid(), list(shape), dtype))

    def ps(self, name, shape, dtype, nbanks=1):
        nbytes = (4 if dtype == F32 else 2)
        for s_ in shape[1:]:
            nbytes *= s_
        assert nbytes == 2048 * nbanks, (name, shape, nbytes)
        self.bankmap[name] = [f"B:{name}:{i}" for i in range(nbanks)]
        return self.pstack.enter_context(self.nc.psum_tensor("p_" + name + self._uid(), list(shape), dtype))

    def _split(self, keys):
        reg, banks = [], []
        for k in keys:
            if k in self.bankmap:
                banks.extend(self.bankmap[k])
            else:
                reg.append(k)
        return reg, banks

    def _bank_deps(self, eng, banks):
        deps = []
        for bk in banks:
            for e2, tok in self.bank_last.get(bk, {}).items():
                if e2 != eng:
                    deps.append(tok)
        return deps

    def _bank_commit(self, eng, banks, token):
        for bk in banks:
            self.bank_last.setdefault(bk, {})[eng] = token

    def _deps(self, reads, writes):
        deps = list(self.barrier_tokens)
        for k in reads:
            t = self.last_w.get(k)
            if t is not None:
                deps.append(t)
        for k in writes:
            t = self.last_w.get(k)
            if t is not None:
                deps.append(t)
            deps.extend(self.readers.get(k, ()))
        return deps

    def _commit(self, token, reads, writes):
        self.last_tok[token[0]] = token[1]
        for k in reads:
            self.readers.setdefault(k, []).append(token)
        for k in writes:
            self.last_w[k] = token
            self.readers[k] = []

    def _waits(self, issuer, deps, skip_key=None):
        kn = self.known[issuer]
        need = {}
        for (key, val) in deps:
            if key == skip_key:
                continue
            if kn.get(key, 0) < val and need.get(key, 0) < val:
                need[key] = val
        for key, val in need.items():
            kn[key] = val
        return list(need.items())

    def op(self, eng, fn, reads=(), writes=()):
        reads, b1 = self._split(reads)
        writes, b2 = self._split(writes)
        banks = set(b1 + b2)
        deps = self._deps(reads, writes) + self._bank_deps(eng, banks)
        key = "e:" + eng
        waits = self._waits(eng, deps, skip_key=key if eng == "pe" else None)
        self.seq[eng] += 1
        token = (key, self.seq[eng])
        self.ops[eng].append((waits, fn, key, 1))
        self._commit(token, reads, writes)
        self._bank_commit(eng, banks, token)
        self.nops += 1
        return token

    def dma(self, q, out, in_, reads=(), writes=(), final=False):
        n = self.ndma[q]
        self.ndma[q] += 1
        K = self.NDMA[q]
        key = f"d:{q}:{n % K}"
        deps = self._deps(reads, writes)
        if n // K > 0:
            deps.append((key, 16 * (n // K)))
        waits = self._waits(q, deps)
        token = (key, 16 * (n // K + 1))
        self.ops[q].append((waits, lambda e: e.dma_start(out=out, in_=in_), key, 16))
        self._commit(token, reads, writes)
        if final:
            self.final_tokens.append(token)
        self.nops += 1
        return token

    def barrier(self):
        self.barrier_tokens = list(self.last_tok.items())

    @contextlib.contextmanager
    def phase(self, last=False):
        self.pstack = contextlib.ExitStack()
        self.barrier()
        yield self
        self._emit(last)
        self.pstack.close()
        self.pstack = None

    def _emit(self, last):
        nc = self.nc
        sems = self.sems
        fin = []
        if last:
            fin = self._waits("sp", self.final_tokens)
        ops = self.ops

        def run(engine, lst, tail=()):
            for (waits, fn, key, amt) in lst:
                for (k, v) in waits:
                    engine.wait_ge(sems[k], v)
                inst = fn(engine)
                inst.then_inc(sems[key], amt)
            for (k, v) in tail:
                engine.wait_ge(sems[k], v)

        with nc.Block() as block:
            @block.sync
            def _(e):
                run(e, ops["sp"], fin)

            @block.tensor
            def _(e):
                run(e, ops["pe"])

            @block.scalar
            def _(e):
                run(e, ops["act"])

            @block.vector
            def _(e):
                run(e, ops["dve"])

            @block.gpsimd
            def _(e):
                run(e, ops["pool"])
        self.ops = {e: [] for e in self.ALLENG}

    def close(self):
        self.gstack.close()


def build_nc(NB, L, NBOUT, G=4, stop=None):
    T = NB * 128
    nc = bass.Bass("TRN2", target_bir_lowering=False)
    dt_in = lambda n, s, d=F32: nc.dram_tensor(n, list(s), d, kind="ExternalInput").ap()
    dt_sc = lambda n, s, d=F32: nc.dram_tensor(n, list(s), d, kind="Internal").ap()

    h0 = dt_in("h0", [T, D])
    validT_d = dt_in("validT", [128, NB])
    validB_d = dt_in("validB", [128, T])
    w_in_d = dt_in("w_in", [L, D, NPROJ])
    w_out_d = dt_in("w_out", [L, D, D])
    nwB_d = dt_in("nwB", [128, L, D])
    fnwB_d = dt_in("fnwB", [128, D])
    caw_d = dt_in("caw", [128, L, 4, 3])
    dcw_d = dt_in("dcw", [128, L, 18, 3])
    sinkB_d = dt_in("sinkB", [128, L, 6])
    alogB_d = dt_in("alogB", [128, L, 12])
    dtbB_d = dt_in("dtbB", [128, L, 12])
    dnwB_d = dt_in("dnwB", [128, L, 128])
    c_identb_d = dt_in("c_identb", [128, 128])
    c_U_d = dt_in("c_U", [128, 2, 128])
    c_maskB_d = dt_in("c_maskB", [128, 2, 128])
    c_strict_d = dt_in("c_strict", [128, 2, 128])
    c_AL_d = dt_in("c_AL", [128, 3, 768])
    out_d = nc.dram_tensor("out", [NBOUT * 128, D], F32, kind="ExternalOutput").ap()

    Wb_in = dt_sc("Wb_in", [L, D, NPROJ], BF16)
    Wb_out = dt_sc("Wb_out", [L, D, D], BF16)
    H = dt_sc("H", [T, D])
    ZS = dt_sc("ZS", [D, T], BF16)
    UC = dt_sc("UC", [512, T])
    CB = dt_sc("CB", [512, T])
    QT = dt_sc("QT", [768, T], BF16)
    KT = dt_sc("KT", [256, T], BF16)
    VV = dt_sc("VV", [T, 256], BF16)
    DQKV = dt_sc("DQKV", [2304, T])
    BG = dt_sc("BG", [T, 24])
    KQT = dt_sc("KQT", [NB, 128, 6 * 2 * 128], BF16)
    KN = dt_sc("KN", [NB, 128, 768], BF16)
    VN = dt_sc("VN", [NB, 128, 768], BF16)
    OF = dt_sc("OF", [NB, 128, 768])
    YDN = dt_sc("YDN", [NB, 128, 768], BF16)

    S = Sched(nc)
    identb = S.sb("identb", [128, 128], BF16)
    identf = S.sb("identf", [128, 128], F32)
    Uf = S.sb("Uf", [128, 2, 128], F32)
    maskB = S.sb("maskB", [128, 2, 128], F32)
    strictM = S.sb("strictM", [128, 2, 128], F32)
    AL = S.sb("AL", [128, 3, 768], F32)
    onesf = S.sb("onesf", [128, 128], F32)
    onesb = S.sb("onesb", [128, 128], BF16)
    validT = S.sb("validT", [128, NB], F32)
    kbias = S.sb("kbias", [128, NB], F32)
    caw = S.sb("caw", [128, L, 4, 3], F32)
    dcw = S.sb("dcw", [128, L, 18, 3], F32)
    sinkB = S.sb("sinkB", [128, L, 6], F32)
    esink = S.sb("esink", [128, L, 6], F32)
    alogB = S.sb("alogB", [128, L, 12], F32)
    negA = S.sb("negA", [128, L, 12], F32)
    dtbB = S.sb("dtbB", [128, L, 12], F32)
    dnwB = S.sb("dnwB", [128, L, 128], F32)
    fnwB = S.sb("fnwB", [128, D], F32)
    nw = S.sb("nw", [128, D], F32)
    betaT = S.sb("betaT", [128, NB, 12], F32)
    nbetaT = S.sb("nbetaT", [128, NB, 12], F32)
    gT = S.sb("gT", [128, NB, 12], F32)
    epsc = S.sb("epsc", [128, 1], F32)
    onec = S.sb("onec", [128, 1], F32)

    with S.phase():
        S.dma("pool", identb[:], c_identb_d, writes=["identb"])
        S.dma("sp", identf[:], c_identb_d, writes=["identf"])
        S.dma("sp", Uf[:], c_U_d, writes=["Uf"])
        S.dma("sp", maskB[:], c_maskB_d, writes=["maskB"])
        S.dma("sp", strictM[:], c_strict_d, writes=["strictM"])
        S.dma("sp", AL[:], c_AL_d, writes=["AL"])
        S.dma("sp", validT[:], validT_d, writes=["validT"])
        S.dma("sp", caw[:], caw_d, writes=["caw"])
        S.dma("sp", dcw[:], dcw_d, writes=["dcw"])
        S.dma("sp", sinkB[:], sinkB_d, writes=["sinkB"])
        S.dma("sp", alogB[:], alogB_d, writes=["alogB"])
        S.dma("sp", dtbB[:], dtbB_d, writes=["dtbB"])
        S.dma("sp", dnwB[:], dnwB_d, writes=["dnwB"])
        S.dma("sp", fnwB[:], fnwB_d, writes=["fnwB"])
        S.op("pool", lambda e: e.memset(onesf[:], 1.0), writes=["onesf"])
        S.op("pool", lambda e: e.memset(onesb[:], 1.0), writes=["onesb"])
        S.op("pool", lambda e: e.memset(epsc[:], EPS), writes=["epsc"])
        S.op("pool", lambda e: e.memset(onec[:], 1.0), writes=["onec"])
        S.op("dve", lambda e: e.tensor_scalar(kbias[:], validT[:], -1.0, -NEG, ALU.add, ALU.mult),
             reads=["validT"], writes=["kbias"])
        S.op("act", lambda e: e.activation(esink[:], sinkB[:], AF.Exp), reads=["sinkB"], writes=["esink"])
        S.op("act", lambda e: e.activation(negA[:], alogB[:], AF.Exp), reads=["alogB"], writes=["negA"])
        S.op("dve", lambda e: e.tensor_scalar(negA[:], negA[:], -1.0, None, ALU.mult),
             reads=["negA"], writes=["negA"])
        for l in range(L):
            for kc in range(KC):
                S.dma("pool", Wb_in[l, kc * 128:(kc + 1) * 128, :], w_in_d[l, kc * 128:(kc + 1) * 128, :],
                      writes=[f"Wb_in{l}"])
            for kc in range(0, KC, 4):
                S.dma("pool", Wb_out[l, kc * 128:(kc + 4) * 128, :], w_out_d[l, kc * 128:(kc + 4) * 128, :],
                      writes=[f"Wb_out{l}"])

    if stop == '0':
        S.close()
        return nc
    for l in range(L):
        Hin = h0 if l == 0 else H
        hin_key = "h0" if l == 0 else "H"
        with S.phase():
            N = G * 128
            ht = [S.sb(f"ht{i}", [128, D], F32) for i in range(2)]
            junk = S.sb("junk", [128, D], BF16)
            xn = S.sb("xn", [128, D], BF16)
            xnT = S.sb("xnT", [128, KC, N], BF16)
            wt = [S.sb(f"wt{i}", [128, KC, 512], BF16) for i in range(2)]
            wlast = S.sb("wlast", [128, KC, 24], BF16)
            cxs = S.sb("cxs", [128, 4, N], F32)
            ef = [S.sb(f"ef{i}", [128, N], F32) for i in range(3)]
            eb = [S.sb(f"eb{i}", [128, N], BF16) for i in range(3)]
            vtok = S.sb("vtok", [128, 256], BF16)
            bgt = S.sb("bgt", [128, 24], F32)
            ss = S.sb("ss", [128, 2], F32)
            ptr = [S.ps(f"ptr{i}", [128, 8, 128], BF16) for i in range(2)]
            pa = [S.ps(f"pa{i}", [128, 512], F32) for i in range(2)]
            pv_ = S.ps("pv", [128, 512], F32)
            pv = pv_[:, 0:256]
            pbg_ = S.ps("pbg", [128, 512], F32)
            pbg = pbg_[:, 0:24]

            S.dma("sp", nw[:], nwB_d[:, l, :], writes=["nw"])
            S.dma("sp", wlast[:], Wb_in[l, :, 7168:7192].rearrange("(kc p) c -> p kc c", p=128),
                  reads=[f"Wb_in{l}"], writes=["wlast"])
            ngroups = (NB + G - 1) // G
            ecnt = [0]
            for gi in range(ngroups):
                b0 = gi * G
                gb = min(G, NB - b0)
                n = gb * 128
                for bi in range(gb):
                    b = b0 + bi
                    hh = ht[b % 2]
                    hk = f"ht{b % 2}"
                    S.dma("sp", hh[:], Hin[b * 128:(b + 1) * 128, :], reads=[hin_key], writes=[hk])
                    S.op("act", lambda e, hh=hh: e.activation(junk[:], hh[:], AF.Square, accum_out=ss[:, 0:1]),
                         reads=[hk], writes=["junk", "ss0"])
                    S.op("act", lambda e: e.activation(ss[:, 1:2], ss[:, 0:1], AF.Sqrt, bias=epsc[:, 0:1], scale=1.0 / D),
                         reads=["ss0", "epsc"], writes=["ss1"])
                    S.op("dve", lambda e: e.reciprocal(ss[:, 1:2], ss[:, 1:2]),
                         reads=["ss1"], writes=["ss1"])
                    S.op("dve", lambda e, hh=hh: e.scalar_tensor_tensor(xn[:], hh[:], ss[:, 1:2], nw[:], ALU.mult, ALU.mult),
                         reads=[hk, "ss1", "nw"], writes=["xn"])
                    for q4 in range(4):
                        pt = ptr[q4 % 2]
                        pk = f"ptr{q4 % 2}"

                        def tr(e, pt=pt, q4=q4):
                            r = None
                            for j in range(4):
                                kc = q4 * 4 + j
                                r = e.transpose(pt[:, j, :], xn[:, kc * 128:(kc + 1) * 128], identb[:])
                            return r
                        S.op("pe", tr, reads=["xn", "identb"], writes=[pk])
                        eng = "act" if q4 % 2 == 0 else "dve"
                        if eng == "act":
                            S.op("act", lambda e, pt=pt, q4=q4, bi=bi: e.copy(xnT[:, q4 * 4:(q4 + 1) * 4, bi * 128:(bi + 1) * 128], pt[:, 0:4, :]),
                                 reads=[pk], writes=[f"xnT{bi}"])
                        else:
                            S.op("dve", lambda e, pt=pt, q4=q4, bi=bi: e.tensor_copy(xnT[:, q4 * 4:(q4 + 1) * 4, bi * 128:(bi + 1) * 128], pt[:, 0:4, :]),
                                 reads=[pk], writes=[f"xnT{bi}"])
                xk = [f"xnT{bi}" for bi in range(gb)]
                for u in range(14):
                    w = wt[u % 2]
                    wk = f"wt{u % 2}"
                    S.dma("sp", w[:], Wb_in[l, :, u * 512:(u + 1) * 512].rearrange("(kc p) c -> p kc c", p=128),
                          reads=[f"Wb_in{l}"], writes=[wk])
                    for c4 in range(4):
                        col = u * 512 + c4 * 128
                        ch = col // 128
                        if 3072 <= col < 3328:
                            continue
                        p = pa[ecnt[0] % 2]
                        pk = f"pa{ecnt[0] % 2}"
                        ecnt[0] += 1

                        def mm(e, p=p, w=w, c4=c4, n=n):
                            r = None
                            for kc in range(KC):
                                r = e.matmul(p[:, 0:n], w[:, kc, c4 * 128:(c4 + 1) * 128], xnT[:, kc, 0:n],
                                             start=(kc == 0), stop=(kc == KC - 1))
                            return r
                        S.op("pe", mm, reads=[wk] + xk, writes=[pk])
                        tcols = slice(b0 * 128, b0 * 128 + n)
                        i3 = ch % 3
                        if col < 512:
                            S.op("act", lambda e, p=p, ch=ch, n=n: e.copy(cxs[:, ch, 0:n], p[:, 0:n]),
                                 reads=[pk], writes=[f"cxs{ch}"])
                        elif col < 1024:
                            S.op("act", lambda e, p=p, i3=i3, n=n: e.copy(ef[i3][:, 0:n], p[:, 0:n]),
                                 reads=[pk], writes=[f"ef{i3}"])
                            S.dma("sp", CB[col - 512:col - 512 + 128, tcols], ef[i3][:, 0:n], reads=[f"ef{i3}"], writes=["CB"])
                        elif col < 1536:
                            cxi = (col - 1024) // 128
                            S.op("dve", lambda e, p=p, i3=i3, n=n, cxi=cxi: e.tensor_tensor(ef[i3][:, 0:n], p[:, 0:n], cxs[:, cxi, 0:n], ALU.mult),
                                 reads=[pk, f"cxs{cxi}"], writes=[f"ef{i3}"])
                            S.dma("sp", UC[col - 1024:col - 1024 + 128, tcols], ef[i3][:, 0:n], reads=[f"ef{i3}"], writes=["UC"])
                        elif col < 2048 or 3328 <= col < 4096 or 6400 <= col < 7168:
                            if col < 2048:
                                zr = col - 1536
                            elif col < 4096:
                                zr = 512 + col - 3328
                            else:
                                zr = 1280 + col - 6400
                            S.op("act", lambda e, p=p, i3=i3, n=n: e.activation(eb[i3][:, 0:n], p[:, 0:n], AF.Silu),
                                 reads=[pk], writes=[f"eb{i3}"])
                            S.dma("sp", ZS[zr:zr + 128, tcols], eb[i3][:, 0:n], reads=[f"eb{i3}"], writes=["ZS"])
                        elif col < 2816:
                            S.op("act", lambda e, p=p, i3=i3, n=n: e.activation(eb[i3][:, 0:n], p[:, 0:n], AF.Copy, scale=128.0 ** -0.5),
                                 reads=[pk], writes=[f"eb{i3}"])
                            S.dma("sp", QT[col - 2048:col - 2048 + 128, tcols], eb[i3][:, 0:n], reads=[f"eb{i3}"], writes=["QT"])
                        elif col < 3072:
                            S.op("dve", lambda e, p=p, i3=i3, n=n: e.tensor_copy(eb[i3][:, 0:n], p[:, 0:n]),
                                 reads=[pk], writes=[f"eb{i3}"])
                            S.dma("sp", KT[col - 2816:col - 2816 + 128, tcols], eb[i3][:, 0:n], reads=[f"eb{i3}"], writes=["KT"])
                        else:
                            r0 = col - 4096
                            S.op("dve", lambda e, p=p, i3=i3, n=n: e.tensor_copy(ef[i3][:, 0:n], p[:, 0:n]),
                                 reads=[pk], writes=[f"ef{i3}"])
                            S.dma("sp", DQKV[r0:r0 + 128, tcols], ef[i3][:, 0:n], reads=[f"ef{i3}"], writes=["DQKV"])
                    if u == 6:
                        for bi in range(gb):
                            b = b0 + bi

                            def mmv(e, w=w, bi=bi):
                                r = None
                                for kc in range(KC):
                                    r = e.matmul(pv, xnT[:, kc, bi * 128:(bi + 1) * 128], w[:, kc, 0:256],
                                                 start=(kc == 0), stop=(kc == KC - 1))
                                return r
                            S.op("pe", mmv, reads=[wk, f"xnT{bi}"], writes=["pv"])
                            S.op("act", lambda e: e.copy(vtok[:], pv), reads=["pv"], writes=["vtok"])
                            S.dma("sp", VV[b * 128:(b + 1) * 128, :], vtok[:], reads=["vtok"], writes=["VV"])
                for bi in range(gb):
                    b = b0 + bi

                    def mmb(e, bi=bi):
                        r = None
                        for kc in range(KC):
                            r = e.matmul(pbg, xnT[:, kc, bi * 128:(bi + 1) * 128], wlast[:, kc, :],
                                         start=(kc == 0), stop=(kc == KC - 1))
                        return r
                    S.op("pe", mmb, reads=["wlast", f"xnT{bi}"], writes=["pbg"])
                    S.op("dve", lambda e: e.tensor_copy(bgt[:], pbg), reads=["pbg"], writes=["bgt"])
                    S.dma("sp", BG[b * 128:(b + 1) * 128, :], bgt[:], reads=["bgt"], writes=["BG"])

        if stop == 'A':
            S.close()
            return nc
        with S.phase():
            raw = S.sb("raw", [128, 18, 130], F32)
            cv = S.sb("cv", [128, 18, 128], F32)
            ctmp = S.sb("ctmp", [128, 18, 128], F32)
            sq = S.sb("sq", [128, 12, 128], BF16)
            rs = S.sb("rs", [128, 12, 128], F32)
            vB = S.sb("vB", [128, 128], F32)
            kq = S.sb("kq", [128, 6, 2, 128], BF16)
            vT = S.sb("vT", [128, 6, 128], BF16)
            kv_tok = S.sb("kv_tok", [128, 12, 128], BF16)
            bgt2 = S.sb("bgt2", [128, 24], F32)
            spt = S.sb("spt", [128, 12], F32)
            pss = S.ps("pss", [128, 12, 128], F32, nbanks=3)
            ptk_ = S.ps("ptk", [128, 16, 128], BF16, nbanks=2)
            ptk = ptk_[:, 0:12, :]
            for b in range(NB):
                t0 = b * 128
                lo = 1 if b == 0 else 0
                hi = 129 if b == NB - 1 else 130
                if b == 0:
                    S.op("pool", lambda e: e.memset(raw[:, :, 0:1], 0.0), writes=["raw"])
                if b == NB - 1:
                    S.op("pool", lambda e: e.memset(raw[:, :, 129:130], 0.0), writes=["raw"])
                S.dma("sp", raw[:, :, lo:hi],
                      DQKV[:, t0 - 1 + lo:t0 - 1 + hi].rearrange("(c p) t -> p c t", p=128),
                      reads=["DQKV"], writes=["raw"])
                S.dma("sp", vB[:], validB_d[:, t0:t0 + 128], writes=["vB"])
                S.dma("sp", bgt2[:], BG[t0:t0 + 128, :], reads=["BG"], writes=["bgt2"])
                S.op("act", lambda e, b=b: e.activation(betaT[:, b, :], bgt2[:, 0:12], AF.Sigmoid),
                     reads=["bgt2"], writes=["betaT"])
                S.op("dve", lambda e, b=b: e.tensor_scalar(nbetaT[:, b, :], betaT[:, b, :], -1.0, None, ALU.mult),
                     reads=["betaT"], writes=["nbetaT"])
                S.op("dve", lambda e: e.tensor_tensor(spt[:], bgt2[:, 12:24], dtbB[:, l, :], ALU.add),
                     reads=["bgt2", "dtbB"], writes=["spt"])
                S.op("act", lambda e: e.activation(spt[:], spt[:], AF.Exp), reads=["spt"], writes=["spt"])
                S.op("act", lambda e: e.activation(spt[:], spt[:], AF.Ln, bias=onec[:, 0:1]), reads=["spt", "onec"], writes=["spt"])
                S.op("dve", lambda e, b=b: e.tensor_tensor(gT[:, b, :], spt[:], negA[:, l, :], ALU.mult),
                     reads=["spt", "negA"], writes=["gT"])
                for c in range(18):
                    if c % 2 == 1:
                        S.op("dve", lambda e, c=c: e.tensor_scalar(cv[:, c, :], raw[:, c, 1:129], dcw[:, l, c, 1:2], None, ALU.mult),
                             reads=["raw", "dcw"], writes=[f"cv{c}"])
                        S.op("dve", lambda e, c=c: e.scalar_tensor_tensor(cv[:, c, :], raw[:, c, 0:128], dcw[:, l, c, 0:1], cv[:, c, :], ALU.mult, ALU.add),
                             reads=["raw", f"cv{c}"], writes=[f"cv{c}"])
                        S.op("dve", lambda e, c=c: e.scalar_tensor_tensor(cv[:, c, :], raw[:, c, 2:130], dcw[:, l, c, 2:3], cv[:, c, :], ALU.mult, ALU.add),
                             reads=["raw", f"cv{c}"], writes=[f"cv{c}"])
                    else:
                        S.op("pool", lambda e, c=c: e.tensor_scalar(cv[:, c, :], raw[:, c, 1:129], dcw[:, l, c, 1:2], None, ALU.mult),
                             reads=["raw", "dcw"], writes=[f"cv{c}"])
                        for (j, sl) in ((0, slice(0, 128)), (2, slice(2, 130))):
                            S.op("pool", lambda e, c=c, j=j, sl=sl: e.tensor_scalar(ctmp[:, c, :], raw[:, c, sl], dcw[:, l, c, j:j + 1], None, ALU.mult),
                                 reads=["raw", "dcw"], writes=[f"ctmp{c}"])
                            S.op("pool", lambda e, c=c: e.tensor_tensor(cv[:, c, :], cv[:, c, :], ctmp[:, c, :], ALU.add),
                                 reads=[f"ctmp{c}", f"cv{c}"], writes=[f"cv{c}"])
                cvk = [f"cv{c}" for c in range(18)]
                S.op("act", lambda e: e.activation(cv[:], cv[:], AF.Silu), reads=cvk, writes=cvk)
                S.op("pool", lambda e: e.tensor_tensor(sq[:], cv[:, 0:12, :], cv[:, 0:12, :], ALU.mult),
                     reads=cvk, writes=["sq"])

                def mss(e):
                    r = None
                    for j in range(3):
                        r = e.matmul(pss[:, j * 4:(j + 1) * 4, :], onesb[:], sq[:, j * 4:(j + 1) * 4, :], start=True, stop=True)
                    return r
                S.op("pe", mss, reads=["sq", "onesb"], writes=["pss"])
                S.op("act", lambda e: e.activation(rs[:], pss[:], AF.Sqrt, bias=epsc[:, 0:1], scale=1.0),
                     reads=["pss", "epsc"], writes=["rs"])
                S.op("dve", lambda e: e.reciprocal(rs[:], rs[:]), reads=["rs"], writes=["rs"])
                S.op("pool", lambda e: e.tensor_tensor(rs[:, 6:12, :], rs[:, 6:12, :], vB[:].unsqueeze(1).broadcast_to([128, 6, 128]), ALU.mult),
                     reads=["rs", "vB"], writes=["rs"])
                S.op("dve", lambda e: e.scalar_tensor_tensor(kq[:, :, 1, :], cv[:, 0:6, :], 128.0 ** -0.5, rs[:, 0:6, :], ALU.mult, ALU.mult),
                     reads=cvk + ["rs"], writes=["kq"])
                S.op("pool", lambda e: e.tensor_tensor(kq[:, :, 0, :], cv[:, 6:12, :], rs[:, 6:12, :], ALU.mult),
                     reads=cvk + ["rs"], writes=["kq"])
                S.op("act", lambda e: e.copy(vT[:], cv[:, 12:18, :]), reads=cvk, writes=["vT"])

                def trk(e):
                    r = None
                    for h in range(6):
                        r = e.transpose(ptk_[:, h, :], kq[:, h, 0, :], identb[:])
                    for h in range(6):
                        r = e.transpose(ptk_[:, 6 + h, :], vT[:, h, :], identb[:])
                    return r
                S.op("pe", trk, reads=["kq", "vT", "identb"], writes=["ptk"])
                S.op("act", lambda e: e.copy(kv_tok[:], ptk), reads=["ptk"], writes=["kv_tok"])
                S.dma("sp", KQT[b], kq[:].rearrange("p h two t -> p (h two t)"), reads=["kq"], writes=["KQT"])
                S.dma("sp", KN[b], kv_tok[:, 0:6, :].rearrange("p h d -> p (h d)"), reads=["kv_tok"], writes=["KN"])
                S.dma("sp", VN[b], kv_tok[:, 6:12, :].rearrange("p h d -> p (h d)"), reads=["kv_tok"], writes=["VN"])

        if stop == 'A2':
            S.close()
            return nc
        with S.phase():
            def T2(name, shape, dt):
                return [S.sb(f"{name}{i}", shape, dt) for i in range(2)]
            kqt = T2("kqt", [128, 6, 2, 128], BF16)
            knt = T2("knt", [128, 6, 128], BF16)
            vnt = T2("vnt", [128, 6, 128], BF16)
            gcs = T2("gcs", [128, 12], F32)
            ngc = T2("ngc", [128, 6], F32)
            egs = T2("egs", [128, 18], F32)
            gU = T2("gU", [128, 6, 128], F32)
            GCs = T2("GCs", [128, 6, 128], F32)
            Et = T2("Et", [128, 6, 128], F32)
            DT = T2("DT", [128, 6, 128], F32)
            EG = T2("EG", [128, 6, 128], F32)
            tt = T2("tt", [128, 6, 128], F32)
            qg = T2("qg", [128, 6, 128], BF16)
            MT = T2("MT", [128, 6, 128], BF16)
            X = [[S.sb(f"X{p}{i}", [128, 6, 3, 128], F32) for i in range(2)] for p in range(2)]
            Pb = T2("Pb", [128, 6, 128], BF16)
            kg = T2("kg", [128, 6, 128], BF16)
            kt = T2("kt", [128, 6, 128], BF16)
            nWT = S.sb("nWT", [128, 6, 128], BF16)
            vnew = S.sb("vnew", [128, 6, 128], BF16)
            Sf = S.sb("Sf", [128, 6, 128], F32)
            Sb = S.sb("Sb", [128, 6, 128], BF16)
            of = T2("of", [128, 6, 128], F32)
            ofl = T2("ofl", [128, 6, 128], F32)
            osq = S.sb("osq", [128, 6, 128], F32)
            zsd = T2("zsd", [128, 6, 128], BF16)
            ydn = S.sb("ydn", [128, 6, 128], BF16)
            rn = S.sb("rn", [128, 12], F32)
            Hps = [S.ps(f"H{h}", [128, 4, 128], F32) for h in range(6)]
            QA = S.ps("QA", [128, 4, 128], F32)
            QB = S.ps("QB", [128, 4, 128], F32)
            onf = S.sb("onf", [128, 6, 128], F32)

            def part0(dirn, b, par):
                S.dma("sp", kqt[par][:].rearrange("p h two t -> p (h two t)"), KQT[b], reads=["KQT"], writes=[f"kqt{par}"])
                S.dma("sp", knt[par][:].rearrange("p h d -> p (h d)"), KN[b], reads=["KN"], writes=[f"knt{par}"])
                S.dma("sp", vnt[par][:].rearrange("p h d -> p (h d)"), VN[b], reads=["VN"], writes=[f"vnt{par}"])
                if dirn == 1:
                    S.dma("sp", ofl[par][:].rearrange("p h d -> p (h d)"), OF[b], reads=["OF"], writes=[f"ofl{par}"])
                    S.dma("sp", zsd[par][:], ZS[1280:2048, b * 128:(b + 1) * 128].rearrange("(h p) t -> p h t", p=128),
                          reads=["ZS"], writes=[f"zsd{par}"])
                gsl = gT[:, b, dirn * 6:(dirn + 1) * 6]

                def mgc(e):
                    e.matmul(Hps[0][:, 3, 0:6], Uf[:, dirn, :], gsl, start=True, stop=True)
                    return e.matmul(Hps[0][:, 3, 6:12], onesf[:], gsl, start=True, stop=True)
                S.op("pe", mgc, reads=["Uf", "onesf", "gT"], writes=["H0"])
                S.op("dve", lambda e: e.tensor_copy(gcs[par][:], Hps[0][:, 3, 0:12]), reads=["H0"], writes=[f"gcs{par}"])
                S.op("dve", lambda e: e.tensor_scalar(ngc[par][:], gcs[par][:, 0:6], -1.0, None, ALU.mult), reads=[f"gcs{par}"], writes=[f"ngc{par}"])
                S.op("dve", lambda e: e.tensor_tensor(egs[par][:, 6:12], gcs[par][:, 6:12], gcs[par][:, 0:6], ALU.subtract), reads=[f"gcs{par}"], writes=[f"egs1{par}"])
                S.op("act", lambda e: e.activation(egs[par][:, 0:6], gcs[par][:, 0:6], AF.Exp), reads=[f"gcs{par}"], writes=[f"egs0{par}"])
                S.op("act", lambda e: e.activation(egs[par][:, 6:12], egs[par][:, 6:12], AF.Exp), reads=[f"egs1{par}"], writes=[f"egs1{par}"])
                S.op("act", lambda e: e.activation(egs[par][:, 12:18], gcs[par][:, 6:12], AF.Exp), reads=[f"gcs{par}"], writes=[f"egs2{par}"])

            def part1(dirn, b, par, h):
                sfx = f"{par}_{h}"
                Hh = Hps[h]
                hk = f"H{h}"
                kq_, kn_ = kqt[par], knt[par]
                gsl = gT[:, b, dirn * 6:(dirn + 1) * 6]
                nb_ = nbetaT[:, b, dirn * 6:(dirn + 1) * 6]
                Ud = Uf[:, dirn, :]
                Xa, Xb = X[par]
                S.op("act", lambda e: e.activation(gU[par][:, h, :], Ud, AF.Copy, scale=gsl[:, h:h + 1]),
                     reads=["Uf", "gT"], writes=[f"gU{sfx}"])
                S.op("act", lambda e: e.copy(Xa[:, h, 2, :], identf[:]), reads=["identf"], writes=[f"XaP{sfx}"])
                yield

                def m1(e):
                    e.matmul(Hh[:, 3, :], onesf[:], gU[par][:, h, :], start=True, stop=True)
                    return e.matmul(Hh[:, 1:3, :], kq_[:, h, 0, :], kq_[:, h, :, :], start=True, stop=True)
                S.op("pe", m1, reads=[f"gU{sfx}", "onesf", f"kqt{par}"], writes=[hk])
                yield
                S.op("dve", lambda e: e.tensor_tensor(Et[par][:, h, :], Hh[:, 3, :], maskB[:, dirn, :], ALU.add),
                     reads=[hk, "maskB"], writes=[f"Et{sfx}"])
                S.op("act", lambda e: e.activation(EG[par][:, h, :], Hh[:, 3, :], AF.Exp), reads=[hk], writes=[f"EG{sfx}"])
                yield
                S.op("act", lambda e: e.activation(DT[par][:, h, :], Et[par][:, h, :], AF.Exp, bias=ngc[par][:, h:h + 1]),
                     reads=[f"Et{sfx}", f"ngc{par}"], writes=[f"DT{sfx}"])
                S.op("dve", lambda e: e.tensor_tensor(qg[par][:, h, :], kq_[:, h, 1, :], EG[par][:, h, :], ALU.mult),
                     reads=[f"kqt{par}", f"EG{sfx}"], writes=[f"qg{sfx}"])
                S.op("act", lambda e: e.activation(kg[par][:, h, :], kn_[:, h, :], AF.Copy, scale=egs[par][:, h:h + 1]),
                     reads=[f"knt{par}", f"egs0{par}"], writes=[f"kg{sfx}"])
                S.op("act", lambda e: e.activation(kt[par][:, h, :], kn_[:, h, :], AF.Copy, scale=egs[par][:, 6 + h:7 + h]),
                     reads=[f"knt{par}", f"egs1{par}"], writes=[f"kt{sfx}"])
                yield
                S.op("pool", lambda e: e.tensor_tensor(tt[par][:, h, :], DT[par][:, h, :], strictM[:, dirn, :], ALU.mult),
                     reads=[f"DT{sfx}", "strictM"], writes=[f"tt{sfx}"])
                S.op("dve", lambda e: e.tensor_tensor(MT[par][:, h, :], Hh[:, 2, :], DT[par][:, h, :], ALU.mult),
                     reads=[hk, f"DT{sfx}"], writes=[f"MT{sfx}"])
                yield
                S.op("dve", lambda e: e.scalar_tensor_tensor(Xa[:, h, 1, :], Hh[:, 1, :], nb_[:, h:h + 1], tt[par][:, h, :], ALU.mult, ALU.mult),
                     reads=[hk, f"tt{sfx}", "nbetaT"], writes=[f"XaQ{sfx}"])
                yield
                S.op("pe", lambda e: e.transpose(Hh[:, 0, :], Xa[:, h, 1, :], identf[:]), reads=[f"XaQ{sfx}", "identf"], writes=[hk])
                yield
                S.op("act", lambda e: e.copy(Xa[:, h, 0, :], Hh[:, 0, :]), reads=[hk], writes=[f"XaT{sfx}"])
                yield
                cur = 0
                for lev in range(7):
                    Xc, Xn_ = (Xa, Xb) if cur == 0 else (Xb, Xa)
                    cc_ = "Xa" if cur == 0 else "Xb"
                    nn_ = "Xb" if cur == 0 else "Xa"

                    def mAB(e, Xc=Xc, lev=lev):
                        if lev < 6:
                            e.matmul(Hh[:, 1:3, :], Xc[:, h, 0, :], Xc[:, h, 1:3, :], start=True, stop=True)
                            return e.matmul(Hh[:, 0, :], Xc[:, h, 1, :], Xc[:, h, 0, :], start=True, stop=True)
                        return e.matmul(Hh[:, 2, :], Xc[:, h, 0, :], Xc[:, h, 2, :], start=True, stop=True)
                    S.op("pe", mAB, reads=[cc_ + "Q" + sfx, cc_ + "T" + sfx, cc_ + "P" + sfx], writes=[hk])
                    yield
                    if lev < 6:
                        S.op("act", lambda e, Xn_=Xn_: e.copy(Xn_[:, h, 0:2, :], Hh[:, 0:2, :]), reads=[hk],
                             writes=[nn_ + "Q" + sfx, nn_ + "T" + sfx])
                        S.op("dve", lambda e, Xn_=Xn_, Xc=Xc: e.tensor_tensor(Xn_[:, h, 2, :], Hh[:, 2, :], Xc[:, h, 2, :], ALU.add),
                             reads=[hk, cc_ + "P" + sfx], writes=[nn_ + "P" + sfx])
                    else:
                        S.op("dve", lambda e, Xc=Xc: e.tensor_tensor(Pb[par][:, h, :], Hh[:, 2, :], Xc[:, h, 2, :], ALU.add),
                             reads=[hk, cc_ + "P" + sfx], writes=[f"Pb{sfx}"])
                    yield
                    cur = 1 - cur

            def part2(dirn, b, par, h):
                sfx = f"{par}_{h}"
                Qp, qk = (QA, "QA") if h % 2 == 0 else (QB, "QB")
                bt_ = betaT[:, b, dirn * 6:(dirn + 1) * 6]
                S.op("pe", lambda e: e.matmul(Qp[:, 0, :], kg[par][:, h, :], Pb[par][:, h, :], start=True, stop=True),
                     reads=[f"kg{sfx}", f"Pb{sfx}", qk], writes=[qk + "W"])
                yield
                S.op("dve", lambda e: e.tensor_scalar(nWT[:, h, :], Qp[:, 0, :], -1.0, None, ALU.mult), reads=[qk + "W", qk], writes=[f"nWT{h}"])
                yield

                def mV(e):
                    e.matmul(Qp[:, 1, :], Pb[par][:, h, :], vnt[par][:, h, :], start=True, stop=False)
                    return e.matmul(Qp[:, 1, :], nWT[:, h, :], Sb[:, h, :], start=False, stop=True)
                S.op("pe", mV, reads=[f"Pb{sfx}", f"vnt{par}", f"nWT{h}", f"Sb{h}", qk], writes=[qk + "V"])
                yield
                S.op("dve", lambda e: e.tensor_scalar(vnew[:, h, :], Qp[:, 1, :], bt_[:, h:h + 1], None, ALU.mult),
                     reads=[qk + "V", qk, "betaT"], writes=[f"vnew{h}"])
                yield

                def mOS(e):
                    e.matmul(Qp[:, 2, :], qg[par][:, h, :], Sb[:, h, :], start=True, stop=False)
                    e.matmul(Qp[:, 2, :], MT[par][:, h, :], vnew[:, h, :], start=False, stop=True)
                    return e.matmul(Qp[:, 3, :], kt[par][:, h, :], vnew[:, h, :], start=True, stop=True)
                S.op("pe", mOS, reads=[f"qg{sfx}", f"Sb{h}", f"MT{sfx}", f"vnew{h}", f"kt{sfx}", qk], writes=[qk + "O", qk + "S"])
                yield
                S.op("dve", lambda e: e.scalar_tensor_tensor(Sf[:, h, :], Sf[:, h, :], egs[par][:, 12 + h:13 + h], Qp[:, 3, :], ALU.mult, ALU.add),
                     reads=[f"Sf{h}", f"egs2{par}", qk + "S", qk], writes=[f"Sf{h}"])
                if dirn == 0:
                    S.op("dve", lambda e: e.tensor_copy(of[par][:, h, :], Qp[:, 2, :]), reads=[qk + "O", qk], writes=[f"of{sfx}"])
                else:
                    S.op("dve", lambda e: e.tensor_tensor(of[par][:, h, :], Qp[:, 2, :], ofl[par][:, h, :], ALU.add),
                         reads=[qk + "O", qk, f"ofl{par}"], writes=[f"of{sfx}"])
                yield
                S.op("act", lambda e: e.copy(Sb[:, h, :], Sf[:, h, :]), reads=[f"Sf{h}"], writes=[f"Sb{h}"])
                yield

            def chain(*gs):
                for g_ in gs:
                    yield from g_

            def part3(dirn, b, par):
                ofk = [f"of{par}_{h}" for h in range(6)]
                if dirn == 0:
                    S.dma("sp", OF[b], of[par][:].rearrange("p h d -> p (h d)"), reads=ofk, writes=["OF"])
                    return
                o_ = of[par]
                S.op("pool", lambda e: e.tensor_tensor(osq[:], o_[:], o_[:], ALU.mult), reads=ofk, writes=["osq"])
                S.op("dve", lambda e: e.tensor_reduce(rn[:, 0:6], osq[:], mybir.AxisListType.X, ALU.add), reads=["osq"], writes=["rn0"])
                S.op("act", lambda e: e.activation(rn[:, 6:12], rn[:, 0:6], AF.Sqrt, bias=epsc[:, 0:1], scale=1.0 / 128), reads=["rn0", "epsc"], writes=["rn1"])
                S.op("dve", lambda e: e.reciprocal(rn[:, 6:12], rn[:, 6:12]), reads=["rn1"], writes=["rn1"])
                for h in range(6):
                    S.op("pool", lambda e, h=h: e.tensor_scalar(osq[:, h, :], o_[:, h, :], rn[:, 6 + h:7 + h], None, ALU.mult),
                         reads=ofk + ["rn1"], writes=["osq"])
                S.op("pool", lambda e: e.tensor_tensor(onf[:], osq[:], dnwB[:, l, :].unsqueeze(1).broadcast_to([128, 6, 128]), ALU.mult),
                     reads=["osq", "dnwB"], writes=["onf"])

                def trO(e):
                    r = None
                    for h in range(3):
                        r = e.transpose(QA[:, h, :], onf[:, h, :], identf[:])
                    for h in range(3):
                        r = e.transpose(QB[:, h, :], onf[:, 3 + h, :], identf[:])
                    return r
                S.op("pe", trO, reads=["onf", "identf", "QA", "QB"], writes=["QAW", "QAV", "QAO", "QBW", "QBV", "QBO"])
                S.op("dve", lambda e: e.tensor_tensor(ydn[:, 0:3, :], QA[:, 0:3, :], zsd[par][:, 0:3, :], ALU.mult),
                     reads=["QAW", "QAV", "QAO", "QA", f"zsd{par}"], writes=["ydn"])
                S.op("dve", lambda e: e.tensor_tensor(ydn[:, 3:6, :], QB[:, 0:3, :], zsd[par][:, 3:6, :], ALU.mult),
                     reads=["QBW", "QBV", "QBO", "QB", f"zsd{par}"], writes=["ydn"])
                S.dma("sp", YDN[b], ydn[:].rearrange("p h t -> p (h t)"), reads=["ydn"], writes=["YDN"])

            rrcnt = [0]

            def rr(gens):
                gens = list(gens)
                while gens:
                    alive = []
                    for gn in gens:
                        if BCUT and rrcnt[0] >= BCUT:
                            return
                        rrcnt[0] += 1
                        try:
                            next(gn)
                            alive.append(gn)
                        except StopIteration:
                            pass
                    gens = alive

            for dirn in range(2):
                S.op("pool", lambda e: e.memset(Sf[:], 0.0), writes=[f"Sf{h}" for h in range(6)])
                S.op("pool", lambda e: e.memset(Sb[:], 0.0), writes=[f"Sb{h}" for h in range(6)])
                blocks = list(range(NB)) if dirn == 0 else list(range(NB - 1, -1, -1))
                for i, b in enumerate(blocks):
                    par = i % 2
                    part0(dirn, b, par)
                    gl = []
                    if i > 0:
                        pb, pp = blocks[i - 1], 1 - par
                        gl.append(chain(*[part2(dirn, pb, pp, h) for h in (0, 2, 4)]))
                        gl.append(chain(*[part2(dirn, pb, pp, h) for h in (1, 3, 5)]))
                    for h in range(6):
                        gl.append(part1(dirn, b, par, h))
                    rr(gl)
                    if i > 0:
                        part3(dirn, blocks[i - 1], 1 - par)
                lp = (len(blocks) - 1) % 2
                rr([chain(*[part2(dirn, blocks[-1], lp, h) for h in (0, 2, 4)]),
                    chain(*[part2(dirn, blocks[-1], lp, h) for h in (1, 3, 5)])])
                part3(dirn, blocks[-1], lp)

        if stop == 'B':
            S.close()
            return nc
        with S.phase(last=(l == L - 1)):
            wout = S.sb("wout", [128, KC, D], BF16)
            hc = S.sb("hc", [128, D], F32)
            hn = S.sb("hn", [128, D], F32)
            yo = S.sb("yo", [128, D], F32)
            junk2 = S.sb("junk2", [128, D], BF16)
            ss2 = S.sb("ss2", [128, 2], F32)
            yT = S.sb("yT", [128, 16, 128], BF16)
            uc = S.sb("uc", [128, 4, 130], F32)
            cbt = S.sb("cbt", [128, 4, 128], F32)
            cvc = S.sb("cvc", [128, 4, 128], F32)
            cvt = S.sb("cvt", [128, 4, 128], F32)
            zst = S.sb("zst", [128, 10, 128], BF16)
            qt = S.sb("qt", [128, 6, 128], BF16)
            ktl = S.sb("ktl", [128, 2, 384], BF16)
            vl = S.sb("vl", [128, 3, 256], BF16)
            Ea = S.sb("Ea", [128, 384], F32)
            PTa = S.sb("PTa", [128, 384], BF16)
            den = S.sb("den", [128, 384], F32)
            oa = S.sb("oa", [128, 384], F32)
            pst = S.ps("pst", [128, 512], F32)[:, 0:384]
            pot = [S.ps(f"pot{g}", [128, 512], F32)[:, 0:384] for g in range(2)]
            pden = [S.ps(f"pden{g}", [128, 512], F32)[:, 0:384] for g in range(2)]
            po = [S.ps(f"po{i}", [128, 512], F32) for i in range(2)]

            S.dma("sp", wout[:], Wb_out[l].rearrange("(kc p) c -> p kc c", p=128), reads=[f"Wb_out{l}"], writes=["wout"])
            for b in range(NB):
                t0 = b * 128
                lo = 1 if b == 0 else 0
                hi = 129 if b == NB - 1 else 130
                if b == 0:
                    S.op("pool", lambda e: e.memset(uc[:, :, 0:1], 0.0), writes=["uc"])
                if b == NB - 1:
                    S.op("pool", lambda e: e.memset(uc[:, :, 129:130], 0.0), writes=["uc"])
                S.dma("sp", uc[:, :, lo:hi], UC[:, t0 - 1 + lo:t0 - 1 + hi].rearrange("(c p) t -> p c t", p=128),
                      reads=["UC"], writes=["uc"])
                S.dma("sp", cbt[:], CB[:, t0:t0 + 128].rearrange("(c p) t -> p c t", p=128), reads=["CB"], writes=["cbt"])
                S.dma("sp", zst[:], ZS[0:1280, t0:t0 + 128].rearrange("(c p) t -> p c t", p=128), reads=["ZS"], writes=["zst"])
                S.dma("sp", hc[:], Hin[t0:t0 + 128, :], reads=[hin_key], writes=["hc"])
                if CCUT == 1:
                    continue
                for c in range(4):
                    S.op("pool", lambda e, c=c: e.tensor_scalar(cvc[:, c, :], uc[:, c, 1:129], caw[:, l, c, 1:2], None, ALU.mult),
                         reads=["uc", "caw"], writes=[f"cvc{c}"])
                    for (j, sl) in ((0, slice(0, 128)), (2, slice(2, 130))):
                        S.op("pool", lambda e, c=c, j=j, sl=sl: e.tensor_scalar(cvt[:, c, :], uc[:, c, sl], caw[:, l, c, j:j + 1], None, ALU.mult),
                             reads=["uc", "caw"], writes=[f"cvt{c}"])
                        S.op("pool", lambda e, c=c: e.tensor_tensor(cvc[:, c, :], cvc[:, c, :], cvt[:, c, :], ALU.add),
                             reads=[f"cvt{c}", f"cvc{c}"], writes=[f"cvc{c}"])
                cvck = [f"cvc{c}" for c in range(4)]
                S.op("pool", lambda e: e.tensor_tensor(cvc[:], cvc[:], cbt[:], ALU.mult), reads=cvck + ["cbt"], writes=cvck)
                S.op("pool", lambda e: e.tensor_tensor(yT[:, 0:4, :], cvc[:], zst[:, 0:4, :], ALU.mult), reads=cvck + ["zst"], writes=["yT_c"])
                if CCUT == 2:
                    continue
                kbs = [kb for kb in (b - 1, b, b + 1) if 0 <= kb < NB]
                k0, k1 = kbs[0], kbs[-1]
                S.dma("sp", qt[:], QT[:, t0:t0 + 128].rearrange("(h p) t -> p h t", p=128), reads=["QT"], writes=["qt"])
                S.dma("sp", ktl[:, :, 0:(k1 - k0 + 1) * 128], KT[:, k0 * 128:(k1 + 1) * 128].rearrange("(g p) t -> p g t", p=128),
                      reads=["KT"], writes=["ktl"])
                S.dma("sp", vl[:, 0:(k1 - k0 + 1), :], VV[k0 * 128:(k1 + 1) * 128, :].rearrange("(j p) c -> p j c", p=128),
                      reads=["VV"], writes=["vl"])
                for g in range(2):
                    for ji, kb in enumerate(kbs):
                        off = kb - b + 1
                        j = kb - k0
                        first = (ji == 0)
                        lastk = (ji == len(kbs) - 1)
                        S.op("pe", lambda e, g=g, j=j: e.matmul(pst, ktl[:, g, j * 128:(j + 1) * 128], qt[:, 3 * g:3 * g + 3, :], start=True, stop=True),
                             reads=["ktl", "qt"], writes=["pst"])
                        S.op("dve", lambda e, g=g, off=off: e.tensor_tensor(Ea[:], pst, AL[:, off, g * 384:(g + 1) * 384], ALU.add),
                             reads=["pst", "AL"], writes=["Ea"])
                        S.op("act", lambda e, kb=kb: e.activation(PTa[:], Ea[:], AF.Exp, bias=kbias[:, kb:kb + 1]),
                             reads=["Ea", "kbias"], writes=["PTa"])

                        def mpv(e, g=g, j=j, first=first, lastk=lastk):
                            e.matmul(pot[g], vl[:, j, g * 128:(g + 1) * 128], PTa[:], start=first, stop=lastk)
                            return e.matmul(pden[g], onesb[:], PTa[:], start=first, stop=lastk)
                        S.op("pe", mpv, reads=["vl", "PTa", "onesb"], writes=[f"pot{g}", f"pden{g}"])
                    for hh in range(3):
                        h = 3 * g + hh
                        S.op("dve", lambda e, g=g, hh=hh, h=h: e.tensor_scalar(den[:, hh * 128:(hh + 1) * 128], pden[g][:, hh * 128:(hh + 1) * 128], esink[:, l, h:h + 1], None, ALU.add),
                             reads=[f"pden{g}", "esink"], writes=["den"])
                    S.op("dve", lambda e: e.reciprocal(den[:], den[:]), reads=["den"], writes=["den"])
                    S.op("dve", lambda e, g=g: e.tensor_tensor(oa[:], pot[g], den[:], ALU.mult), reads=[f"pot{g}", "den"], writes=["oa"])
                    S.op("pool", lambda e, g=g: e.tensor_tensor(yT[:, 4 + 3 * g:7 + 3 * g, :], oa[:].rearrange("p (h t) -> p h t", h=3), zst[:, 4 + 3 * g:7 + 3 * g, :], ALU.mult),
                         reads=["oa", "zst"], writes=[f"yT_a{g}"])
                if CCUT == 3:
                    continue
                S.dma("sp", yT[:, 10:16, :].rearrange("p h t -> p (h t)"), YDN[b], reads=["YDN"], writes=["yT_d"])
                if CCUT == 4:
                    continue
                ykeys = ["yT_c", "yT_a0", "yT_a1", "yT_d"]
                for n4 in range(4):
                    p = po[n4 % 2]
                    pk = f"po{n4 % 2}"

                    def mo(e, p=p, n4=n4):
                        r = None
                        for mc in range(16):
                            r = e.matmul(p[:], yT[:, mc, :], wout[:, mc, n4 * 512:(n4 + 1) * 512], start=(mc == 0), stop=(mc == 15))
                        return r
                    S.op("pe", mo, reads=ykeys + ["wout"], writes=[pk])
                    S.op("dve", lambda e, p=p, n4=n4, b=b: e.scalar_tensor_tensor(hn[:, n4 * 512:(n4 + 1) * 512], p[:], validT[:, b:b + 1], hc[:, n4 * 512:(n4 + 1) * 512], ALU.mult, ALU.add),
                         reads=[pk, "validT", "hc"], writes=[f"hn{n4}"])
                if CCUT == 5:
                    continue
                hnk = [f"hn{n4}" for n4 in range(4)]
                if l < L - 1:
                    S.dma("sp", H[t0:t0 + 128, :], hn[:], reads=hnk, writes=["H"])
                elif 1 <= b <= NBOUT:
                    S.op("act", lambda e: e.activation(junk2[:], hn[:], AF.Square, accum_out=ss2[:, 0:1]), reads=hnk, writes=["junk2", "ss2a"])
                    if CCUT == 6:
                        continue
                    S.op("act", lambda e: e.activation(ss2[:, 1:2], ss2[:, 0:1], AF.Sqrt, bias=epsc[:, 0:1], scale=1.0 / D), reads=["ss2a", "epsc"], writes=["ss2b"])
                    if CCUT == 7:
                        continue
                    S.op("dve", lambda e: e.reciprocal(ss2[:, 1:2], ss2[:, 1:2]), reads=["ss2b"], writes=["ss2b"])
                    if CCUT == 8:
                        continue
                    S.op("dve", lambda e: e.scalar_tensor_tensor(yo[:], hn[:], ss2[:, 1:2], fnwB[:], ALU.mult, ALU.mult),
                         reads=hnk + ["ss2b", "fnwB"], writes=["yo"])
                    if CCUT == 9:
                        continue
                    S.dma("sp", out_d[(b - 1) * 128:b * 128, :], yo[:], reads=["yo"], writes=["out"], final=True)
    S.close()
    return nc


def make_consts():
    import ml_dtypes
    i = np.arange(128)
    c = {}
    c["c_identb"] = np.eye(128, dtype=np.float32)
    U = np.zeros((128, 2, 128), np.float32)
    U[:, 0, :] = (i[:, None] <= i[None, :])
    U[:, 1, :] = (i[:, None] >= i[None, :])
    c["c_U"] = U
    mB = np.zeros((128, 2, 128), np.float32)
    mB[:, 0, :] = np.where(i[None, :] < i[:, None], NEG, 0.0)
    mB[:, 1, :] = np.where(i[None, :] > i[:, None], NEG, 0.0)
    c["c_maskB"] = mB
    st = np.zeros((128, 2, 128), np.float32)
    st[:, 0, :] = (i[None, :] > i[:, None])
    st[:, 1, :] = (i[None, :] < i[:, None])
    c["c_strict"] = st
    slopes = np.exp2(-8.0 * np.arange(1, 7, dtype=np.float32) / 6).astype(np.float32)
    AL = np.zeros((128, 3, 6, 128), np.float32)
    s = i[:, None].astype(np.float32)
    q = i[None, :].astype(np.float32)
    for off in range(3):
        dist = np.abs(q - (s + (off - 1) * 128))
        for h in range(6):
            AL[:, off, h, :] = np.where(dist <= 128, -slopes[h] * dist, NEG)
    c["c_AL"] = AL.reshape(128, 3, 768)
    return c


def prep_weights(norm_w, w_in, conv_a_w, attn_sink, dn_conv_w, dn_a_log, dn_dt_bias, dn_norm_w, w_out, final_norm_w):
    L = w_in.shape[0]
    rep = lambda a: np.ascontiguousarray(np.broadcast_to(a[None], (128,) + a.shape)).astype(np.float32)
    m = {}
    m["w_in"] = np.ascontiguousarray(w_in, dtype=np.float32)
    m["w_out"] = np.ascontiguousarray(w_out, dtype=np.float32)
    m["nwB"] = rep(np.asarray(norm_w, np.float32))
    m["fnwB"] = rep(np.asarray(final_norm_w, np.float32))
    m["caw"] = np.ascontiguousarray(np.asarray(conv_a_w, np.float32).reshape(L, 3, 4, 128).transpose(3, 0, 2, 1))
    m["dcw"] = np.ascontiguousarray(np.asarray(dn_conv_w, np.float32).reshape(L, 3, 18, 128).transpose(3, 0, 2, 1))
    m["sinkB"] = rep(np.asarray(attn_sink, np.float32))
    m["alogB"] = rep(np.asarray(dn_a_log, np.float32).reshape(L, 12))
    m["dtbB"] = rep(np.asarray(dn_dt_bias, np.float32).reshape(L, 12))
    m["dnwB"] = rep(np.asarray(dn_norm_w, np.float32))
    return m


def core_inputs(x_seq, meta_tokens, NB):
    T = NB * 128
    h0 = np.zeros((T, D), np.float32)
    valid = np.zeros((T,), np.float32)
    if x_seq is not None:
        Sx = x_seq.shape[0]
        h0[PAD:LEAD] = meta_tokens
        h0[LEAD:LEAD + Sx] = x_seq
        valid[PAD:LEAD + Sx] = 1.0
    return {
        "h0": h0,
        "validT": np.ascontiguousarray(valid.reshape(NB, 128).T),
        "validB": np.ascontiguousarray(np.broadcast_to(valid[None], (128, T))),
    }


def kernel(x_prompt, x_sample, meta_tokens, norm_w, w_in, conv_a_w, attn_sink, dn_conv_w,
           dn_a_log, dn_dt_bias, dn_norm_w, w_out, final_norm_w):
    x_prompt = np.asarray(x_prompt, np.float32)
    x_sample = np.asarray(x_sample, np.float32)
    meta_tokens = np.asarray(meta_tokens, np.float32)
    L = w_in.shape[0]
    Sp = x_prompt.shape[1]
    Ss = x_sample.shape[1]
    NB = (LEAD + max(Sp, Ss)) // 128
    NBOUT = NB - 1
    shared = prep_weights(norm_w, np.asarray(w_in), conv_a_w, attn_sink, dn_conv_w, dn_a_log, dn_dt_bias,
                          dn_norm_w, np.asarray(w_out), final_norm_w)
    shared.update(make_consts())
    seqs = [x_prompt[i] for i in range(x_prompt.shape[0])] + [x_sample[i] for i in range(x_sample.shape[0])]
    assert len(seqs) <= 8
    in_maps = []
    for c in range(8):
        m = dict(shared)
        m.update(core_inputs(seqs[c] if c < len(seqs) else None, meta_tokens, NB))
        in_maps.append(m)
    nc = build_nc(NB, L, NBOUT)
    res = run_bass_kernel_spmd(nc, in_maps, core_ids=list(range(8)))
    outs = [np.asarray(r["out"]) for r in res.results]
    nP = x_prompt.shape[0]
    y_prompt = np.stack([outs[i][:Sp] for i in range(nP)], axis=0).astype(np.float32)
    y_sample = np.stack([outs[nP + i][:Ss] for i in range(x_sample.shape[0])], axis=0).astype(np.float32)
    return (y_prompt, y_sample)
```

```python
import contextlib
import numpy as np
import concourse.bass as bass
import concourse.mybir as mybir
from concourse.bass_utils import run_bass_kernel_spmd

F32 = mybir.dt.float32
BF16 = mybir.dt.bfloat16
AF = mybir.ActivationFunctionType
ALU = mybir.AluOpType

import os as _os
BCUT = int(_os.environ.get('BCUT', '0'))
CCUT = int(_os.environ.get('CCUT', '0'))
D = 2048
KC = 16
NPROJ = 7192
LEAD = 128
PAD = 112
NEG = -1.0e9
EPS = 1e-6


class Sched:
    COMPUTE = ("pe", "act", "dve", "pool")
    ALLENG = ("pe", "act", "dve", "pool", "sp")
    NDMA = {"sp": 8, "pool": 4}

    def __init__(self, nc):
        self.nc = nc
        self.gstack = contextlib.ExitStack()
        self.sems = {}
        for e in self.COMPUTE:
            self.sems["e:" + e] = self.gstack.enter_context(nc.semaphore("s_" + e))
        for q, k in self.NDMA.items():
            for j in range(k):
                self.sems[f"d:{q}:{j}"] = self.gstack.enter_context(nc.semaphore(f"d_{q}{j}"))
        self.seq = {e: 0 for e in self.COMPUTE}
        self.ndma = {q: 0 for q in self.NDMA}
        self.last_tok = {}
        self.known = {e: {} for e in self.ALLENG}
        self.last_w = {}
        self.readers = {}
        self.ops = {e: [] for e in self.ALLENG}
        self.final_tokens = []
        self.barrier_tokens = []
        self.pstack = None
        self.nops = 0
        self.bankmap = {}
        self.bank_last = {}

    def _uid(self):
        self.uid = getattr(self, "uid", 0) + 1
        return f"_{self.uid}"

    def sb(self, name, shape, dtype, glob=False):
        st = self.gstack if (glob or self.pstack is None) else self.pstack
        return st.enter_context(self.nc.sbuf_tensor("s_" + name + self._uid(), list(shape), dtype))

    def ps(self, name, shape, dtype, nbanks=1):
        nbytes = (4 if dtype == F32 else 2)
        for s_ in shape[1:]:
            nbytes *= s_
        assert nbytes == 2048 * nbanks, (name, shape, nbytes)
        self.bankmap[name] = [f"B:{name}:{i}" for i in range(nbanks)]
        return self.pstack.enter_context(self.nc.psum_tensor("p_" + name + self._uid(), list(shape), dtype))

    def _split(self, keys):
        reg, banks = [], []
        for k in keys:
            if k in self.bankmap:
                banks.extend(self.bankmap[k])
            else:
                reg.append(k)
        return reg, banks

    def _bank_deps(self, eng, banks):
        deps = []
        for bk in banks:
            for e2, tok in self.bank_last.get(bk, {}).items():
                if e2 != eng:
                    deps.append(tok)
        return deps

    def _bank_commit(self, eng, banks, token):
        for bk in banks:
            self.bank_last.setdefault(bk, {})[eng] = token

    def _deps(self, reads, writes):
        deps = list(self.barrier_tokens)
        for k in reads:
            t = self.last_w.get(k)
            if t is not None:
                deps.append(t)
        for k in writes:
            t = self.last_w.get(k)
            if t is not None:
                deps.append(t)
            deps.extend(self.readers.get(k, ()))
        return deps

    def _commit(self, token, reads, writes):
        self.last_tok[token[0]] = token[1]
        for k in reads:
            self.readers.setdefault(k, []).append(token)
        for k in writes:
            self.last_w[k] = token
            self.readers[k] = []

    def _waits(self, issuer, deps, skip_key=None):
        kn = self.known[issuer]
        need = {}
        for (key, val) in deps:
            if key == skip_key:
                continue
            if kn.get(key, 0) < val and need.get(key, 0) < val:
                need[key] = val
        for key, val in need.items():
            kn[key] = val
        return list(need.items())

    def op(self, eng, fn, reads=(), writes=()):
        reads, b1 = self._split(reads)
        writes, b2 = self._split(writes)
        banks = set(b1 + b2)
        deps = self._deps(reads, writes) + self._bank_deps(eng, banks)
        key = "e:" + eng
        waits = self._waits(eng, deps, skip_key=key if eng == "pe" else None)
        self.seq[eng] += 1
        token = (key, self.seq[eng])
        self.ops[eng].append((waits, fn, key, 1))
        self._commit(token, reads, writes)
        self._bank_commit(eng, banks, token)
        self.nops += 1
        return token

    def dma(self, q, out, in_, reads=(), writes=(), final=False):
        n = self.ndma[q]
        self.ndma[q] += 1
        K = self.NDMA[q]
        key = f"d:{q}:{n % K}"
        deps = self._deps(reads, writes)
        if n // K > 0:
            deps.append((key, 16 * (n // K)))
        waits = self._waits(q, deps)
        token = (key, 16 * (n // K + 1))
        self.ops[q].append((waits, lambda e: e.dma_start(out=out, in_=in_), key, 16))
        self._commit(token, reads, writes)
        if final:
            self.final_tokens.append(token)
        self.nops += 1
        return token

    def barrier(self):
        self.barrier_tokens = list(self.last_tok.items())

    @contextlib.contextmanager
    def phase(self, last=False):
        self.pstack = contextlib.ExitStack()
        self.barrier()
        yield self
        self._emit(last)
        self.pstack.close()
        self.pstack = None

    def _emit(self, last):
        nc = self.nc
        sems = self.sems
        fin = []
        if last:
            fin = self._waits("sp", self.final_tokens)
        ops = self.ops

        def run(engine, lst, tail=()):
            for (waits, fn, key, amt) in lst:
                for (k, v) in waits:
                    engine.wait_ge(sems[k], v)
                inst = fn(engine)
                inst.then_inc(sems[key], amt)
            for (k, v) in tail:
                engine.wait_ge(sems[k], v)

        with nc.Block() as block:
            @block.sync
            def _(e):
                run(e, ops["sp"], fin)

            @block.tensor
            def _(e):
                run(e, ops["pe"])

            @block.scalar
            def _(e):
                run(e, ops["act"])

            @block.vector
            def _(e):
                run(e, ops["dve"])

            @block.gpsimd
            def _(e):
                run(e, ops["pool"])
        self.ops = {e: [] for e in self.ALLENG}

    def close(self):
        self.gstack.close()


def build_nc(NB, L, NBOUT, G=4, stop=None):
    T = NB * 128
    nc = bass.Bass("TRN2", target_bir_lowering=False)
    dt_in = lambda n, s, d=F32: nc.dram_tensor(n, list(s), d, kind="ExternalInput").ap()
    dt_sc = lambda n, s, d=F32: nc.dram_tensor(n, list(s), d, kind="Internal").ap()

    h0 = dt_in("h0", [T, D])
    validT_d = dt_in("validT", [128, NB])
    validB_d = dt_in("validB", [128, T])
    w_in_d = dt_in("w_in", [L, D, NPROJ])
    w_out_d = dt_in("w_out", [L, D, D])
    nwB_d = dt_in("nwB", [128, L, D])
    fnwB_d = dt_in("fnwB", [128, D])
    caw_d = dt_in("caw", [128, L, 4, 3])
    dcw_d = dt_in("dcw", [128, L, 18, 3])
    sinkB_d = dt_in("sinkB", [128, L, 6])
    alogB_d = dt_in("alogB", [128, L, 12])
    dtbB_d = dt_in("dtbB", [128, L, 12])
    dnwB_d = dt_in("dnwB", [128, L, 128])
    c_identb_d = dt_in("c_identb", [128, 128])
    c_U_d = dt_in("c_U", [128, 2, 128])
    c_maskB_d = dt_in("c_maskB", [128, 2, 128])
    c_strict_d = dt_in("c_strict", [128, 2, 128])
    c_AL_d = dt_in("c_AL", [128, 3, 768])
    out_d = nc.dram_tensor("out", [NBOUT * 128, D], F32, kind="ExternalOutput").ap()

    Wb_in = dt_sc("Wb_in", [L, D, NPROJ], BF16)
    Wb_out = dt_sc("Wb_out", [L, D, D], BF16)
    H = dt_sc("H", [T, D])
    ZS = dt_sc("ZS", [D, T], BF16)
    UC = dt_sc("UC", [512, T])
    CB = dt_sc("CB", [512, T])
    QT = dt_sc("QT", [768, T], BF16)
    KT = dt_sc("KT", [256, T], BF16)
    VV = dt_sc("VV", [T, 256], BF16)
    DQKV = dt_sc("DQKV", [2304, T])
    BG = dt_sc("BG", [T, 24])
    KQT = dt_sc("KQT", [NB, 128, 6 * 2 * 128], BF16)
    KN = dt_sc("KN", [NB, 128, 768], BF16)
    VN = dt_sc("VN", [NB, 128, 768], BF16)
    OF = dt_sc("OF", [NB, 128, 768])
    YDN = dt_sc("YDN", [NB, 128, 768], BF16)

    S = Sched(nc)
    identb = S.sb("identb", [128, 128], BF16)
    identf = S.sb("identf", [128, 128], F32)
    Uf = S.sb("Uf", [128, 2, 128], F32)
    maskB = S.sb("maskB", [128, 2, 128], F32)
    strictM = S.sb("strictM", [128, 2, 128], F32)
    AL = S.sb("AL", [128, 3, 768], F32)
    onesf = S.sb("onesf", [128, 128], F32)
    onesb = S.sb("onesb", [128, 128], BF16)
    validT = S.sb("validT", [128, NB], F32)
    kbias = S.sb("kbias", [128, NB], F32)
    caw = S.sb("caw", [128, L, 4, 3], F32)
    dcw = S.sb("dcw", [128, L, 18, 3], F32)
    sinkB = S.sb("sinkB", [128, L, 6], F32)
    esink = S.sb("esink", [128, L, 6], F32)
    alogB = S.sb("alogB", [128, L, 12], F32)
    negA = S.sb("negA", [128, L, 12], F32)
    dtbB = S.sb("dtbB", [128, L, 12], F32)
    dnwB = S.sb("dnwB", [128, L, 128], F32)
    fnwB = S.sb("fnwB", [128, D], F32)
    nw = S.sb("nw", [128, D], F32)
    betaT = S.sb("betaT", [128, NB, 12], F32)
    nbetaT = S.sb("nbetaT", [128, NB, 12], F32)
    gT = S.sb("gT", [128, NB, 12], F32)
    epsc = S.sb("epsc", [128, 1], F32)
    onec = S.sb("onec", [128, 1], F32)

    with S.phase():
        S.dma("pool", identb[:], c_identb_d, writes=["identb"])
        S.dma("sp", identf[:], c_identb_d, writes=["identf"])
        S.dma("sp", Uf[:], c_U_d, writes=["Uf"])
        S.dma("sp", maskB[:], c_maskB_d, writes=["maskB"])
        S.dma("sp", strictM[:], c_strict_d, writes=["strictM"])
        S.dma("sp", AL[:], c_AL_d, writes=["AL"])
        S.dma("sp", validT[:], validT_d, writes=["validT"])
        S.dma("sp", caw[:], caw_d, writes=["caw"])
        S.dma("sp", dcw[:], dcw_d, writes=["dcw"])
        S.dma("sp", sinkB[:], sinkB_d, writes=["sinkB"])
        S.dma("sp", alogB[:], alogB_d, writes=["alogB"])
        S.dma("sp", dtbB[:], dtbB_d, writes=["dtbB"])
        S.dma("sp", dnwB[:], dnwB_d, writes=["dnwB"])
        S.dma("sp", fnwB[:], fnwB_d, writes=["fnwB"])
        S.op("pool", lambda e: e.memset(onesf[:], 1.0), writes=["onesf"])
        S.op("pool", lambda e: e.memset(onesb[:], 1.0), writes=["onesb"])
        S.op("pool", lambda e: e.memset(epsc[:], EPS), writes=["epsc"])
        S.op("pool", lambda e: e.memset(onec[:], 1.0), writes=["onec"])
        S.op("dve", lambda e: e.tensor_scalar(kbias[:], validT[:], -1.0, -NEG, ALU.add, ALU.mult),
             reads=["validT"], writes=["kbias"])
        S.op("act", lambda e: e.activation(esink[:], sinkB[:], AF.Exp), reads=["sinkB"], writes=["esink"])
        S.op("act", lambda e: e.activation(negA[:], alogB[:], AF.Exp), reads=["alogB"], writes=["negA"])
        S.op("dve", lambda e: e.tensor_scalar(negA[:], negA[:], -1.0, None, ALU.mult),
             reads=["negA"], writes=["negA"])
        def convert_weights(lw):
            for kc in range(KC):
                S.dma("pool", Wb_in[lw, kc * 128:(kc + 1) * 128, :], w_in_d[lw, kc * 128:(kc + 1) * 128, :],
                      writes=[f"Wb_in{lw}"])
            for kc in range(0, KC, 4):
                S.dma("pool", Wb_out[lw, kc * 128:(kc + 4) * 128, :], w_out_d[lw, kc * 128:(kc + 4) * 128, :],
                      writes=[f"Wb_out{lw}"])
        convert_weights(0)

    if stop == '0':
        S.close()
        return nc
    for l in range(L):
        Hin = h0 if l == 0 else H
        hin_key = "h0" if l == 0 else "H"
        with S.phase():
            N = G * 128
            ht = [S.sb(f"ht{i}", [128, D], F32) for i in range(2)]
            junk = S.sb("junk", [128, D], BF16)
            xn = S.sb("xn", [128, D], BF16)
            xnT2 = [S.sb(f"xnT{i}", [128, KC, N], BF16) for i in range(2)]
            wt = [S.sb(f"wt{i}", [128, KC, 512], BF16) for i in range(2)]
            wlast = S.sb("wlast", [128, KC, 24], BF16)
            cxs = S.sb("cxs", [128, 4, N], F32)
            ef = [S.sb(f"ef{i}", [128, N], F32) for i in range(3)]
            eb = [S.sb(f"eb{i}", [128, N], BF16) for i in range(3)]
            vtok = S.sb("vtok", [128, 256], BF16)
            bgt = S.sb("bgt", [128, 24], F32)
            ss = S.sb("ss", [128, 2], F32)
            ptr = [S.ps(f"ptr{i}", [128, 8, 128], BF16) for i in range(2)]
            pa = [S.ps(f"pa{i}", [128, 512], F32) for i in range(2)]
            pv_ = S.ps("pv", [128, 512], F32)
            pv = pv_[:, 0:256]
            pbg_ = S.ps("pbg", [128, 512], F32)
            pbg = pbg_[:, 0:24]

            S.dma("sp", nw[:], nwB_d[:, l, :], writes=["nw"])
            S.dma("sp", wlast[:], Wb_in[l, :, 7168:7192].rearrange("(kc p) c -> p kc c", p=128),
                  reads=[f"Wb_in{l}"], writes=["wlast"])
            ngroups = (NB + G - 1) // G
            ecnt = [0]

            def norm_gen(gi):
                gp = gi % 2
                xnT = xnT2[gp]
                b0 = gi * G
                gb = min(G, NB - b0)
                for bi in range(gb):
                    b = b0 + bi
                    hh = ht[b % 2]
                    hk = f"ht{b % 2}"
                    S.dma("sp", hh[:], Hin[b * 128:(b + 1) * 128, :], reads=[hin_key], writes=[hk])
                    S.op("act", lambda e, hh=hh: e.activation(junk[:], hh[:], AF.Square, accum_out=ss[:, 0:1]),
                         reads=[hk], writes=["junk", "ss0"])
                    S.op("act", lambda e: e.activation(ss[:, 1:2], ss[:, 0:1], AF.Sqrt, bias=epsc[:, 0:1], scale=1.0 / D),
                         reads=["ss0", "epsc"], writes=["ss1"])
                    S.op("dve", lambda e: e.reciprocal(ss[:, 1:2], ss[:, 1:2]),
                         reads=["ss1"], writes=["ss1"])
                    S.op("dve", lambda e, hh=hh: e.scalar_tensor_tensor(xn[:], hh[:], ss[:, 1:2], nw[:], ALU.mult, ALU.mult),
                         reads=[hk, "ss1", "nw"], writes=["xn"])
                    yield
                    for q4 in range(4):
                        pt = ptr[q4 % 2]
                        pk = f"ptr{q4 % 2}"

                        def tr(e, pt=pt, q4=q4):
                            r = None
                            for j in range(4):
                                kc = q4 * 4 + j
                                r = e.transpose(pt[:, j, :], xn[:, kc * 128:(kc + 1) * 128], identb[:])
                            return r
                        S.op("pe", tr, reads=["xn", "identb"], writes=[pk])
                        if q4 % 2 == 0:
                            S.op("act", lambda e, pt=pt, q4=q4, bi=bi: e.copy(xnT[:, q4 * 4:(q4 + 1) * 4, bi * 128:(bi + 1) * 128], pt[:, 0:4, :]),
                                 reads=[pk], writes=[f"xnT{gp}_{bi}"])
                        else:
                            S.op("dve", lambda e, pt=pt, q4=q4, bi=bi: e.tensor_copy(xnT[:, q4 * 4:(q4 + 1) * 4, bi * 128:(bi + 1) * 128], pt[:, 0:4, :]),
                                 reads=[pk], writes=[f"xnT{gp}_{bi}"])
                        if q4 % 2 == 1:
                            yield

            def drain(gen, n=None):
                if gen is None:
                    return
                k = 0
                for _ in gen:
                    k += 1
                    if n is not None and k >= n:
                        return

            drain(norm_gen(0))
            for gi in range(ngroups):
                gp = gi % 2
                xnT = xnT2[gp]
                b0 = gi * G
                gb = min(G, NB - b0)
                n = gb * 128
                nxt = norm_gen(gi + 1) if gi + 1 < ngroups else None
                xk = [f"xnT{gp}_{bi}" for bi in range(gb)]
                for u in range(14):
                    w = wt[u % 2]
                    wk = f"wt{u % 2}"
                    S.dma("sp", w[:], Wb_in[l, :, u * 512:(u + 1) * 512].rearrange("(kc p) c -> p kc c", p=128),
                          reads=[f"Wb_in{l}"], writes=[wk])
                    for c4 in range(4):
                        col = u * 512 + c4 * 128
                        ch = col // 128
                        if 3072 <= col < 3328:
                            continue
                        p = pa[ecnt[0] % 2]
                        pk = f"pa{ecnt[0] % 2}"
                        ecnt[0] += 1

                        def mm(e, p=p, w=w, c4=c4, n=n, xnT=xnT):
                            r = None
                            for kc in range(KC):
                                r = e.matmul(p[:, 0:n], w[:, kc, c4 * 128:(c4 + 1) * 128], xnT[:, kc, 0:n],
                                             start=(kc == 0), stop=(kc == KC - 1))
                            return r
                        S.op("pe", mm, reads=[wk] + xk, writes=[pk])
                        tcols = slice(b0 * 128, b0 * 128 + n)
                        i3 = ch % 3
                        if col < 512:
                            S.op("act", lambda e, p=p, ch=ch, n=n: e.copy(cxs[:, ch, 0:n], p[:, 0:n]),
                                 reads=[pk], writes=[f"cxs{ch}"])
                        elif col < 1024:
                            S.op("act", lambda e, p=p, i3=i3, n=n: e.copy(ef[i3][:, 0:n], p[:, 0:n]),
                                 reads=[pk], writes=[f"ef{i3}"])
                            S.dma("pool", CB[col - 512:col - 512 + 128, tcols], ef[i3][:, 0:n], reads=[f"ef{i3}"], writes=["CB"])
                        elif col < 1536:
                            cxi = (col - 1024) // 128
                            S.op("dve", lambda e, p=p, i3=i3, n=n, cxi=cxi: e.tensor_tensor(ef[i3][:, 0:n], p[:, 0:n], cxs[:, cxi, 0:n], ALU.mult),
                                 reads=[pk, f"cxs{cxi}"], writes=[f"ef{i3}"])
                            S.dma("pool", UC[col - 1024:col - 1024 + 128, tcols], ef[i3][:, 0:n], reads=[f"ef{i3}"], writes=["UC"])
                        elif col < 2048 or 3328 <= col < 4096 or 6400 <= col < 7168:
                            if col < 2048:
                                zr = col - 1536
                            elif col < 4096:
                                zr = 512 + col - 3328
                            else:
                                zr = 1280 + col - 6400
                            S.op("act", lambda e, p=p, i3=i3, n=n: e.activation(eb[i3][:, 0:n], p[:, 0:n], AF.Silu),
                                 reads=[pk], writes=[f"eb{i3}"])
                            S.dma("pool", ZS[zr:zr + 128, tcols], eb[i3][:, 0:n], reads=[f"eb{i3}"], writes=["ZS"])
                        elif col < 2816:
                            S.op("act", lambda e, p=p, i3=i3, n=n: e.activation(eb[i3][:, 0:n], p[:, 0:n], AF.Copy, scale=128.0 ** -0.5),
                                 reads=[pk], writes=[f"eb{i3}"])
                            S.dma("pool", QT[col - 2048:col - 2048 + 128, tcols], eb[i3][:, 0:n], reads=[f"eb{i3}"], writes=["QT"])
                        elif col < 3072:
                            S.op("dve", lambda e, p=p, i3=i3, n=n: e.tensor_copy(eb[i3][:, 0:n], p[:, 0:n]),
                                 reads=[pk], writes=[f"eb{i3}"])
                            S.dma("pool", KT[col - 2816:col - 2816 + 128, tcols], eb[i3][:, 0:n], reads=[f"eb{i3}"], writes=["KT"])
                        else:
                            r0 = col - 4096
                            S.op("dve", lambda e, p=p, i3=i3, n=n: e.tensor_copy(ef[i3][:, 0:n], p[:, 0:n]),
                                 reads=[pk], writes=[f"ef{i3}"])
                            S.dma("pool", DQKV[r0:r0 + 128, tcols], ef[i3][:, 0:n], reads=[f"ef{i3}"], writes=["DQKV"])
                    if u == 6:
                        for bi in range(gb):
                            b = b0 + bi

                            def mmv(e, w=w, bi=bi, xnT=xnT):
                                r = None
                                for kc in range(KC):
                                    r = e.matmul(pv, xnT[:, kc, bi * 128:(bi + 1) * 128], w[:, kc, 0:256],
                                                 start=(kc == 0), stop=(kc == KC - 1))
                                return r
                            S.op("pe", mmv, reads=[wk, f"xnT{gp}_{bi}"], writes=["pv"])
                            S.op("act", lambda e: e.copy(vtok[:], pv), reads=["pv"], writes=["vtok"])
                            S.dma("pool", VV[b * 128:(b + 1) * 128, :], vtok[:], reads=["vtok"], writes=["VV"])
                    if u >= 2:
                        drain(nxt, 1)
                for bi in range(gb):
                    b = b0 + bi

                    def mmb(e, bi=bi, xnT=xnT):
                        r = None
                        for kc in range(KC):
                            r = e.matmul(pbg, xnT[:, kc, bi * 128:(bi + 1) * 128], wlast[:, kc, :],
                                         start=(kc == 0), stop=(kc == KC - 1))
                        return r
                    S.op("pe", mmb, reads=["wlast", f"xnT{gp}_{bi}"], writes=["pbg"])
                    S.op("dve", lambda e: e.tensor_copy(bgt[:], pbg), reads=["pbg"], writes=["bgt"])
                    S.dma("pool", BG[b * 128:(b + 1) * 128, :], bgt[:], reads=["bgt"], writes=["BG"])
                drain(nxt)

        if stop == 'A':
            S.close()
            return nc
        with S.phase():
            def T2(name, shape, dt):
                return [S.sb(f"{name}{i}", shape, dt) for i in range(2)]
            raw = T2("raw", [128, 18, 130], F32)
            cv = T2("cv", [128, 18, 128], F32)
            sq = T2("sq", [128, 12, 128], BF16)
            rs = T2("rs", [128, 12, 128], F32)
            vB = T2("vB", [128, 128], F32)
            kq = T2("kq", [128, 6, 2, 128], BF16)
            vT = T2("vT", [128, 6, 128], BF16)
            kv_tok = T2("kv_tok", [128, 12, 128], BF16)
            bgt2 = T2("bgt2", [128, 24], F32)
            spt = T2("spt", [128, 12], F32)
            pss = S.ps("pss", [128, 12, 128], F32, nbanks=3)
            ptk_ = S.ps("ptk", [128, 16, 128], BF16, nbanks=2)

            def a2_block(b):
                par = b % 2
                P = str(par)
                raw_, cv_, sq_, rs_, vB_, kq_, vT_, kvt_, bg_, sp_ = (raw[par], cv[par], sq[par], rs[par], vB[par], kq[par],
                                                                     vT[par], kv_tok[par], bgt2[par], spt[par])
                t0 = b * 128
                lo = 1 if b == 0 else 0
                hi = 129 if b == NB - 1 else 130
                if lo == 1:
                    S.op("pool", lambda e: e.memset(raw_[:, :, 0:1], 0.0), writes=["raw" + P])
                if hi == 129:
                    S.op("pool", lambda e: e.memset(raw_[:, :, 129:130], 0.0), writes=["raw" + P])
                S.dma("sp", raw_[:, :, lo:hi],
                      DQKV[:, t0 - 1 + lo:t0 - 1 + hi].rearrange("(c p) t -> p c t", p=128),
                      reads=["DQKV"], writes=["raw" + P])
                S.dma("sp", vB_[:], validB_d[:, t0:t0 + 128], writes=["vB" + P])
                S.dma("sp", bg_[:], BG[t0:t0 + 128, :], reads=["BG"], writes=["bgt2" + P])
                S.op("act", lambda e: e.activation(betaT[:, b, :], bg_[:, 0:12], AF.Sigmoid),
                     reads=["bgt2" + P], writes=["betaT"])
                S.op("dve", lambda e: e.tensor_scalar(nbetaT[:, b, :], betaT[:, b, :], -1.0, None, ALU.mult),
                     reads=["betaT"], writes=["nbetaT"])
                S.op("dve", lambda e: e.tensor_tensor(sp_[:], bg_[:, 12:24], dtbB[:, l, :], ALU.add),
                     reads=["bgt2" + P, "dtbB"], writes=["spt" + P])
                S.op("act", lambda e: e.activation(sp_[:], sp_[:], AF.Exp), reads=["spt" + P], writes=["spt" + P])
                S.op("act", lambda e: e.activation(sp_[:], sp_[:], AF.Ln, bias=onec[:, 0:1]), reads=["spt" + P, "onec"], writes=["spt" + P])
                S.op("dve", lambda e: e.tensor_tensor(gT[:, b, :], sp_[:], negA[:, l, :], ALU.mult),
                     reads=["spt" + P, "negA"], writes=["gT"])
                for c in range(18):
                    ck = f"cv{P}_{c}"
                    S.op("act", lambda e, c=c: e.activation(cv_[:, c, :], raw_[:, c, 1:129], AF.Copy, scale=dcw[:, l, c, 1:2]),
                         reads=["raw" + P, "dcw"], writes=[ck])
                    S.op("dve", lambda e, c=c: e.scalar_tensor_tensor(cv_[:, c, :], raw_[:, c, 0:128], dcw[:, l, c, 0:1], cv_[:, c, :], ALU.mult, ALU.add),
                         reads=["raw" + P, ck], writes=[ck])
                    S.op("dve", lambda e, c=c: e.scalar_tensor_tensor(cv_[:, c, :], raw_[:, c, 2:130], dcw[:, l, c, 2:3], cv_[:, c, :], ALU.mult, ALU.add),
                         reads=["raw" + P, ck], writes=[ck])
                cvk = [f"cv{P}_{c}" for c in range(18)]
                S.op("act", lambda e: e.activation(cv_[:], cv_[:], AF.Silu), reads=cvk, writes=cvk)
                S.op("act", lambda e: e.activation(sq_[:], cv_[:, 0:12, :], AF.Square), reads=cvk, writes=["sq" + P])

                def mss(e):
                    r = None
                    for j in range(3):
                        r = e.matmul(pss[:, j * 4:(j + 1) * 4, :], onesb[:], sq_[:, j * 4:(j + 1) * 4, :], start=True, stop=True)
                    return r
                S.op("pe", mss, reads=["sq" + P, "onesb"], writes=["pss"])
                S.op("act", lambda e: e.activation(rs_[:], pss[:], AF.Sqrt, bias=epsc[:, 0:1], scale=1.0),
                     reads=["pss", "epsc"], writes=["rs" + P])
                S.op("dve", lambda e: e.reciprocal(rs_[:], rs_[:]), reads=["rs" + P], writes=["rs" + P])
                S.op("dve", lambda e: e.tensor_tensor(rs_[:, 6:12, :], rs_[:, 6:12, :], vB_[:].unsqueeze(1).broadcast_to([128, 6, 128]), ALU.mult),
                     reads=["rs" + P, "vB" + P], writes=["rs" + P])
                S.op("dve", lambda e: e.scalar_tensor_tensor(kq_[:, :, 1, :], cv_[:, 0:6, :], 128.0 ** -0.5, rs_[:, 0:6, :], ALU.mult, ALU.mult),
                     reads=cvk + ["rs" + P], writes=["kq" + P])
                S.op("dve", lambda e: e.tensor_tensor(kq_[:, :, 0, :], cv_[:, 6:12, :], rs_[:, 6:12, :], ALU.mult),
                     reads=cvk + ["rs" + P], writes=["kq" + P])
                S.op("dve", lambda e: e.tensor_copy(vT_[:], cv_[:, 12:18, :]), reads=cvk, writes=["vT" + P])

                def trk(e):
                    r = None
                    for h in range(6):
                        r = e.transpose(ptk_[:, h, :], kq_[:, h, 0, :], identb[:])
                    for h in range(6):
                        r = e.transpose(ptk_[:, 6 + h, :], vT_[:, h, :], identb[:])
                    return r
                S.op("pe", trk, reads=["kq" + P, "vT" + P, "identb"], writes=["ptk"])
                S.op("act", lambda e: e.copy(kvt_[:], ptk_[:, 0:12, :]), reads=["ptk"], writes=["kv_tok" + P])
                S.dma("sp", KQT[b], kq_[:].rearrange("p h two t -> p (h two t)"), reads=["kq" + P], writes=["KQT"])
                S.dma("sp", KN[b], kvt_[:, 0:6, :].rearrange("p h d -> p (h d)"), reads=["kv_tok" + P], writes=["KN"])
                S.dma("sp", VN[b], kvt_[:, 6:12, :].rearrange("p h d -> p (h d)"), reads=["kv_tok" + P], writes=["VN"])

            if l + 1 < L:
                convert_weights(l + 1)
            for b in range(NB):
                a2_block(b)

        if stop == 'A2':
            S.close()
            return nc
        with S.phase():
            def T2(name, shape, dt):
                return [S.sb(f"{name}{i}", shape, dt) for i in range(2)]
            kqt = T2("kqt", [128, 6, 2, 128], BF16)
            knt = T2("knt", [128, 6, 128], BF16)
            vnt = T2("vnt", [128, 6, 128], BF16)
            gcs = T2("gcs", [128, 12], F32)
            ngc = T2("ngc", [128, 6], F32)
            egs = T2("egs", [128, 18], F32)
            gU = T2("gU", [128, 6, 128], F32)
            GCs = T2("GCs", [128, 6, 128], F32)
            Et = T2("Et", [128, 6, 128], F32)
            DT = T2("DT", [128, 6, 128], F32)
            EG = T2("EG", [128, 6, 128], F32)
            tt = T2("tt", [128, 6, 128], F32)
            qg = T2("qg", [128, 6, 128], BF16)
            MT = T2("MT", [128, 6, 128], BF16)
            X = [[S.sb(f"X{p}{i}", [128, 6, 3, 128], F32) for i in range(2)] for p in range(2)]
            Pb = T2("Pb", [128, 6, 128], BF16)
            kg = T2("kg", [128, 6, 128], BF16)
            kt = T2("kt", [128, 6, 128], BF16)
            nWT = S.sb("nWT", [128, 6, 128], BF16)
            vnew = S.sb("vnew", [128, 6, 128], BF16)
            Sf = S.sb("Sf", [128, 6, 128], F32)
            Sb = S.sb("Sb", [128, 6, 128], BF16)
            of = T2("of", [128, 6, 128], F32)
            ofl = T2("ofl", [128, 6, 128], F32)
            osq = S.sb("osq", [128, 6, 128], F32)
            zsd = T2("zsd", [128, 6, 128], BF16)
            ydn = S.sb("ydn", [128, 6, 128], BF16)
            rn = S.sb("rn", [128, 12], F32)
            Hps = [S.ps(f"H{h}", [128, 4, 128], F32) for h in range(6)]
            QA = S.ps("QA", [128, 4, 128], F32)
            QB = S.ps("QB", [128, 4, 128], F32)
            onf = S.sb("onf", [128, 6, 128], F32)

            def part0(dirn, b, par):
                S.dma("sp", kqt[par][:].rearrange("p h two t -> p (h two t)"), KQT[b], reads=["KQT"], writes=[f"kqt{par}"])
                S.dma("sp", knt[par][:].rearrange("p h d -> p (h d)"), KN[b], reads=["KN"], writes=[f"knt{par}"])
                S.dma("sp", vnt[par][:].rearrange("p h d -> p (h d)"), VN[b], reads=["VN"], writes=[f"vnt{par}"])
                if dirn == 1:
                    S.dma("sp", ofl[par][:].rearrange("p h d -> p (h d)"), OF[b], reads=["OF"], writes=[f"ofl{par}"])
                    S.dma("sp", zsd[par][:], ZS[1280:2048, b * 128:(b + 1) * 128].rearrange("(h p) t -> p h t", p=128),
                          reads=["ZS"], writes=[f"zsd{par}"])
                gsl = gT[:, b, dirn * 6:(dirn + 1) * 6]

                def mgc(e):
                    e.matmul(Hps[0][:, 3, 0:6], Uf[:, dirn, :], gsl, start=True, stop=True)
                    return e.matmul(Hps[0][:, 3, 6:12], onesf[:], gsl, start=True, stop=True)
                S.op("pe", mgc, reads=["Uf", "onesf", "gT"], writes=["H0"])
                S.op("dve", lambda e: e.tensor_copy(gcs[par][:], Hps[0][:, 3, 0:12]), reads=["H0"], writes=[f"gcs{par}"])
                S.op("dve", lambda e: e.tensor_scalar(ngc[par][:], gcs[par][:, 0:6], -1.0, None, ALU.mult), reads=[f"gcs{par}"], writes=[f"ngc{par}"])
                S.op("dve", lambda e: e.tensor_tensor(egs[par][:, 6:12], gcs[par][:, 6:12], gcs[par][:, 0:6], ALU.subtract), reads=[f"gcs{par}"], writes=[f"egs1{par}"])
                S.op("act", lambda e: e.activation(egs[par][:, 0:6], gcs[par][:, 0:6], AF.Exp), reads=[f"gcs{par}"], writes=[f"egs0{par}"])
                S.op("act", lambda e: e.activation(egs[par][:, 6:12], egs[par][:, 6:12], AF.Exp), reads=[f"egs1{par}"], writes=[f"egs1{par}"])
                S.op("act", lambda e: e.activation(egs[par][:, 12:18], gcs[par][:, 6:12], AF.Exp), reads=[f"gcs{par}"], writes=[f"egs2{par}"])

            def part1(dirn, b, par, h):
                sfx = f"{par}_{h}"
                Hh = Hps[h]
                hk = f"H{h}"
                kq_, kn_ = kqt[par], knt[par]
                gsl = gT[:, b, dirn * 6:(dirn + 1) * 6]
                nb_ = nbetaT[:, b, dirn * 6:(dirn + 1) * 6]
                Ud = Uf[:, dirn, :]
                Xa, Xb = X[par]
                S.op("act", lambda e: e.activation(gU[par][:, h, :], Ud, AF.Copy, scale=gsl[:, h:h + 1]),
                     reads=["Uf", "gT"], writes=[f"gU{sfx}"])
                S.op("act", lambda e: e.copy(Xa[:, h, 2, :], identf[:]), reads=["identf"], writes=[f"XaP{sfx}"])
                yield

                def m1(e):
                    e.matmul(Hh[:, 3, :], onesf[:], gU[par][:, h, :], start=True, stop=True)
                    return e.matmul(Hh[:, 1:3, :], kq_[:, h, 0, :], kq_[:, h, :, :], start=True, stop=True)
                S.op("pe", m1, reads=[f"gU{sfx}", "onesf", f"kqt{par}"], writes=[hk])
                yield
                S.op("dve", lambda e: e.tensor_tensor(Et[par][:, h, :], Hh[:, 3, :], maskB[:, dirn, :], ALU.add),
                     reads=[hk, "maskB"], writes=[f"Et{sfx}"])
                S.op("act", lambda e: e.activation(EG[par][:, h, :], Hh[:, 3, :], AF.Exp), reads=[hk], writes=[f"EG{sfx}"])
                yield
                S.op("act", lambda e: e.activation(DT[par][:, h, :], Et[par][:, h, :], AF.Exp, bias=ngc[par][:, h:h + 1]),
                     reads=[f"Et{sfx}", f"ngc{par}"], writes=[f"DT{sfx}"])
                S.op("dve", lambda e: e.tensor_tensor(qg[par][:, h, :], kq_[:, h, 1, :], EG[par][:, h, :], ALU.mult),
                     reads=[f"kqt{par}", f"EG{sfx}"], writes=[f"qg{sfx}"])
                S.op("act", lambda e: e.activation(kg[par][:, h, :], kn_[:, h, :], AF.Copy, scale=egs[par][:, h:h + 1]),
                     reads=[f"knt{par}", f"egs0{par}"], writes=[f"kg{sfx}"])
                S.op("act", lambda e: e.activation(kt[par][:, h, :], kn_[:, h, :], AF.Copy, scale=egs[par][:, 6 + h:7 + h]),
                     reads=[f"knt{par}", f"egs1{par}"], writes=[f"kt{sfx}"])
                yield
                S.op("pool", lambda e: e.tensor_tensor(tt[par][:, h, :], DT[par][:, h, :], strictM[:, dirn, :], ALU.mult),
                     reads=[f"DT{sfx}", "strictM"], writes=[f"tt{sfx}"])
                S.op("dve", lambda e: e.tensor_tensor(MT[par][:, h, :], Hh[:, 2, :], DT[par][:, h, :], ALU.mult),
                     reads=[hk, f"DT{sfx}"], writes=[f"MT{sfx}"])
                yield
                S.op("dve", lambda e: e.scalar_tensor_tensor(Xa[:, h, 1, :], Hh[:, 1, :], nb_[:, h:h + 1], tt[par][:, h, :], ALU.mult, ALU.mult),
                     reads=[hk, f"tt{sfx}", "nbetaT"], writes=[f"XaQ{sfx}"])
                yield
                S.op("pe", lambda e: e.transpose(Hh[:, 0, :], Xa[:, h, 1, :], identf[:]), reads=[f"XaQ{sfx}", "identf"], writes=[hk])
                yield
                S.op("act", lambda e: e.copy(Xa[:, h, 0, :], Hh[:, 0, :]), reads=[hk], writes=[f"XaT{sfx}"])
                yield
                cur = 0
                for lev in range(7):
                    Xc, Xn_ = (Xa, Xb) if cur == 0 else (Xb, Xa)
                    cc_ = "Xa" if cur == 0 else "Xb"
                    nn_ = "Xb" if cur == 0 else "Xa"

                    def mAB(e, Xc=Xc, lev=lev):
                        if lev < 6:
                            e.matmul(Hh[:, 1:3, :], Xc[:, h, 0, :], Xc[:, h, 1:3, :], start=True, stop=True)
                            return e.matmul(Hh[:, 0, :], Xc[:, h, 1, :], Xc[:, h, 0, :], start=True, stop=True)
                        return e.matmul(Hh[:, 2, :], Xc[:, h, 0, :], Xc[:, h, 2, :], start=True, stop=True)
                    S.op("pe", mAB, reads=[cc_ + "Q" + sfx, cc_ + "T" + sfx, cc_ + "P" + sfx], writes=[hk])
                    yield
                    if lev < 6:
                        S.op("act", lambda e, Xn_=Xn_: e.copy(Xn_[:, h, 0:2, :], Hh[:, 0:2, :]), reads=[hk],
                             writes=[nn_ + "Q" + sfx, nn_ + "T" + sfx])
                        S.op("dve", lambda e, Xn_=Xn_, Xc=Xc: e.tensor_tensor(Xn_[:, h, 2, :], Hh[:, 2, :], Xc[:, h, 2, :], ALU.add),
                             reads=[hk, cc_ + "P" + sfx], writes=[nn_ + "P" + sfx])
                    else:
                        S.op("dve", lambda e, Xc=Xc: e.tensor_tensor(Pb[par][:, h, :], Hh[:, 2, :], Xc[:, h, 2, :], ALU.add),
                             reads=[hk, cc_ + "P" + sfx], writes=[f"Pb{sfx}"])
                    yield
                    cur = 1 - cur

            def part2(dirn, b, par, h):
                sfx = f"{par}_{h}"
                Qp, qk = (QA, "QA") if h % 2 == 0 else (QB, "QB")
                bt_ = betaT[:, b, dirn * 6:(dirn + 1) * 6]
                S.op("pe", lambda e: e.matmul(Qp[:, 0, :], kg[par][:, h, :], Pb[par][:, h, :], start=True, stop=True),
                     reads=[f"kg{sfx}", f"Pb{sfx}", qk], writes=[qk + "W"])
                yield
                S.op("dve", lambda e: e.tensor_scalar(nWT[:, h, :], Qp[:, 0, :], -1.0, None, ALU.mult), reads=[qk + "W", qk], writes=[f"nWT{h}"])
                yield

                def mV(e):
                    e.matmul(Qp[:, 1, :], Pb[par][:, h, :], vnt[par][:, h, :], start=True, stop=False)
                    return e.matmul(Qp[:, 1, :], nWT[:, h, :], Sb[:, h, :], start=False, stop=True)
                S.op("pe", mV, reads=[f"Pb{sfx}", f"vnt{par}", f"nWT{h}", f"Sb{h}", qk], writes=[qk + "V"])
                yield
                S.op("dve", lambda e: e.tensor_scalar(vnew[:, h, :], Qp[:, 1, :], bt_[:, h:h + 1], None, ALU.mult),
                     reads=[qk + "V", qk, "betaT"], writes=[f"vnew{h}"])
                yield

                def mOS(e):
                    e.matmul(Qp[:, 2, :], qg[par][:, h, :], Sb[:, h, :], start=True, stop=False)
                    e.matmul(Qp[:, 2, :], MT[par][:, h, :], vnew[:, h, :], start=False, stop=True)
                    return e.matmul(Qp[:, 3, :], kt[par][:, h, :], vnew[:, h, :], start=True, stop=True)
                S.op("pe", mOS, reads=[f"qg{sfx}", f"Sb{h}", f"MT{sfx}", f"vnew{h}", f"kt{sfx}", qk], writes=[qk + "O", qk + "S"])
                yield
                S.op("dve", lambda e: e.scalar_tensor_tensor(Sf[:, h, :], Sf[:, h, :], egs[par][:, 12 + h:13 + h], Qp[:, 3, :], ALU.mult, ALU.add),
                     reads=[f"Sf{h}", f"egs2{par}", qk + "S", qk], writes=[f"Sf{h}"])
                if dirn == 0:
                    S.op("dve", lambda e: e.tensor_copy(of[par][:, h, :], Qp[:, 2, :]), reads=[qk + "O", qk], writes=[f"of{sfx}"])
                else:
                    S.op("dve", lambda e: e.tensor_tensor(of[par][:, h, :], Qp[:, 2, :], ofl[par][:, h, :], ALU.add),
                         reads=[qk + "O", qk, f"ofl{par}"], writes=[f"of{sfx}"])
                yield
                S.op("act", lambda e: e.copy(Sb[:, h, :], Sf[:, h, :]), reads=[f"Sf{h}"], writes=[f"Sb{h}"])
                yield

            def chain(*gs):
                for g_ in gs:
                    yield from g_

            def part3(dirn, b, par):
                ofk = [f"of{par}_{h}" for h in range(6)]
                if dirn == 0:
                    S.dma("sp", OF[b], of[par][:].rearrange("p h d -> p (h d)"), reads=ofk, writes=["OF"])
                    return
                o_ = of[par]
                S.op("pool", lambda e: e.tensor_tensor(osq[:], o_[:], o_[:], ALU.mult), reads=ofk, writes=["osq"])
                S.op("dve", lambda e: e.tensor_reduce(rn[:, 0:6], osq[:], mybir.AxisListType.X, ALU.add), reads=["osq"], writes=["rn0"])
                S.op("act", lambda e: e.activation(rn[:, 6:12], rn[:, 0:6], AF.Sqrt, bias=epsc[:, 0:1], scale=1.0 / 128), reads=["rn0", "epsc"], writes=["rn1"])
                S.op("dve", lambda e: e.reciprocal(rn[:, 6:12], rn[:, 6:12]), reads=["rn1"], writes=["rn1"])
                for h in range(6):
                    S.op("pool", lambda e, h=h: e.tensor_scalar(osq[:, h, :], o_[:, h, :], rn[:, 6 + h:7 + h], None, ALU.mult),
                         reads=ofk + ["rn1"], writes=["osq"])
                S.op("pool", lambda e: e.tensor_tensor(onf[:], osq[:], dnwB[:, l, :].unsqueeze(1).broadcast_to([128, 6, 128]), ALU.mult),
                     reads=["osq", "dnwB"], writes=["onf"])

                def trO(e):
                    r = None
                    for h in range(3):
                        r = e.transpose(QA[:, h, :], onf[:, h, :], identf[:])
                    for h in range(3):
                        r = e.transpose(QB[:, h, :], onf[:, 3 + h, :], identf[:])
                    return r
                S.op("pe", trO, reads=["onf", "identf", "QA", "QB"], writes=["QAW", "QAV", "QAO", "QBW", "QBV", "QBO"])
                S.op("dve", lambda e: e.tensor_tensor(ydn[:, 0:3, :], QA[:, 0:3, :], zsd[par][:, 0:3, :], ALU.mult),
                     reads=["QAW", "QAV", "QAO", "QA", f"zsd{par}"], writes=["ydn"])
                S.op("dve", lambda e: e.tensor_tensor(ydn[:, 3:6, :], QB[:, 0:3, :], zsd[par][:, 3:6, :], ALU.mult),
                     reads=["QBW", "QBV", "QBO", "QB", f"zsd{par}"], writes=["ydn"])
                S.dma("sp", YDN[b], ydn[:].rearrange("p h t -> p (h t)"), reads=["ydn"], writes=["YDN"])

            rrcnt = [0]

            def rr(gens):
                gens = list(gens)
                while gens:
                    alive = []
                    for gn in gens:
                        if BCUT and rrcnt[0] >= BCUT:
                            return
                        rrcnt[0] += 1
                        try:
                            next(gn)
                            alive.append(gn)
                        except StopIteration:
                            pass
                    gens = alive

            for dirn in range(2):
                S.op("pool", lambda e: e.memset(Sf[:], 0.0), writes=[f"Sf{h}" for h in range(6)])
                S.op("pool", lambda e: e.memset(Sb[:], 0.0), writes=[f"Sb{h}" for h in range(6)])
                blocks = list(range(NB)) if dirn == 0 else list(range(NB - 1, -1, -1))
                for i, b in enumerate(blocks):
                    par = i % 2
                    part0(dirn, b, par)
                    gl = []
                    if i > 0:
                        pb, pp = blocks[i - 1], 1 - par
                        gl.append(chain(*[part2(dirn, pb, pp, h) for h in (0, 2, 4)]))
                        gl.append(chain(*[part2(dirn, pb, pp, h) for h in (1, 3, 5)]))
                    for h in range(6):
                        gl.append(part1(dirn, b, par, h))
                    rr(gl)
                    if i > 0:
                        part3(dirn, blocks[i - 1], 1 - par)
                lp = (len(blocks) - 1) % 2
                rr([chain(*[part2(dirn, blocks[-1], lp, h) for h in (0, 2, 4)]),
                    chain(*[part2(dirn, blocks[-1], lp, h) for h in (1, 3, 5)])])
                part3(dirn, blocks[-1], lp)

        if stop == 'B':
            S.close()
            return nc
        with S.phase(last=(l == L - 1)):
            def T2(name, shape, dt):
                return [S.sb(f"{name}{i}", shape, dt) for i in range(2)]
            wout = S.sb("wout", [128, KC, D], BF16)
            hc2 = T2("hc", [128, D], F32)
            hn2 = T2("hn", [128, D], F32)
            yo = S.sb("yo", [128, D], F32)
            junk2 = S.sb("junk2", [128, D], BF16)
            ss2 = S.sb("ss2", [128, 2], F32)
            yT2 = T2("yT", [128, 16, 128], BF16)
            uc2 = T2("uc", [128, 4, 130], F32)
            cbt2 = T2("cbt", [128, 4, 128], F32)
            cvc2 = T2("cvc", [128, 4, 128], F32)
            zst2 = T2("zst", [128, 10, 128], BF16)
            qt2 = T2("qt", [128, 6, 128], BF16)
            ktl2 = T2("ktl", [128, 2, 384], BF16)
            vl2 = T2("vl", [128, 3, 256], BF16)
            Ea2 = T2("Ea", [128, 384], F32)
            PTa2 = T2("PTa", [128, 384], BF16)
            den = S.sb("den", [128, 384], F32)
            oa = S.sb("oa", [128, 384], F32)
            pst2 = [S.ps(f"pst{i}", [128, 512], F32)[:, 0:384] for i in range(2)]
            pot = [S.ps(f"pot{g}", [128, 512], F32)[:, 0:384] for g in range(2)]
            pden = [S.ps(f"pden{g}", [128, 512], F32)[:, 0:384] for g in range(2)]
            po = [S.ps(f"po{i}", [128, 512], F32) for i in range(2)]

            S.dma("sp", wout[:], Wb_out[l].rearrange("(kc p) c -> p kc c", p=128), reads=[f"Wb_out{l}"], writes=["wout"])
            scnt = [0]

            def c_block(b):
                par = b % 2
                P = str(par)
                hc, hn, yT, uc, cbt, cvc, zst, qt, ktl, vl = (hc2[par], hn2[par], yT2[par], uc2[par], cbt2[par], cvc2[par],
                                                              zst2[par], qt2[par], ktl2[par], vl2[par])
                t0 = b * 128
                lo = 1 if b == 0 else 0
                hi = 129 if b == NB - 1 else 130
                if lo == 1:
                    S.op("pool", lambda e: e.memset(uc[:, :, 0:1], 0.0), writes=["uc" + P])
                if hi == 129:
                    S.op("pool", lambda e: e.memset(uc[:, :, 129:130], 0.0), writes=["uc" + P])
                S.dma("sp", uc[:, :, lo:hi], UC[:, t0 - 1 + lo:t0 - 1 + hi].rearrange("(c p) t -> p c t", p=128),
                      reads=["UC"], writes=["uc" + P])
                S.dma("sp", cbt[:], CB[:, t0:t0 + 128].rearrange("(c p) t -> p c t", p=128), reads=["CB"], writes=["cbt" + P])
                S.dma("sp", zst[:], ZS[0:1280, t0:t0 + 128].rearrange("(c p) t -> p c t", p=128), reads=["ZS"], writes=["zst" + P])
                S.dma("sp", hc[:], Hin[t0:t0 + 128, :], reads=[hin_key], writes=["hc" + P])
                kbs = [kb for kb in (b - 1, b, b + 1) if 0 <= kb < NB]
                k0, k1 = kbs[0], kbs[-1]
                S.dma("sp", qt[:], QT[:, t0:t0 + 128].rearrange("(h p) t -> p h t", p=128), reads=["QT"], writes=["qt" + P])
                S.dma("sp", ktl[:, :, 0:(k1 - k0 + 1) * 128], KT[:, k0 * 128:(k1 + 1) * 128].rearrange("(g p) t -> p g t", p=128),
                      reads=["KT"], writes=["ktl" + P])
                S.dma("sp", vl[:, 0:(k1 - k0 + 1), :], VV[k0 * 128:(k1 + 1) * 128, :].rearrange("(j p) c -> p j c", p=128),
                      reads=["VV"], writes=["vl" + P])
                S.dma("sp", yT[:, 10:16, :].rearrange("p h t -> p (h t)"), YDN[b], reads=["YDN"], writes=["yT_d" + P])
                for c in range(4):
                    ck = f"cvc{P}_{c}"
                    S.op("act", lambda e, c=c: e.activation(cvc[:, c, :], uc[:, c, 1:129], AF.Copy, scale=caw[:, l, c, 1:2]),
                         reads=["uc" + P, "caw"], writes=[ck])
                    S.op("dve", lambda e, c=c: e.scalar_tensor_tensor(cvc[:, c, :], uc[:, c, 0:128], caw[:, l, c, 0:1], cvc[:, c, :], ALU.mult, ALU.add),
                         reads=["uc" + P, ck], writes=[ck])
                    S.op("dve", lambda e, c=c: e.scalar_tensor_tensor(cvc[:, c, :], uc[:, c, 2:130], caw[:, l, c, 2:3], cvc[:, c, :], ALU.mult, ALU.add),
                         reads=["uc" + P, ck], writes=[ck])
                cvck = [f"cvc{P}_{c}" for c in range(4)]
                S.op("pool", lambda e: e.tensor_tensor(cvc[:], cvc[:], cbt[:], ALU.mult), reads=cvck + ["cbt" + P], writes=cvck)
                S.op("pool", lambda e: e.tensor_tensor(yT[:, 0:4, :], cvc[:], zst[:, 0:4, :], ALU.mult), reads=cvck + ["zst" + P], writes=["yT_c" + P])
                for g in range(2):
                    for ji, kb in enumerate(kbs):
                        off = kb - b + 1
                        j = kb - k0
                        first = (ji == 0)
                        lastk = (ji == len(kbs) - 1)
                        si = scnt[0] % 2
                        scnt[0] += 1
                        pst, Ea, PTa = pst2[si], Ea2[si], PTa2[si]
                        S.op("pe", lambda e, g=g, j=j, pst=pst: e.matmul(pst, ktl[:, g, j * 128:(j + 1) * 128], qt[:, 3 * g:3 * g + 3, :], start=True, stop=True),
                             reads=["ktl" + P, "qt" + P], writes=[f"pst{si}"])
                        S.op("dve", lambda e, g=g, off=off, pst=pst, Ea=Ea: e.tensor_tensor(Ea[:], pst, AL[:, off, g * 384:(g + 1) * 384], ALU.add),
                             reads=[f"pst{si}", "AL"], writes=[f"Ea{si}"])
                        S.op("act", lambda e, kb=kb, Ea=Ea, PTa=PTa: e.activation(PTa[:], Ea[:], AF.Exp, bias=kbias[:, kb:kb + 1]),
                             reads=[f"Ea{si}", "kbias"], writes=[f"PTa{si}"])

                        def mpv(e, g=g, j=j, first=first, lastk=lastk, PTa=PTa):
                            e.matmul(pot[g], vl[:, j, g * 128:(g + 1) * 128], PTa[:], start=first, stop=lastk)
                            return e.matmul(pden[g], onesb[:], PTa[:], start=first, stop=lastk)
                        S.op("pe", mpv, reads=["vl" + P, f"PTa{si}", "onesb"], writes=[f"pot{g}", f"pden{g}"])
                    for hh in range(3):
                        h = 3 * g + hh
                        S.op("dve", lambda e, g=g, hh=hh, h=h: e.tensor_scalar(den[:, hh * 128:(hh + 1) * 128], pden[g][:, hh * 128:(hh + 1) * 128], esink[:, l, h:h + 1], None, ALU.add),
                             reads=[f"pden{g}", "esink"], writes=["den"])
                    S.op("dve", lambda e: e.reciprocal(den[:], den[:]), reads=["den"], writes=["den"])
                    S.op("dve", lambda e, g=g: e.tensor_tensor(oa[:], pot[g], den[:], ALU.mult), reads=[f"pot{g}", "den"], writes=["oa"])
                    S.op("pool", lambda e, g=g: e.tensor_tensor(yT[:, 4 + 3 * g:7 + 3 * g, :], oa[:].rearrange("p (h t) -> p h t", h=3), zst[:, 4 + 3 * g:7 + 3 * g, :], ALU.mult),
                         reads=["oa", "zst" + P], writes=[f"yT_a{g}" + P])
                ykeys = ["yT_c" + P, "yT_a0" + P, "yT_a1" + P, "yT_d" + P]
                for n4 in range(4):
                    p = po[n4 % 2]
                    pk = f"po{n4 % 2}"

                    def mo(e, p=p, n4=n4):
                        r = None
                        for mc in range(16):
                            r = e.matmul(p[:], yT[:, mc, :], wout[:, mc, n4 * 512:(n4 + 1) * 512], start=(mc == 0), stop=(mc == 15))
                        return r
                    S.op("pe", mo, reads=ykeys + ["wout"], writes=[pk])
                    S.op("dve", lambda e, p=p, n4=n4: e.scalar_tensor_tensor(hn[:, n4 * 512:(n4 + 1) * 512], p[:], validT[:, b:b + 1], hc[:, n4 * 512:(n4 + 1) * 512], ALU.mult, ALU.add),
                         reads=[pk, "validT", "hc" + P], writes=[f"hn{P}_{n4}"])
                hnk = [f"hn{P}_{n4}" for n4 in range(4)]
                if l < L - 1:
                    S.dma("pool", H[t0:t0 + 128, :], hn[:], reads=hnk, writes=["H"])
                elif 1 <= b <= NBOUT:
                    S.op("act", lambda e: e.activation(junk2[:], hn[:], AF.Square, accum_out=ss2[:, 0:1]), reads=hnk, writes=["junk2", "ss2a"])
                    S.op("act", lambda e: e.activation(ss2[:, 1:2], ss2[:, 0:1], AF.Sqrt, bias=epsc[:, 0:1], scale=1.0 / D), reads=["ss2a", "epsc"], writes=["ss2b"])
                    S.op("dve", lambda e: e.reciprocal(ss2[:, 1:2], ss2[:, 1:2]), reads=["ss2b"], writes=["ss2b"])
                    S.op("dve", lambda e: e.scalar_tensor_tensor(yo[:], hn[:], ss2[:, 1:2], fnwB[:], ALU.mult, ALU.mult),
                         reads=hnk + ["ss2b", "fnwB"], writes=["yo"])
                    S.dma("pool", out_d[(b - 1) * 128:b * 128, :], yo[:], reads=["yo"], writes=["out"], final=True)

            for b in range(NB):
                c_block(b)
    S.close()
    return nc


def make_consts():
    import ml_dtypes
    i = np.arange(128)
    c = {}
    c["c_identb"] = np.eye(128, dtype=np.float32)
    U = np.zeros((128, 2, 128), np.float32)
    U[:, 0, :] = (i[:, None] <= i[None, :])
    U[:, 1, :] = (i[:, None] >= i[None, :])
    c["c_U"] = U
    mB = np.zeros((128, 2, 128), np.float32)
    mB[:, 0, :] = np.where(i[None, :] < i[:, None], NEG, 0.0)
    mB[:, 1, :] = np.where(i[None, :] > i[:, None], NEG, 0.0)
    c["c_maskB"] = mB
    st = np.zeros((128, 2, 128), np.float32)
    st[:, 0, :] = (i[None, :] > i[:, None])
    st[:, 1, :] = (i[None, :] < i[:, None])
    c["c_strict"] = st
    slopes = np.exp2(-8.0 * np.arange(1, 7, dtype=np.float32) / 6).astype(np.float32)
    AL = np.zeros((128, 3, 6, 128), np.float32)
    s = i[:, None].astype(np.float32)
    q = i[None, :].astype(np.float32)
    for off in range(3):
        dist = np.abs(q - (s + (off - 1) * 128))
        for h in range(6):
            AL[:, off, h, :] = np.where(dist <= 128, -slopes[h] * dist, NEG)
    c["c_AL"] = AL.reshape(128, 3, 768)
    return c


def prep_weights(norm_w, w_in, conv_a_w, attn_sink, dn_conv_w, dn_a_log, dn_dt_bias, dn_norm_w, w_out, final_norm_w):
    L = w_in.shape[0]
    rep = lambda a: np.ascontiguousarray(np.broadcast_to(a[None], (128,) + a.shape)).astype(np.float32)
    m = {}
    m["w_in"] = np.ascontiguousarray(w_in, dtype=np.float32)
    m["w_out"] = np.ascontiguousarray(w_out, dtype=np.float32)
    m["nwB"] = rep(np.asarray(norm_w, np.float32))
    m["fnwB"] = rep(np.asarray(final_norm_w, np.float32))
    m["caw"] = np.ascontiguousarray(np.asarray(conv_a_w, np.float32).reshape(L, 3, 4, 128).transpose(3, 0, 2, 1))
    m["dcw"] = np.ascontiguousarray(np.asarray(dn_conv_w, np.float32).reshape(L, 3, 18, 128).transpose(3, 0, 2, 1))
    m["sinkB"] = rep(np.asarray(attn_sink, np.float32))
    m["alogB"] = rep(np.asarray(dn_a_log, np.float32).reshape(L, 12))
    m["dtbB"] = rep(np.asarray(dn_dt_bias, np.float32).reshape(L, 12))
    m["dnwB"] = rep(np.asarray(dn_norm_w, np.float32))
    return m


def core_inputs(x_seq, meta_tokens, NB):
    T = NB * 128
    h0 = np.zeros((T, D), np.float32)
    valid = np.zeros((T,), np.float32)
    if x_seq is not None:
        Sx = x_seq.shape[0]
        h0[PAD:LEAD] = meta_tokens
        h0[LEAD:LEAD + Sx] = x_seq
        valid[PAD:LEAD + Sx] = 1.0
    return {
        "h0": h0,
        "validT": np.ascontiguousarray(valid.reshape(NB, 128).T),
        "validB": np.ascontiguousarray(np.broadcast_to(valid[None], (128, T))),
    }


def kernel(x_prompt, x_sample, meta_tokens, norm_w, w_in, conv_a_w, attn_sink, dn_conv_w,
           dn_a_log, dn_dt_bias, dn_norm_w, w_out, final_norm_w):
    x_prompt = np.asarray(x_prompt, np.float32)
    x_sample = np.asarray(x_sample, np.float32)
    meta_tokens = np.asarray(meta_tokens, np.float32)
    L = w_in.shape[0]
    Sp = x_prompt.shape[1]
    Ss = x_sample.shape[1]
    NB = (LEAD + max(Sp, Ss)) // 128
    NBOUT = NB - 1
    shared = prep_weights(norm_w, np.asarray(w_in), conv_a_w, attn_sink, dn_conv_w, dn_a_log, dn_dt_bias,
                          dn_norm_w, np.asarray(w_out), final_norm_w)
    shared.update(make_consts())
    seqs = [x_prompt[i] for i in range(x_prompt.shape[0])] + [x_sample[i] for i in range(x_sample.shape[0])]
    assert len(seqs) <= 8
    in_maps = []
    for c in range(8):
        m = dict(shared)
        m.update(core_inputs(seqs[c] if c < len(seqs) else None, meta_tokens, NB))
        in_maps.append(m)
    nc = build_nc(NB, L, NBOUT)
    res = run_bass_kernel_spmd(nc, in_maps, core_ids=list(range(8)))
    outs = [np.asarray(r["out"]) for r in res.results]
    nP = x_prompt.shape[0]
    y_prompt = np.stack([outs[i][:Sp] for i in range(nP)], axis=0).astype(np.float32)
    y_sample = np.stack([outs[nP + i][:Ss] for i in range(x_sample.shape[0])], axis=0).astype(np.float32)
    return (y_prompt, y_sample)
```

```python
import contextlib
import numpy as np
import concourse.bass as bass
import concourse.mybir as mybir
from concourse.bass_utils import run_bass_kernel_spmd

F32 = mybir.dt.float32
BF16 = mybir.dt.bfloat16
F32R = mybir.dt.float32r
AF = mybir.ActivationFunctionType
ALU = mybir.AluOpType

import os as _os
BCUT = int(_os.environ.get('BCUT', '0'))
CCUT = int(_os.environ.get('CCUT', '0'))
D = 2048
KC = 16
NPROJ = 7192
LEAD = 128
PAD = 112
NEG = -1.0e9
EPS = 1e-6


class Sched:
    COMPUTE = ("pe", "act", "dve", "pool")
    ALLENG = ("pe", "act", "dve", "pool", "sp")
    NDMA = {"sp": 8, "pool": 4}

    def __init__(self, nc):
        self.nc = nc
        self.gstack = contextlib.ExitStack()
        self.sems = {}
        for e in self.COMPUTE:
            self.sems["e:" + e] = self.gstack.enter_context(nc.semaphore("s_" + e))
        for q, k in self.NDMA.items():
            for j in range(k):
                self.sems[f"d:{q}:{j}"] = self.gstack.enter_context(nc.semaphore(f"d_{q}{j}"))
        self.seq = {e: 0 for e in self.COMPUTE}
        self.ndma = {q: 0 for q in self.NDMA}
        self.last_tok = {}
        self.known = {e: {} for e in self.ALLENG}
        self.last_w = {}
        self.readers = {}
        self.ops = {e: [] for e in self.ALLENG}
        self.final_tokens = []
        self.barrier_tokens = []
        self.pstack = None
        self.nops = 0
        self.bankmap = {}
        self.bank_last = {}

    def _uid(self):
        self.uid = getattr(self, "uid", 0) + 1
        return f"_{self.uid}"

    def sb(self, name, shape, dtype, glob=False):
        st = self.gstack if (glob or self.pstack is None) else self.pstack
        return st.enter_context(self.nc.sbuf_tensor("s_" + name + self._uid(), list(shape), dtype))

    def ps(self, name, shape, dtype, nbanks=1):
        nbytes = (4 if dtype == F32 else 2)
        for s_ in shape[1:]:
            nbytes *= s_
        assert nbytes == 2048 * nbanks, (name, shape, nbytes)
        self.bankmap[name] = [f"B:{name}:{i}" for i in range(nbanks)]
        return self.pstack.enter_context(self.nc.psum_tensor("p_" + name + self._uid(), list(shape), dtype))

    def _split(self, keys):
        reg, banks = [], []
        for k in keys:
            if k in self.bankmap:
                banks.extend(self.bankmap[k])
            else:
                reg.append(k)
        return reg, banks

    def _bank_deps(self, eng, banks):
        deps = []
        for bk in banks:
            for e2, tok in self.bank_last.get(bk, {}).items():
                if e2 != eng:
                    deps.append(tok)
        return deps

    def _bank_commit(self, eng, banks, token):
        for bk in banks:
            self.bank_last.setdefault(bk, {})[eng] = token

    def _deps(self, reads, writes):
        deps = list(self.barrier_tokens)
        for k in reads:
            t = self.last_w.get(k)
            if t is not None:
                deps.append(t)
        for k in writes:
            t = self.last_w.get(k)
            if t is not None:
                deps.append(t)
            deps.extend(self.readers.get(k, ()))
        return deps

    def _commit(self, token, reads, writes):
        self.last_tok[token[0]] = token[1]
        for k in reads:
            self.readers.setdefault(k, []).append(token)
        for k in writes:
            self.last_w[k] = token
            self.readers[k] = []

    def _waits(self, issuer, deps, skip_key=None):
        kn = self.known[issuer]
        need = {}
        for (key, val) in deps:
            if key == skip_key:
                continue
            if kn.get(key, 0) < val and need.get(key, 0) < val:
                need[key] = val
        for key, val in need.items():
            kn[key] = val
        return list(need.items())

    def op(self, eng, fn, reads=(), writes=()):
        reads, b1 = self._split(reads)
        writes, b2 = self._split(writes)
        banks = set(b1 + b2)
        deps = self._deps(reads, writes) + self._bank_deps(eng, banks)
        key = "e:" + eng
        waits = self._waits(eng, deps, skip_key=key if eng == "pe" else None)
        self.seq[eng] += 1
        token = (key, self.seq[eng])
        self.ops[eng].append((waits, fn, key, 1))
        self._commit(token, reads, writes)
        self._bank_commit(eng, banks, token)
        self.nops += 1
        return token

    def dma(self, q, out, in_, reads=(), writes=(), final=False):
        n = self.ndma[q]
        self.ndma[q] += 1
        K = self.NDMA[q]
        key = f"d:{q}:{n % K}"
        deps = self._deps(reads, writes)
        if n // K > 0:
            deps.append((key, 16 * (n // K)))
        waits = self._waits(q, deps)
        token = (key, 16 * (n // K + 1))
        self.ops[q].append((waits, lambda e: e.dma_start(out=out, in_=in_), key, 16))
        self._commit(token, reads, writes)
        if final:
            self.final_tokens.append(token)
        self.nops += 1
        return token

    def barrier(self):
        self.barrier_tokens = list(self.last_tok.items())

    @contextlib.contextmanager
    def phase(self, last=False):
        self.pstack = contextlib.ExitStack()
        self.barrier()
        yield self
        self._emit(last)
        self.pstack.close()
        self.pstack = None

    def _emit(self, last):
        nc = self.nc
        sems = self.sems
        fin = []
        if last:
            fin = self._waits("sp", self.final_tokens)
        ops = self.ops

        def run(engine, lst, tail=()):
            for (waits, fn, key, amt) in lst:
                for (k, v) in waits:
                    engine.wait_ge(sems[k], v)
                inst = fn(engine)
                inst.then_inc(sems[key], amt)
            for (k, v) in tail:
                engine.wait_ge(sems[k], v)

        with nc.Block() as block:
            @block.sync
            def _(e):
                run(e, ops["sp"], fin)

            @block.tensor
            def _(e):
                run(e, ops["pe"])

            @block.scalar
            def _(e):
                run(e, ops["act"])

            @block.vector
            def _(e):
                run(e, ops["dve"])

            @block.gpsimd
            def _(e):
                run(e, ops["pool"])
        self.ops = {e: [] for e in self.ALLENG}

    def close(self):
        self.gstack.close()


def build_nc(NB, L, NBOUT, G=4, stop=None):
    T = NB * 128
    nc = bass.Bass("TRN2", target_bir_lowering=False)
    dt_in = lambda n, s, d=F32: nc.dram_tensor(n, list(s), d, kind="ExternalInput").ap()
    dt_sc = lambda n, s, d=F32: nc.dram_tensor(n, list(s), d, kind="Internal").ap()

    h0 = dt_in("h0", [T, D])
    validT_d = dt_in("validT", [128, NB])
    validB_d = dt_in("validB", [128, T])
    w_in_d = dt_in("w_in", [L, D, NPROJ])
    w_out_d = dt_in("w_out", [L, D, D])
    nwB_d = dt_in("nwB", [128, L, D])
    fnwB_d = dt_in("fnwB", [128, D])
    caw_d = dt_in("caw", [128, L, 4, 3])
    dcw_d = dt_in("dcw", [128, L, 18, 3])
    sinkB_d = dt_in("sinkB", [128, L, 6])
    alogB_d = dt_in("alogB", [128, L, 12])
    dtbB_d = dt_in("dtbB", [128, L, 12])
    dnwB_d = dt_in("dnwB", [128, L, 128])
    c_identb_d = dt_in("c_identb", [128, 128])
    c_U_d = dt_in("c_U", [128, 2, 128])
    c_maskB_d = dt_in("c_maskB", [128, 2, 128])
    c_strict_d = dt_in("c_strict", [128, 2, 128])
    c_AL_d = dt_in("c_AL", [128, 3, 768])
    out_d = nc.dram_tensor("out", [NBOUT * 128, D], F32, kind="ExternalOutput").ap()

    Wb_in = dt_sc("Wb_in", [L, D, NPROJ], BF16)
    Wb_out = dt_sc("Wb_out", [L, D, D], BF16)
    H = dt_sc("H", [T, D])
    ZS = dt_sc("ZS", [D, T], BF16)
    UC = dt_sc("UC", [512, T])
    CB = dt_sc("CB", [512, T])
    QT = dt_sc("QT", [768, T], BF16)
    KT = dt_sc("KT", [256, T], BF16)
    VV = dt_sc("VV", [T, 256], BF16)
    DQKV = dt_sc("DQKV", [2304, T])
    BG = dt_sc("BG", [T, 24])
    KQT = dt_sc("KQT", [NB, 128, 6 * 2 * 128], BF16)
    KN = dt_sc("KN", [NB, 128, 768], BF16)
    VN = dt_sc("VN", [NB, 128, 768], BF16)
    OF = dt_sc("OF", [NB, 128, 768])
    YDN = dt_sc("YDN", [NB, 128, 768], BF16)

    S = Sched(nc)
    identb = S.sb("identb", [128, 128], BF16)
    identf = S.sb("identf", [128, 128], F32)
    Uf = S.sb("Uf", [128, 2, 128], F32)
    maskB = S.sb("maskB", [128, 2, 128], F32)
    strictM = S.sb("strictM", [128, 2, 128], F32)
    AL = S.sb("AL", [128, 3, 768], F32)
    onesf = S.sb("onesf", [128, 128], F32)
    onesb = S.sb("onesb", [128, 128], BF16)
    validT = S.sb("validT", [128, NB], F32)
    kbias = S.sb("kbias", [128, NB], F32)
    caw = S.sb("caw", [128, L, 4, 3], F32)
    dcw = S.sb("dcw", [128, L, 18, 3], F32)
    sinkB = S.sb("sinkB", [128, L, 6], F32)
    esink = S.sb("esink", [128, L, 6], F32)
    alogB = S.sb("alogB", [128, L, 12], F32)
    negA = S.sb("negA", [128, L, 12], F32)
    dtbB = S.sb("dtbB", [128, L, 12], F32)
    dnwB = S.sb("dnwB", [128, L, 128], F32)
    fnwB = S.sb("fnwB", [128, D], F32)
    nw = S.sb("nw", [128, D], F32)
    betaT = S.sb("betaT", [128, NB, 12], F32)
    nbetaT = S.sb("nbetaT", [128, NB, 12], F32)
    gT = S.sb("gT", [128, NB, 12], F32)
    epsc = S.sb("epsc", [128, 1], F32)
    onec = S.sb("onec", [128, 1], F32)

    with S.phase():
        S.dma("pool", identb[:], c_identb_d, writes=["identb"])
        S.dma("sp", identf[:], c_identb_d, writes=["identf"])
        S.dma("sp", Uf[:], c_U_d, writes=["Uf"])
        S.dma("sp", maskB[:], c_maskB_d, writes=["maskB"])
        S.dma("sp", strictM[:], c_strict_d, writes=["strictM"])
        S.dma("sp", AL[:], c_AL_d, writes=["AL"])
        S.dma("sp", validT[:], validT_d, writes=["validT"])
        S.dma("sp", caw[:], caw_d, writes=["caw"])
        S.dma("sp", dcw[:], dcw_d, writes=["dcw"])
        S.dma("sp", sinkB[:], sinkB_d, writes=["sinkB"])
        S.dma("sp", alogB[:], alogB_d, writes=["alogB"])
        S.dma("sp", dtbB[:], dtbB_d, writes=["dtbB"])
        S.dma("sp", dnwB[:], dnwB_d, writes=["dnwB"])
        S.dma("sp", fnwB[:], fnwB_d, writes=["fnwB"])
        S.op("pool", lambda e: e.memset(onesf[:], 1.0), writes=["onesf"])
        S.op("pool", lambda e: e.memset(onesb[:], 1.0), writes=["onesb"])
        S.op("pool", lambda e: e.memset(epsc[:], EPS), writes=["epsc"])
        S.op("pool", lambda e: e.memset(onec[:], 1.0), writes=["onec"])
        S.op("dve", lambda e: e.tensor_scalar(kbias[:], validT[:], -1.0, -NEG, ALU.add, ALU.mult),
             reads=["validT"], writes=["kbias"])
        S.op("act", lambda e: e.activation(esink[:], sinkB[:], AF.Exp), reads=["sinkB"], writes=["esink"])
        S.op("act", lambda e: e.activation(negA[:], alogB[:], AF.Exp), reads=["alogB"], writes=["negA"])
        S.op("dve", lambda e: e.tensor_scalar(negA[:], negA[:], -1.0, None, ALU.mult),
             reads=["negA"], writes=["negA"])
        def convert_weights(lw):
            for kc in range(KC):
                S.dma("pool", Wb_in[lw, kc * 128:(kc + 1) * 128, :], w_in_d[lw, kc * 128:(kc + 1) * 128, :],
                      writes=[f"Wb_in{lw}"])
            for kc in range(0, KC, 4):
                S.dma("pool", Wb_out[lw, kc * 128:(kc + 4) * 128, :], w_out_d[lw, kc * 128:(kc + 4) * 128, :],
                      writes=[f"Wb_out{lw}"])
        convert_weights(0)

    if stop == '0':
        S.close()
        return nc
    for l in range(L):
        Hin = h0 if l == 0 else H
        hin_key = "h0" if l == 0 else "H"
        with S.phase():
            N = G * 128
            ht = [S.sb(f"ht{i}", [128, D], F32) for i in range(2)]
            junk = S.sb("junk", [128, D], BF16)
            xn = S.sb("xn", [128, D], BF16)
            xnT2 = [S.sb(f"xnT{i}", [128, KC, N], BF16) for i in range(2)]
            wt = [S.sb(f"wt{i}", [128, KC, 512], BF16) for i in range(2)]
            wlast = S.sb("wlast", [128, KC, 24], BF16)
            cxs = S.sb("cxs", [128, 4, N], F32)
            ef = [S.sb(f"ef{i}", [128, N], F32) for i in range(3)]
            eb = [S.sb(f"eb{i}", [128, N], BF16) for i in range(3)]
            vtok = S.sb("vtok", [128, 256], BF16)
            bgt = S.sb("bgt", [128, 24], F32)
            ss = S.sb("ss", [128, 2], F32)
            ptr = [S.ps(f"ptr{i}", [128, 8, 128], BF16) for i in range(2)]
            pa = [S.ps(f"pa{i}", [128, 512], F32) for i in range(2)]
            pv_ = S.ps("pv", [128, 512], F32)
            pv = pv_[:, 0:256]
            pbg_ = S.ps("pbg", [128, 512], F32)
            pbg = pbg_[:, 0:24]

            S.dma("sp", nw[:], nwB_d[:, l, :], writes=["nw"])
            S.dma("sp", wlast[:], Wb_in[l, :, 7168:7192].rearrange("(kc p) c -> p kc c", p=128),
                  reads=[f"Wb_in{l}"], writes=["wlast"])
            ngroups = (NB + G - 1) // G
            ecnt = [0]

            def norm_gen(gi):
                gp = gi % 2
                xnT = xnT2[gp]
                b0 = gi * G
                gb = min(G, NB - b0)
                for bi in range(gb):
                    b = b0 + bi
                    hh = ht[b % 2]
                    hk = f"ht{b % 2}"
                    S.dma("sp", hh[:], Hin[b * 128:(b + 1) * 128, :], reads=[hin_key], writes=[hk])
                    S.op("act", lambda e, hh=hh: e.activation(junk[:], hh[:], AF.Square, accum_out=ss[:, 0:1]),
                         reads=[hk], writes=["junk", "ss0"])
                    S.op("act", lambda e: e.activation(ss[:, 1:2], ss[:, 0:1], AF.Sqrt, bias=epsc[:, 0:1], scale=1.0 / D),
                         reads=["ss0", "epsc"], writes=["ss1"])
                    S.op("dve", lambda e: e.reciprocal(ss[:, 1:2], ss[:, 1:2]),
                         reads=["ss1"], writes=["ss1"])
                    S.op("dve", lambda e, hh=hh: e.scalar_tensor_tensor(xn[:], hh[:], ss[:, 1:2], nw[:], ALU.mult, ALU.mult),
                         reads=[hk, "ss1", "nw"], writes=["xn"])
                    yield
                    for q4 in range(4):
                        pt = ptr[q4 % 2]
                        pk = f"ptr{q4 % 2}"

                        def tr(e, pt=pt, q4=q4):
                            r = None
                            for j in range(4):
                                kc = q4 * 4 + j
                                r = e.transpose(pt[:, j, :], xn[:, kc * 128:(kc + 1) * 128], identb[:])
                            return r
                        S.op("pe", tr, reads=["xn", "identb"], writes=[pk])
                        if q4 % 2 == 0:
                            S.op("act", lambda e, pt=pt, q4=q4, bi=bi: e.copy(xnT[:, q4 * 4:(q4 + 1) * 4, bi * 128:(bi + 1) * 128], pt[:, 0:4, :]),
                                 reads=[pk], writes=[f"xnT{gp}_{bi}"])
                        else:
                            S.op("dve", lambda e, pt=pt, q4=q4, bi=bi: e.tensor_copy(xnT[:, q4 * 4:(q4 + 1) * 4, bi * 128:(bi + 1) * 128], pt[:, 0:4, :]),
                                 reads=[pk], writes=[f"xnT{gp}_{bi}"])
                        if q4 % 2 == 1:
                            yield

            def drain(gen, n=None):
                if gen is None:
                    return
                k = 0
                for _ in gen:
                    k += 1
                    if n is not None and k >= n:
                        return

            drain(norm_gen(0))
            for gi in range(ngroups):
                gp = gi % 2
                xnT = xnT2[gp]
                b0 = gi * G
                gb = min(G, NB - b0)
                n = gb * 128
                nxt = norm_gen(gi + 1) if gi + 1 < ngroups else None
                xk = [f"xnT{gp}_{bi}" for bi in range(gb)]
                for u in range(14):
                    w = wt[u % 2]
                    wk = f"wt{u % 2}"
                    S.dma("sp", w[:], Wb_in[l, :, u * 512:(u + 1) * 512].rearrange("(kc p) c -> p kc c", p=128),
                          reads=[f"Wb_in{l}"], writes=[wk])
                    for c4 in range(4):
                        col = u * 512 + c4 * 128
                        ch = col // 128
                        if 3072 <= col < 3328:
                            continue
                        p = pa[ecnt[0] % 2]
                        pk = f"pa{ecnt[0] % 2}"
                        ecnt[0] += 1

                        def mm(e, p=p, w=w, c4=c4, n=n, xnT=xnT):
                            r = None
                            for kc in range(KC):
                                r = e.matmul(p[:, 0:n], w[:, kc, c4 * 128:(c4 + 1) * 128], xnT[:, kc, 0:n],
                                             start=(kc == 0), stop=(kc == KC - 1))
                            return r
                        S.op("pe", mm, reads=[wk] + xk, writes=[pk])
                        tcols = slice(b0 * 128, b0 * 128 + n)
                        i3 = ch % 3
                        if col < 512:
                            S.op("act", lambda e, p=p, ch=ch, n=n: e.copy(cxs[:, ch, 0:n], p[:, 0:n]),
                                 reads=[pk], writes=[f"cxs{ch}"])
                        elif col < 1024:
                            S.op("act", lambda e, p=p, i3=i3, n=n: e.copy(ef[i3][:, 0:n], p[:, 0:n]),
                                 reads=[pk], writes=[f"ef{i3}"])
                            S.dma("pool", CB[col - 512:col - 512 + 128, tcols], ef[i3][:, 0:n], reads=[f"ef{i3}"], writes=["CB"])
                        elif col < 1536:
                            cxi = (col - 1024) // 128
                            S.op("dve", lambda e, p=p, i3=i3, n=n, cxi=cxi: e.tensor_tensor(ef[i3][:, 0:n], p[:, 0:n], cxs[:, cxi, 0:n], ALU.mult),
                                 reads=[pk, f"cxs{cxi}"], writes=[f"ef{i3}"])
                            S.dma("pool", UC[col - 1024:col - 1024 + 128, tcols], ef[i3][:, 0:n], reads=[f"ef{i3}"], writes=["UC"])
                        elif col < 2048 or 3328 <= col < 4096 or 6400 <= col < 7168:
                            if col < 2048:
                                zr = col - 1536
                            elif col < 4096:
                                zr = 512 + col - 3328
                            else:
                                zr = 1280 + col - 6400
                            S.op("act", lambda e, p=p, i3=i3, n=n: e.activation(eb[i3][:, 0:n], p[:, 0:n], AF.Silu),
                                 reads=[pk], writes=[f"eb{i3}"])
                            S.dma("pool", ZS[zr:zr + 128, tcols], eb[i3][:, 0:n], reads=[f"eb{i3}"], writes=["ZS"])
                        elif col < 2816:
                            S.op("act", lambda e, p=p, i3=i3, n=n: e.activation(eb[i3][:, 0:n], p[:, 0:n], AF.Copy, scale=128.0 ** -0.5),
                                 reads=[pk], writes=[f"eb{i3}"])
                            S.dma("pool", QT[col - 2048:col - 2048 + 128, tcols], eb[i3][:, 0:n], reads=[f"eb{i3}"], writes=["QT"])
                        elif col < 3072:
                            S.op("dve", lambda e, p=p, i3=i3, n=n: e.tensor_copy(eb[i3][:, 0:n], p[:, 0:n]),
                                 reads=[pk], writes=[f"eb{i3}"])
                            S.dma("pool", KT[col - 2816:col - 2816 + 128, tcols], eb[i3][:, 0:n], reads=[f"eb{i3}"], writes=["KT"])
                        else:
                            r0 = col - 4096
                            S.op("dve", lambda e, p=p, i3=i3, n=n: e.tensor_copy(ef[i3][:, 0:n], p[:, 0:n]),
                                 reads=[pk], writes=[f"ef{i3}"])
                            S.dma("pool", DQKV[r0:r0 + 128, tcols], ef[i3][:, 0:n], reads=[f"ef{i3}"], writes=["DQKV"])
                    if u == 6:
                        for bi in range(gb):
                            b = b0 + bi

                            def mmv(e, w=w, bi=bi, xnT=xnT):
                                r = None
                                for kc in range(KC):
                                    r = e.matmul(pv, xnT[:, kc, bi * 128:(bi + 1) * 128], w[:, kc, 0:256],
                                                 start=(kc == 0), stop=(kc == KC - 1))
                                return r
                            S.op("pe", mmv, reads=[wk, f"xnT{gp}_{bi}"], writes=["pv"])
                            S.op("act", lambda e: e.copy(vtok[:], pv), reads=["pv"], writes=["vtok"])
                            S.dma("pool", VV[b * 128:(b + 1) * 128, :], vtok[:], reads=["vtok"], writes=["VV"])
                    if u >= 2:
                        drain(nxt, 1)
                for bi in range(gb):
                    b = b0 + bi

                    def mmb(e, bi=bi, xnT=xnT):
                        r = None
                        for kc in range(KC):
                            r = e.matmul(pbg, xnT[:, kc, bi * 128:(bi + 1) * 128], wlast[:, kc, :],
                                         start=(kc == 0), stop=(kc == KC - 1))
                        return r
                    S.op("pe", mmb, reads=["wlast", f"xnT{gp}_{bi}"], writes=["pbg"])
                    S.op("dve", lambda e: e.tensor_copy(bgt[:], pbg), reads=["pbg"], writes=["bgt"])
                    S.dma("pool", BG[b * 128:(b + 1) * 128, :], bgt[:], reads=["bgt"], writes=["BG"])
                drain(nxt)

        if stop == 'A':
            S.close()
            return nc
        with S.phase():
            def T2(name, shape, dt):
                return [S.sb(f"{name}{i}", shape, dt) for i in range(2)]
            raw = T2("raw", [128, 18, 130], F32)
            cv = T2("cv", [128, 18, 128], F32)
            sq = T2("sq", [128, 12, 128], BF16)
            rs = T2("rs", [128, 12, 128], F32)
            vB = T2("vB", [128, 128], F32)
            kq = T2("kq", [128, 6, 2, 128], BF16)
            vT = T2("vT", [128, 6, 128], BF16)
            kv_tok = T2("kv_tok", [128, 12, 128], BF16)
            bgt2 = T2("bgt2", [128, 24], F32)
            spt = T2("spt", [128, 12], F32)
            pss = S.ps("pss", [128, 12, 128], F32, nbanks=3)
            ptk_ = S.ps("ptk", [128, 16, 128], BF16, nbanks=2)

            def a2_block(b):
                par = b % 2
                P = str(par)
                raw_, cv_, sq_, rs_, vB_, kq_, vT_, kvt_, bg_, sp_ = (raw[par], cv[par], sq[par], rs[par], vB[par], kq[par],
                                                                     vT[par], kv_tok[par], bgt2[par], spt[par])
                t0 = b * 128
                lo = 1 if b == 0 else 0
                hi = 129 if b == NB - 1 else 130
                if lo == 1:
                    S.op("pool", lambda e: e.memset(raw_[:, :, 0:1], 0.0), writes=["raw" + P])
                if hi == 129:
                    S.op("pool", lambda e: e.memset(raw_[:, :, 129:130], 0.0), writes=["raw" + P])
                S.dma("sp", raw_[:, :, lo:hi],
                      DQKV[:, t0 - 1 + lo:t0 - 1 + hi].rearrange("(c p) t -> p c t", p=128),
                      reads=["DQKV"], writes=["raw" + P])
                S.dma("sp", vB_[:], validB_d[:, t0:t0 + 128], writes=["vB" + P])
                S.dma("sp", bg_[:], BG[t0:t0 + 128, :], reads=["BG"], writes=["bgt2" + P])
                S.op("act", lambda e: e.activation(betaT[:, b, :], bg_[:, 0:12], AF.Sigmoid),
                     reads=["bgt2" + P], writes=["betaT"])
                S.op("dve", lambda e: e.tensor_scalar(nbetaT[:, b, :], betaT[:, b, :], -1.0, None, ALU.mult),
                     reads=["betaT"], writes=["nbetaT"])
                S.op("dve", lambda e: e.tensor_tensor(sp_[:], bg_[:, 12:24], dtbB[:, l, :], ALU.add),
                     reads=["bgt2" + P, "dtbB"], writes=["spt" + P])
                S.op("act", lambda e: e.activation(sp_[:], sp_[:], AF.Exp), reads=["spt" + P], writes=["spt" + P])
                S.op("act", lambda e: e.activation(sp_[:], sp_[:], AF.Ln, bias=onec[:, 0:1]), reads=["spt" + P, "onec"], writes=["spt" + P])
                S.op("dve", lambda e: e.tensor_tensor(gT[:, b, :], sp_[:], negA[:, l, :], ALU.mult),
                     reads=["spt" + P, "negA"], writes=["gT"])
                for c in range(18):
                    ck = f"cv{P}_{c}"
                    S.op("act", lambda e, c=c: e.activation(cv_[:, c, :], raw_[:, c, 1:129], AF.Copy, scale=dcw[:, l, c, 1:2]),
                         reads=["raw" + P, "dcw"], writes=[ck])
                    S.op("dve", lambda e, c=c: e.scalar_tensor_tensor(cv_[:, c, :], raw_[:, c, 0:128], dcw[:, l, c, 0:1], cv_[:, c, :], ALU.mult, ALU.add),
                         reads=["raw" + P, ck], writes=[ck])
                    S.op("dve", lambda e, c=c: e.scalar_tensor_tensor(cv_[:, c, :], raw_[:, c, 2:130], dcw[:, l, c, 2:3], cv_[:, c, :], ALU.mult, ALU.add),
                         reads=["raw" + P, ck], writes=[ck])
                cvk = [f"cv{P}_{c}" for c in range(18)]
                S.op("act", lambda e: e.activation(cv_[:], cv_[:], AF.Silu), reads=cvk, writes=cvk)
                S.op("act", lambda e: e.activation(sq_[:], cv_[:, 0:12, :], AF.Square), reads=cvk, writes=["sq" + P])

                def mss(e):
                    r = None
                    for j in range(3):
                        r = e.matmul(pss[:, j * 4:(j + 1) * 4, :], onesb[:], sq_[:, j * 4:(j + 1) * 4, :], start=True, stop=True)
                    return r
                S.op("pe", mss, reads=["sq" + P, "onesb"], writes=["pss"])
                S.op("act", lambda e: e.activation(rs_[:], pss[:], AF.Ln, bias=epsc[:, 0:1], scale=1.0),
                     reads=["pss", "epsc"], writes=["rs" + P])
                S.op("act", lambda e: e.activation(rs_[:], rs_[:], AF.Exp, scale=-0.5), reads=["rs" + P], writes=["rs" + P])
                S.op("dve", lambda e: e.tensor_tensor(rs_[:, 6:12, :], rs_[:, 6:12, :], vB_[:].unsqueeze(1).broadcast_to([128, 6, 128]), ALU.mult),
                     reads=["rs" + P, "vB" + P], writes=["rs" + P])
                S.op("dve", lambda e: e.scalar_tensor_tensor(kq_[:, :, 1, :], cv_[:, 0:6, :], 128.0 ** -0.5, rs_[:, 0:6, :], ALU.mult, ALU.mult),
                     reads=cvk + ["rs" + P], writes=["kq" + P])
                S.op("dve", lambda e: e.tensor_tensor(kq_[:, :, 0, :], cv_[:, 6:12, :], rs_[:, 6:12, :], ALU.mult),
                     reads=cvk + ["rs" + P], writes=["kq" + P])
                S.op("dve", lambda e: e.tensor_copy(vT_[:], cv_[:, 12:18, :]), reads=cvk, writes=["vT" + P])

                def trk(e):
                    r = None
                    for h in range(6):
                        r = e.transpose(ptk_[:, h, :], kq_[:, h, 0, :], identb[:])
                    for h in range(6):
                        r = e.transpose(ptk_[:, 6 + h, :], vT_[:, h, :], identb[:])
                    return r
                S.op("pe", trk, reads=["kq" + P, "vT" + P, "identb"], writes=["ptk"])
                S.op("act", lambda e: e.copy(kvt_[:], ptk_[:, 0:12, :]), reads=["ptk"], writes=["kv_tok" + P])
                S.dma("sp", KQT[b], kq_[:].rearrange("p h two t -> p (h two t)"), reads=["kq" + P], writes=["KQT"])
                S.dma("sp", KN[b], kvt_[:, 0:6, :].rearrange("p h d -> p (h d)"), reads=["kv_tok" + P], writes=["KN"])
                S.dma("sp", VN[b], kvt_[:, 6:12, :].rearrange("p h d -> p (h d)"), reads=["kv_tok" + P], writes=["VN"])

            if l + 1 < L:
                convert_weights(l + 1)
            for b in range(NB):
                a2_block(b)

        if stop == 'A2':
            S.close()
            return nc
        with S.phase():
            def T2(name, shape, dt):
                return [S.sb(f"{name}{i}", shape, dt) for i in range(2)]
            kqt = T2("kqt", [128, 6, 2, 128], BF16)
            knt = T2("knt", [128, 6, 128], BF16)
            vnt = T2("vnt", [128, 6, 128], BF16)
            gcs = T2("gcs", [128, 12], F32)
            ngc = T2("ngc", [128, 6], F32)
            egs = T2("egs", [128, 18], F32)
            gU = T2("gU", [128, 6, 128], F32)
            GCs = T2("GCs", [128, 6, 128], F32)
            Et = T2("Et", [128, 6, 128], F32)
            DT = T2("DT", [128, 6, 128], F32)
            EG = T2("EG", [128, 6, 128], F32)
            tt = T2("tt", [128, 6, 128], F32)
            qg = T2("qg", [128, 6, 128], BF16)
            MT = T2("MT", [128, 6, 128], BF16)
            X = [[S.sb(f"X{p}{i}", [128, 6, 3, 128], F32R) for i in range(2)] for p in range(2)]
            identr = S.sb("identr", [128, 128], F32R)
            S.op("dve", lambda e: e.tensor_copy(identr[:], identf[:]), reads=["identf"], writes=["identr"])
            Pb = T2("Pb", [128, 6, 128], BF16)
            kg = T2("kg", [128, 6, 128], BF16)
            kt = T2("kt", [128, 6, 128], BF16)
            nWT = S.sb("nWT", [128, 6, 128], BF16)
            vnew = S.sb("vnew", [128, 6, 128], BF16)
            Sf = S.sb("Sf", [128, 6, 128], F32)
            Sb = S.sb("Sb", [128, 6, 128], BF16)
            of = T2("of", [128, 6, 128], F32)
            ofl = T2("ofl", [128, 6, 128], F32)
            osq = S.sb("osq", [128, 6, 128], F32)
            zsd = T2("zsd", [128, 6, 128], BF16)
            ydn = S.sb("ydn", [128, 6, 128], BF16)
            rn = S.sb("rn", [128, 12], F32)
            Hps = [S.ps(f"H{h}", [128, 4, 128], F32) for h in range(6)]
            QA = S.ps("QA", [128, 4, 128], F32)
            QB = S.ps("QB", [128, 4, 128], F32)
            onf = S.sb("onf", [128, 6, 128], F32)

            def part0(dirn, b, par):
                S.dma("sp", kqt[par][:].rearrange("p h two t -> p (h two t)"), KQT[b], reads=["KQT"], writes=[f"kqt{par}"])
                S.dma("sp", knt[par][:].rearrange("p h d -> p (h d)"), KN[b], reads=["KN"], writes=[f"knt{par}"])
                S.dma("sp", vnt[par][:].rearrange("p h d -> p (h d)"), VN[b], reads=["VN"], writes=[f"vnt{par}"])
                if dirn == 1:
                    S.dma("sp", ofl[par][:].rearrange("p h d -> p (h d)"), OF[b], reads=["OF"], writes=[f"ofl{par}"])
                    S.dma("sp", zsd[par][:], ZS[1280:2048, b * 128:(b + 1) * 128].rearrange("(h p) t -> p h t", p=128),
                          reads=["ZS"], writes=[f"zsd{par}"])
                gsl = gT[:, b, dirn * 6:(dirn + 1) * 6]

                def mgc(e):
                    e.matmul(Hps[0][:, 3, 0:6], Uf[:, dirn, :], gsl, start=True, stop=True)
                    return e.matmul(Hps[0][:, 3, 6:12], onesf[:], gsl, start=True, stop=True)
                S.op("pe", mgc, reads=["Uf", "onesf", "gT"], writes=["H0"])
                S.op("dve", lambda e: e.tensor_copy(gcs[par][:], Hps[0][:, 3, 0:12]), reads=["H0"], writes=[f"gcs{par}"])
                S.op("dve", lambda e: e.tensor_scalar(ngc[par][:], gcs[par][:, 0:6], -1.0, None, ALU.mult), reads=[f"gcs{par}"], writes=[f"ngc{par}"])
                S.op("dve", lambda e: e.tensor_tensor(egs[par][:, 6:12], gcs[par][:, 6:12], gcs[par][:, 0:6], ALU.subtract), reads=[f"gcs{par}"], writes=[f"egs1{par}"])
                S.op("act", lambda e: e.activation(egs[par][:, 0:6], gcs[par][:, 0:6], AF.Exp), reads=[f"gcs{par}"], writes=[f"egs0{par}"])
                S.op("act", lambda e: e.activation(egs[par][:, 6:12], egs[par][:, 6:12], AF.Exp), reads=[f"egs1{par}"], writes=[f"egs1{par}"])
                S.op("act", lambda e: e.activation(egs[par][:, 12:18], gcs[par][:, 6:12], AF.Exp), reads=[f"gcs{par}"], writes=[f"egs2{par}"])

            def part1(dirn, b, par, h):
                sfx = f"{par}_{h}"
                Hh = Hps[h]
                hk = f"H{h}"
                kq_, kn_ = kqt[par], knt[par]
                gsl = gT[:, b, dirn * 6:(dirn + 1) * 6]
                nb_ = nbetaT[:, b, dirn * 6:(dirn + 1) * 6]
                Ud = Uf[:, dirn, :]
                Xa, Xb = X[par]
                S.op("act", lambda e: e.activation(gU[par][:, h, :], Ud, AF.Copy, scale=gsl[:, h:h + 1]),
                     reads=["Uf", "gT"], writes=[f"gU{sfx}"])
                S.op("act", lambda e: e.copy(Xa[:, h, 2, :], identf[:]), reads=["identf"], writes=[f"XaP{sfx}"])
                yield

                def m1(e):
                    e.matmul(Hh[:, 3, :], onesf[:], gU[par][:, h, :], start=True, stop=True)
                    return e.matmul(Hh[:, 1:3, :], kq_[:, h, 0, :], kq_[:, h, :, :], start=True, stop=True)
                S.op("pe", m1, reads=[f"gU{sfx}", "onesf", f"kqt{par}"], writes=[hk])
                yield
                S.op("dve", lambda e: e.tensor_tensor(Et[par][:, h, :], Hh[:, 3, :], maskB[:, dirn, :], ALU.add),
                     reads=[hk, "maskB"], writes=[f"Et{sfx}"])
                S.op("act", lambda e: e.activation(EG[par][:, h, :], Hh[:, 3, :], AF.Exp), reads=[hk], writes=[f"EG{sfx}"])
                yield
                S.op("act", lambda e: e.activation(DT[par][:, h, :], Et[par][:, h, :], AF.Exp, bias=ngc[par][:, h:h + 1]),
                     reads=[f"Et{sfx}", f"ngc{par}"], writes=[f"DT{sfx}"])
                S.op("dve", lambda e: e.tensor_tensor(qg[par][:, h, :], kq_[:, h, 1, :], EG[par][:, h, :], ALU.mult),
                     reads=[f"kqt{par}", f"EG{sfx}"], writes=[f"qg{sfx}"])
                S.op("act", lambda e: e.activation(kg[par][:, h, :], kn_[:, h, :], AF.Copy, scale=egs[par][:, h:h + 1]),
                     reads=[f"knt{par}", f"egs0{par}"], writes=[f"kg{sfx}"])
                S.op("act", lambda e: e.activation(kt[par][:, h, :], kn_[:, h, :], AF.Copy, scale=egs[par][:, 6 + h:7 + h]),
                     reads=[f"knt{par}", f"egs1{par}"], writes=[f"kt{sfx}"])
                yield
                S.op("pool", lambda e: e.tensor_tensor(tt[par][:, h, :], DT[par][:, h, :], strictM[:, dirn, :], ALU.mult),
                     reads=[f"DT{sfx}", "strictM"], writes=[f"tt{sfx}"])
                S.op("dve", lambda e: e.tensor_tensor(MT[par][:, h, :], Hh[:, 2, :], DT[par][:, h, :], ALU.mult),
                     reads=[hk, f"DT{sfx}"], writes=[f"MT{sfx}"])
                yield
                S.op("dve", lambda e: e.scalar_tensor_tensor(Xa[:, h, 1, :], Hh[:, 1, :], nb_[:, h:h + 1], tt[par][:, h, :], ALU.mult, ALU.mult),
                     reads=[hk, f"tt{sfx}", "nbetaT"], writes=[f"XaQ{sfx}"])
                yield
                S.op("pe", lambda e: e.matmul(Hh[:, 0, :], Xa[:, h, 1, :], identr[:], start=True, stop=True), reads=[f"XaQ{sfx}", "identr"], writes=[hk])
                yield
                S.op("act", lambda e: e.copy(Xa[:, h, 0, :], Hh[:, 0, :]), reads=[hk], writes=[f"XaT{sfx}"])
                yield
                cur = 0
                for lev in range(7):
                    Xc, Xn_ = (Xa, Xb) if cur == 0 else (Xb, Xa)
                    cc_ = "Xa" if cur == 0 else "Xb"
                    nn_ = "Xb" if cur == 0 else "Xa"

                    def mAB(e, Xc=Xc, lev=lev):
                        if lev < 6:
                            e.matmul(Hh[:, 1:3, :], Xc[:, h, 0, :], Xc[:, h, 1:3, :], start=True, stop=True)
                            return e.matmul(Hh[:, 0, :], Xc[:, h, 1, :], Xc[:, h, 0, :], start=True, stop=True)
                        return e.matmul(Hh[:, 2, :], Xc[:, h, 0, :], Xc[:, h, 2, :], start=True, stop=True)
                    S.op("pe", mAB, reads=[cc_ + "Q" + sfx, cc_ + "T" + sfx, cc_ + "P" + sfx], writes=[hk])
                    yield
                    if lev < 6:
                        S.op("act", lambda e, Xn_=Xn_: e.copy(Xn_[:, h, 0:2, :], Hh[:, 0:2, :]), reads=[hk],
                             writes=[nn_ + "Q" + sfx, nn_ + "T" + sfx])
                        S.op("dve", lambda e, Xn_=Xn_, Xc=Xc: e.tensor_tensor(Xn_[:, h, 2, :], Hh[:, 2, :], Xc[:, h, 2, :], ALU.add),
                             reads=[hk, cc_ + "P" + sfx], writes=[nn_ + "P" + sfx])
                    else:
                        S.op("dve", lambda e, Xc=Xc: e.tensor_tensor(Pb[par][:, h, :], Hh[:, 2, :], Xc[:, h, 2, :], ALU.add),
                             reads=[hk, cc_ + "P" + sfx], writes=[f"Pb{sfx}"])
                    yield
                    cur = 1 - cur

            def part2(dirn, b, par, h):
                sfx = f"{par}_{h}"
                Qp, qk = (QA, "QA") if h % 2 == 0 else (QB, "QB")
                bt_ = betaT[:, b, dirn * 6:(dirn + 1) * 6]
                S.op("pe", lambda e: e.matmul(Qp[:, 0, :], kg[par][:, h, :], Pb[par][:, h, :], start=True, stop=True),
                     reads=[f"kg{sfx}", f"Pb{sfx}", qk], writes=[qk + "W"])
                yield
                S.op("dve", lambda e: e.tensor_scalar(nWT[:, h, :], Qp[:, 0, :], -1.0, None, ALU.mult), reads=[qk + "W", qk], writes=[f"nWT{h}"])
                yield

                def mV(e):
                    e.matmul(Qp[:, 1, :], Pb[par][:, h, :], vnt[par][:, h, :], start=True, stop=False)
                    return e.matmul(Qp[:, 1, :], nWT[:, h, :], Sb[:, h, :], start=False, stop=True)
                S.op("pe", mV, reads=[f"Pb{sfx}", f"vnt{par}", f"nWT{h}", f"Sb{h}", qk], writes=[qk + "V"])
                yield
                S.op("dve", lambda e: e.tensor_scalar(vnew[:, h, :], Qp[:, 1, :], bt_[:, h:h + 1], None, ALU.mult),
                     reads=[qk + "V", qk, "betaT"], writes=[f"vnew{h}"])
                yield

                def mOS(e):
                    e.matmul(Qp[:, 2, :], qg[par][:, h, :], Sb[:, h, :], start=True, stop=False)
                    e.matmul(Qp[:, 2, :], MT[par][:, h, :], vnew[:, h, :], start=False, stop=True)
                    return e.matmul(Qp[:, 3, :], kt[par][:, h, :], vnew[:, h, :], start=True, stop=True)
                S.op("pe", mOS, reads=[f"qg{sfx}", f"Sb{h}", f"MT{sfx}", f"vnew{h}", f"kt{sfx}", qk], writes=[qk + "O", qk + "S"])
                yield
                S.op("dve", lambda e: e.scalar_tensor_tensor(Sf[:, h, :], Sf[:, h, :], egs[par][:, 12 + h:13 + h], Qp[:, 3, :], ALU.mult, ALU.add),
                     reads=[f"Sf{h}", f"egs2{par}", qk + "S", qk], writes=[f"Sf{h}"])
                if dirn == 0:
                    S.op("dve", lambda e: e.tensor_copy(of[par][:, h, :], Qp[:, 2, :]), reads=[qk + "O", qk], writes=[f"of{sfx}"])
                else:
                    S.op("dve", lambda e: e.tensor_tensor(of[par][:, h, :], Qp[:, 2, :], ofl[par][:, h, :], ALU.add),
                         reads=[qk + "O", qk, f"ofl{par}"], writes=[f"of{sfx}"])
                yield
                S.op("act", lambda e: e.copy(Sb[:, h, :], Sf[:, h, :]), reads=[f"Sf{h}"], writes=[f"Sb{h}"])
                yield

            def chain(*gs):
                for g_ in gs:
                    yield from g_

            def part3(dirn, b, par):
                ofk = [f"of{par}_{h}" for h in range(6)]
                if dirn == 0:
                    S.dma("sp", OF[b], of[par][:].rearrange("p h d -> p (h d)"), reads=ofk, writes=["OF"])
                    return
                o_ = of[par]
                S.op("pool", lambda e: e.tensor_tensor(osq[:], o_[:], o_[:], ALU.mult), reads=ofk, writes=["osq"])
                S.op("dve", lambda e: e.tensor_reduce(rn[:, 0:6], osq[:], mybir.AxisListType.X, ALU.add), reads=["osq"], writes=["rn0"])
                S.op("act", lambda e: e.activation(rn[:, 6:12], rn[:, 0:6], AF.Sqrt, bias=epsc[:, 0:1], scale=1.0 / 128), reads=["rn0", "epsc"], writes=["rn1"])
                S.op("dve", lambda e: e.reciprocal(rn[:, 6:12], rn[:, 6:12]), reads=["rn1"], writes=["rn1"])
                for h in range(6):
                    S.op("pool", lambda e, h=h: e.tensor_scalar(osq[:, h, :], o_[:, h, :], rn[:, 6 + h:7 + h], None, ALU.mult),
                         reads=ofk + ["rn1"], writes=["osq"])
                S.op("pool", lambda e: e.tensor_tensor(onf[:], osq[:], dnwB[:, l, :].unsqueeze(1).broadcast_to([128, 6, 128]), ALU.mult),
                     reads=["osq", "dnwB"], writes=["onf"])

                def trO(e):
                    r = None
                    for h in range(3):
                        r = e.transpose(QA[:, h, :], onf[:, h, :], identf[:])
                    for h in range(3):
                        r = e.transpose(QB[:, h, :], onf[:, 3 + h, :], identf[:])
                    return r
                S.op("pe", trO, reads=["onf", "identf", "QA", "QB"], writes=["QAW", "QAV", "QAO", "QBW", "QBV", "QBO"])
                S.op("dve", lambda e: e.tensor_tensor(ydn[:, 0:3, :], QA[:, 0:3, :], zsd[par][:, 0:3, :], ALU.mult),
                     reads=["QAW", "QAV", "QAO", "QA", f"zsd{par}"], writes=["ydn"])
                S.op("dve", lambda e: e.tensor_tensor(ydn[:, 3:6, :], QB[:, 0:3, :], zsd[par][:, 3:6, :], ALU.mult),
                     reads=["QBW", "QBV", "QBO", "QB", f"zsd{par}"], writes=["ydn"])
                S.dma("sp", YDN[b], ydn[:].rearrange("p h t -> p (h t)"), reads=["ydn"], writes=["YDN"])

            rrcnt = [0]

            def rr(gens):
                gens = list(gens)
                while gens:
                    alive = []
                    for gn in gens:
                        if BCUT and rrcnt[0] >= BCUT:
                            return
                        rrcnt[0] += 1
                        try:
                            next(gn)
                            alive.append(gn)
                        except StopIteration:
                            pass
                    gens = alive

            for dirn in range(2):
                S.op("pool", lambda e: e.memset(Sf[:], 0.0), writes=[f"Sf{h}" for h in range(6)])
                S.op("pool", lambda e: e.memset(Sb[:], 0.0), writes=[f"Sb{h}" for h in range(6)])
                blocks = list(range(NB)) if dirn == 0 else list(range(NB - 1, -1, -1))
                for i, b in enumerate(blocks):
                    par = i % 2
                    part0(dirn, b, par)
                    gl = []
                    if i > 0:
                        pb, pp = blocks[i - 1], 1 - par
                        gl.append(chain(*[part2(dirn, pb, pp, h) for h in (0, 2, 4)]))
                        gl.append(chain(*[part2(dirn, pb, pp, h) for h in (1, 3, 5)]))
                    for h in range(6):
                        gl.append(part1(dirn, b, par, h))
                    rr(gl)
                    if i > 0:
                        part3(dirn, blocks[i - 1], 1 - par)
                lp = (len(blocks) - 1) % 2
                rr([chain(*[part2(dirn, blocks[-1], lp, h) for h in (0, 2, 4)]),
                    chain(*[part2(dirn, blocks[-1], lp, h) for h in (1, 3, 5)])])
                part3(dirn, blocks[-1], lp)

        if stop == 'B':
            S.close()
            return nc
        with S.phase(last=(l == L - 1)):
            def T2(name, shape, dt):
                return [S.sb(f"{name}{i}", shape, dt) for i in range(2)]
            wout = S.sb("wout", [128, KC, D], BF16)
            hc2 = T2("hc", [128, D], F32)
            hn2 = T2("hn", [128, D], F32)
            yo = S.sb("yo", [128, D], F32)
            junk2 = S.sb("junk2", [128, D], BF16)
            ss2 = S.sb("ss2", [128, 2], F32)
            yT2 = T2("yT", [128, 16, 128], BF16)
            uc2 = T2("uc", [128, 4, 130], F32)
            cbt2 = T2("cbt", [128, 4, 128], F32)
            cvc2 = T2("cvc", [128, 4, 128], F32)
            zst2 = T2("zst", [128, 10, 128], BF16)
            qt2 = T2("qt", [128, 6, 128], BF16)
            ktl2 = T2("ktl", [128, 2, 384], BF16)
            vl2 = T2("vl", [128, 3, 256], BF16)
            Ea2 = T2("Ea", [128, 384], F32)
            PTa2 = T2("PTa", [128, 384], BF16)
            den = S.sb("den", [128, 384], F32)
            oa = S.sb("oa", [128, 384], F32)
            pst2 = [S.ps(f"pst{i}", [128, 512], F32)[:, 0:384] for i in range(2)]
            pot = [S.ps(f"pot{g}", [128, 512], F32)[:, 0:384] for g in range(2)]
            pden = [S.ps(f"pden{g}", [128, 512], F32)[:, 0:384] for g in range(2)]
            po = [S.ps(f"po{i}", [128, 512], F32) for i in range(2)]

            S.dma("sp", wout[:], Wb_out[l].rearrange("(kc p) c -> p kc c", p=128), reads=[f"Wb_out{l}"], writes=["wout"])
            scnt = [0]

            def att_gen(b):
                par = b % 2
                P = str(par)
                hc, hn, yT, uc, cbt, cvc, zst, qt, ktl, vl = (hc2[par], hn2[par], yT2[par], uc2[par], cbt2[par], cvc2[par],
                                                              zst2[par], qt2[par], ktl2[par], vl2[par])
                t0 = b * 128
                lo = 1 if b == 0 else 0
                hi = 129 if b == NB - 1 else 130
                if lo == 1:
                    S.op("pool", lambda e: e.memset(uc[:, :, 0:1], 0.0), writes=["uc" + P])
                if hi == 129:
                    S.op("pool", lambda e: e.memset(uc[:, :, 129:130], 0.0), writes=["uc" + P])
                S.dma("sp", uc[:, :, lo:hi], UC[:, t0 - 1 + lo:t0 - 1 + hi].rearrange("(c p) t -> p c t", p=128),
                      reads=["UC"], writes=["uc" + P])
                S.dma("sp", cbt[:], CB[:, t0:t0 + 128].rearrange("(c p) t -> p c t", p=128), reads=["CB"], writes=["cbt" + P])
                S.dma("sp", zst[:], ZS[0:1280, t0:t0 + 128].rearrange("(c p) t -> p c t", p=128), reads=["ZS"], writes=["zst" + P])
                S.dma("sp", hc[:], Hin[t0:t0 + 128, :], reads=[hin_key], writes=["hc" + P])
                kbs = [kb for kb in (b - 1, b, b + 1) if 0 <= kb < NB]
                k0, k1 = kbs[0], kbs[-1]
                S.dma("sp", qt[:], QT[:, t0:t0 + 128].rearrange("(h p) t -> p h t", p=128), reads=["QT"], writes=["qt" + P])
                S.dma("sp", ktl[:, :, 0:(k1 - k0 + 1) * 128], KT[:, k0 * 128:(k1 + 1) * 128].rearrange("(g p) t -> p g t", p=128),
                      reads=["KT"], writes=["ktl" + P])
                S.dma("sp", vl[:, 0:(k1 - k0 + 1), :], VV[k0 * 128:(k1 + 1) * 128, :].rearrange("(j p) c -> p j c", p=128),
                      reads=["VV"], writes=["vl" + P])
                S.dma("sp", yT[:, 10:16, :].rearrange("p h t -> p (h t)"), YDN[b], reads=["YDN"], writes=["yT_d" + P])
                for c in range(4):
                    ck = f"cvc{P}_{c}"
                    S.op("act", lambda e, c=c: e.activation(cvc[:, c, :], uc[:, c, 1:129], AF.Copy, scale=caw[:, l, c, 1:2]),
                         reads=["uc" + P, "caw"], writes=[ck])
                    S.op("dve", lambda e, c=c: e.scalar_tensor_tensor(cvc[:, c, :], uc[:, c, 0:128], caw[:, l, c, 0:1], cvc[:, c, :], ALU.mult, ALU.add),
                         reads=["uc" + P, ck], writes=[ck])
                    S.op("dve", lambda e, c=c: e.scalar_tensor_tensor(cvc[:, c, :], uc[:, c, 2:130], caw[:, l, c, 2:3], cvc[:, c, :], ALU.mult, ALU.add),
                         reads=["uc" + P, ck], writes=[ck])
                cvck = [f"cvc{P}_{c}" for c in range(4)]
                S.op("pool", lambda e: e.tensor_tensor(cvc[:], cvc[:], cbt[:], ALU.mult), reads=cvck + ["cbt" + P], writes=cvck)
                S.op("pool", lambda e: e.tensor_tensor(yT[:, 0:4, :], cvc[:], zst[:, 0:4, :], ALU.mult), reads=cvck + ["zst" + P], writes=["yT_c" + P])
                yield
                for g in range(2):
                    for ji, kb in enumerate(kbs):
                        off = kb - b + 1
                        j = kb - k0
                        first = (ji == 0)
                        lastk = (ji == len(kbs) - 1)
                        si = scnt[0] % 2
                        scnt[0] += 1
                        pst, Ea, PTa = pst2[si], Ea2[si], PTa2[si]
                        S.op("pe", lambda e, g=g, j=j, pst=pst: e.matmul(pst, ktl[:, g, j * 128:(j + 1) * 128], qt[:, 3 * g:3 * g + 3, :], start=True, stop=True),
                             reads=["ktl" + P, "qt" + P], writes=[f"pst{si}"])
                        yield
                        S.op("dve", lambda e, g=g, off=off, pst=pst, Ea=Ea: e.tensor_tensor(Ea[:], pst, AL[:, off, g * 384:(g + 1) * 384], ALU.add),
                             reads=[f"pst{si}", "AL"], writes=[f"Ea{si}"])
                        S.op("act", lambda e, kb=kb, Ea=Ea, PTa=PTa: e.activation(PTa[:], Ea[:], AF.Exp, bias=kbias[:, kb:kb + 1]),
                             reads=[f"Ea{si}", "kbias"], writes=[f"PTa{si}"])
                        yield

                        def mpv(e, g=g, j=j, first=first, lastk=lastk, PTa=PTa):
                            e.matmul(pot[g], vl[:, j, g * 128:(g + 1) * 128], PTa[:], start=first, stop=lastk)
                            return e.matmul(pden[g], onesb[:], PTa[:], start=first, stop=lastk)
                        S.op("pe", mpv, reads=["vl" + P, f"PTa{si}", "onesb"], writes=[f"pot{g}", f"pden{g}"])
                        yield
                    for hh in range(3):
                        h = 3 * g + hh
                        S.op("dve", lambda e, g=g, hh=hh, h=h: e.tensor_scalar(den[:, hh * 128:(hh + 1) * 128], pden[g][:, hh * 128:(hh + 1) * 128], esink[:, l, h:h + 1], None, ALU.add),
                             reads=[f"pden{g}", "esink"], writes=["den"])
                    S.op("dve", lambda e: e.reciprocal(den[:], den[:]), reads=["den"], writes=["den"])
                    S.op("dve", lambda e, g=g: e.tensor_tensor(oa[:], pot[g], den[:], ALU.mult), reads=[f"pot{g}", "den"], writes=["oa"])
                    S.op("pool", lambda e, g=g: e.tensor_tensor(yT[:, 4 + 3 * g:7 + 3 * g, :], oa[:].rearrange("p (h t) -> p h t", h=3), zst[:, 4 + 3 * g:7 + 3 * g, :], ALU.mult),
                         reads=["oa", "zst" + P], writes=[f"yT_a{g}" + P])
                    yield

            def proj_gen(b):
                par = b % 2
                P = str(par)
                hc, hn, yT = hc2[par], hn2[par], yT2[par]
                t0 = b * 128
                ykeys = ["yT_c" + P, "yT_a0" + P, "yT_a1" + P, "yT_d" + P]
                for n4 in range(4):
                    p = po[n4 % 2]
                    pk = f"po{n4 % 2}"

                    def mo(e, p=p, n4=n4):
                        r = None
                        for mc in range(16):
                            r = e.matmul(p[:], yT[:, mc, :], wout[:, mc, n4 * 512:(n4 + 1) * 512], start=(mc == 0), stop=(mc == 15))
                        return r
                    S.op("pe", mo, reads=ykeys + ["wout"], writes=[pk])
                    yield
                    S.op("dve", lambda e, p=p, n4=n4: e.scalar_tensor_tensor(hn[:, n4 * 512:(n4 + 1) * 512], p[:], validT[:, b:b + 1], hc[:, n4 * 512:(n4 + 1) * 512], ALU.mult, ALU.add),
                         reads=[pk, "validT", "hc" + P], writes=[f"hn{P}_{n4}"])
                    yield
                hnk = [f"hn{P}_{n4}" for n4 in range(4)]
                if l < L - 1:
                    S.dma("pool", H[t0:t0 + 128, :], hn[:], reads=hnk, writes=["H"])
                elif 1 <= b <= NBOUT:
                    S.op("act", lambda e: e.activation(junk2[:], hn[:], AF.Square, accum_out=ss2[:, 0:1]), reads=hnk, writes=["junk2", "ss2a"])
                    S.op("act", lambda e: e.activation(ss2[:, 1:2], ss2[:, 0:1], AF.Sqrt, bias=epsc[:, 0:1], scale=1.0 / D), reads=["ss2a", "epsc"], writes=["ss2b"])
                    S.op("dve", lambda e: e.reciprocal(ss2[:, 1:2], ss2[:, 1:2]), reads=["ss2b"], writes=["ss2b"])
                    S.op("dve", lambda e: e.scalar_tensor_tensor(yo[:], hn[:], ss2[:, 1:2], fnwB[:], ALU.mult, ALU.mult),
                         reads=hnk + ["ss2b", "fnwB"], writes=["yo"])
                    S.dma("pool", out_d[(b - 1) * 128:b * 128, :], yo[:], reads=["yo"], writes=["out"], final=True)

            def rr2(gens):
                gens = [g_ for g_ in gens if g_ is not None]
                while gens:
                    alive = []
                    for gn in gens:
                        try:
                            next(gn)
                            alive.append(gn)
                        except StopIteration:
                            pass
                    gens = alive

            rr2([att_gen(0)])
            for b in range(NB):
                rr2([proj_gen(b), att_gen(b + 1) if b + 1 < NB else None])
    S.close()
    return nc


def make_consts():
    import ml_dtypes
    i = np.arange(128)
    c = {}
    c["c_identb"] = np.eye(128, dtype=np.float32)
    U = np.zeros((128, 2, 128), np.float32)
    U[:, 0, :] = (i[:, None] <= i[None, :])
    U[:, 1, :] = (i[:, None] >= i[None, :])
    c["c_U"] = U
    mB = np.zeros((128, 2, 128), np.float32)
    mB[:, 0, :] = np.where(i[None, :] < i[:, None], NEG, 0.0)
    mB[:, 1, :] = np.where(i[None, :] > i[:, None], NEG, 0.0)
    c["c_maskB"] = mB
    st = np.zeros((128, 2, 128), np.float32)
    st[:, 0, :] = (i[None, :] > i[:, None])
    st[:, 1, :] = (i[None, :] < i[:, None])
    c["c_strict"] = st
    slopes = np.exp2(-8.0 * np.arange(1, 7, dtype=np.float32) / 6).astype(np.float32)
    AL = np.zeros((128, 3, 6, 128), np.float32)
    s = i[:, None].astype(np.float32)
    q = i[None, :].astype(np.float32)
    for off in range(3):
        dist = np.abs(q - (s + (off - 1) * 128))
        for h in range(6):
            AL[:, off, h, :] = np.where(dist <= 128, -slopes[h] * dist, NEG)
    c["c_AL"] = AL.reshape(128, 3, 768)
    return c


def prep_weights(norm_w, w_in, conv_a_w, attn_sink, dn_conv_w, dn_a_log, dn_dt_bias, dn_norm_w, w_out, final_norm_w):
    L = w_in.shape[0]
    rep = lambda a: np.ascontiguousarray(np.broadcast_to(a[None], (128,) + a.shape)).astype(np.float32)
    m = {}
    m["w_in"] = np.ascontiguousarray(w_in, dtype=np.float32)
    m["w_out"] = np.ascontiguousarray(w_out, dtype=np.float32)
    m["nwB"] = rep(np.asarray(norm_w, np.float32))
    m["fnwB"] = rep(np.asarray(final_norm_w, np.float32))
    m["caw"] = np.ascontiguousarray(np.asarray(conv_a_w, np.float32).reshape(L, 3, 4, 128).transpose(3, 0, 2, 1))
    m["dcw"] = np.ascontiguousarray(np.asarray(dn_conv_w, np.float32).reshape(L, 3, 18, 128).transpose(3, 0, 2, 1))
    m["sinkB"] = rep(np.asarray(attn_sink, np.float32))
    m["alogB"] = rep(np.asarray(dn_a_log, np.float32).reshape(L, 12))
    m["dtbB"] = rep(np.asarray(dn_dt_bias, np.float32).reshape(L, 12))
    m["dnwB"] = rep(np.asarray(dn_norm_w, np.float32))
    return m


def core_inputs(x_seq, meta_tokens, NB):
    T = NB * 128
    h0 = np.zeros((T, D), np.float32)
    valid = np.zeros((T,), np.float32)
    if x_seq is not None:
        Sx = x_seq.shape[0]
        h0[PAD:LEAD] = meta_tokens
        h0[LEAD:LEAD + Sx] = x_seq
        valid[PAD:LEAD + Sx] = 1.0
    return {
        "h0": h0,
        "validT": np.ascontiguousarray(valid.reshape(NB, 128).T),
        "validB": np.ascontiguousarray(np.broadcast_to(valid[None], (128, T))),
    }


def kernel(x_prompt, x_sample, meta_tokens, norm_w, w_in, conv_a_w, attn_sink, dn_conv_w,
           dn_a_log, dn_dt_bias, dn_norm_w, w_out, final_norm_w):
    x_prompt = np.asarray(x_prompt, np.float32)
    x_sample = np.asarray(x_sample, np.float32)
    meta_tokens = np.asarray(meta_tokens, np.float32)
    L = w_in.shape[0]
    Sp = x_prompt.shape[1]
    Ss = x_sample.shape[1]
    NB = (LEAD + max(Sp, Ss)) // 128
    NBOUT = NB - 1
    shared = prep_weights(norm_w, np.asarray(w_in), conv_a_w, attn_sink, dn_conv_w, dn_a_log, dn_dt_bias,
                          dn_norm_w, np.asarray(w_out), final_norm_w)
    shared.update(make_consts())
    seqs = [x_prompt[i] for i in range(x_prompt.shape[0])] + [x_sample[i] for i in range(x_sample.shape[0])]
    assert len(seqs) <= 8
    in_maps = []
    for c in range(8):
        m = dict(shared)
        m.update(core_inputs(seqs[c] if c < len(seqs) else None, meta_tokens, NB))
        in_maps.append(m)
    nc = build_nc(NB, L, NBOUT)
    res = run_bass_kernel_spmd(nc, in_maps, core_ids=list(range(8)))
    outs = [np.asarray(r["out"]) for r in res.results]
    nP = x_prompt.shape[0]
    y_prompt = np.stack([outs[i][:Sp] for i in range(nP)], axis=0).astype(np.float32)
    y_sample = np.stack([outs[nP + i][:Ss] for i in range(x_sample.shape[0])], axis=0).astype(np.float32)
    return (y_prompt, y_sample)
```

```python
import contextlib
import numpy as np
import concourse.bass as bass
import concourse.mybir as mybir
from concourse.bass_utils import run_bass_kernel_spmd

F32 = mybir.dt.float32
BF16 = mybir.dt.bfloat16
F32R = mybir.dt.float32r
AF = mybir.ActivationFunctionType
ALU = mybir.AluOpType

import os as _os
BCUT = int(_os.environ.get('BCUT', '0'))
CCUT = int(_os.environ.get('CCUT', '0'))
D = 2048
KC = 16
NPROJ = 7192
LEAD = 128
PAD = 112
NEG = -1.0e9
EPS = 1e-6


class Sched:
    COMPUTE = ("pe", "act", "dve", "pool")
    ALLENG = ("pe", "act", "dve", "pool", "sp")
    NDMA = {"sp": 8, "pool": 4}

    def __init__(self, nc):
        self.nc = nc
        self.gstack = contextlib.ExitStack()
        self.sems = {}
        for e in self.COMPUTE:
            self.sems["e:" + e] = self.gstack.enter_context(nc.semaphore("s_" + e))
        for q, k in self.NDMA.items():
            for j in range(k):
                self.sems[f"d:{q}:{j}"] = self.gstack.enter_context(nc.semaphore(f"d_{q}{j}"))
        self.seq = {e: 0 for e in self.COMPUTE}
        self.ndma = {q: 0 for q in self.NDMA}
        self.last_tok = {}
        self.known = {e: {} for e in self.ALLENG}
        self.last_w = {}
        self.readers = {}
        self.ops = {e: [] for e in self.ALLENG}
        self.final_tokens = []
        self.barrier_tokens = []
        self.pstack = None
        self.nops = 0
        self.bankmap = {}
        self.bank_last = {}

    def _uid(self):
        self.uid = getattr(self, "uid", 0) + 1
        return f"_{self.uid}"

    def sb(self, name, shape, dtype, glob=False):
        st = self.gstack if (glob or self.pstack is None) else self.pstack
        return st.enter_context(self.nc.sbuf_tensor("s_" + name + self._uid(), list(shape), dtype))

    def ps(self, name, shape, dtype, nbanks=1):
        nbytes = (4 if dtype == F32 else 2)
        for s_ in shape[1:]:
            nbytes *= s_
        assert nbytes == 2048 * nbanks, (name, shape, nbytes)
        self.bankmap[name] = [f"B:{name}:{i}" for i in range(nbanks)]
        return self.pstack.enter_context(self.nc.psum_tensor("p_" + name + self._uid(), list(shape), dtype))

    def _split(self, keys):
        reg, banks = [], []
        for k in keys:
            if k in self.bankmap:
                banks.extend(self.bankmap[k])
            else:
                reg.append(k)
        return reg, banks

    def _bank_deps(self, eng, banks):
        deps = []
        for bk in banks:
            for e2, tok in self.bank_last.get(bk, {}).items():
                if e2 != eng:
                    deps.append(tok)
        return deps

    def _bank_commit(self, eng, banks, token):
        for bk in banks:
            self.bank_last.setdefault(bk, {})[eng] = token

    def _deps(self, reads, writes):
        deps = list(self.barrier_tokens)
        for k in reads:
            t = self.last_w.get(k)
            if t is not None:
                deps.append(t)
        for k in writes:
            t = self.last_w.get(k)
            if t is not None:
                deps.append(t)
            deps.extend(self.readers.get(k, ()))
        return deps

    def _commit(self, token, reads, writes):
        self.last_tok[token[0]] = token[1]
        for k in reads:
            self.readers.setdefault(k, []).append(token)
        for k in writes:
            self.last_w[k] = token
            self.readers[k] = []

    def _waits(self, issuer, deps, skip_key=None):
        kn = self.known[issuer]
        need = {}
        for (key, val) in deps:
            if key == skip_key:
                continue
            if kn.get(key, 0) < val and need.get(key, 0) < val:
                need[key] = val
        for key, val in need.items():
            kn[key] = val
        return list(need.items())

    def op(self, eng, fn, reads=(), writes=()):
        reads, b1 = self._split(reads)
        writes, b2 = self._split(writes)
        banks = set(b1 + b2)
        deps = self._deps(reads, writes) + self._bank_deps(eng, banks)
        key = "e:" + eng
        waits = self._waits(eng, deps, skip_key=key if eng == "pe" else None)
        self.seq[eng] += 1
        token = (key, self.seq[eng])
        self.ops[eng].append((waits, fn, key, 1))
        self._commit(token, reads, writes)
        self._bank_commit(eng, banks, token)
        self.nops += 1
        return token

    def dma(self, q, out, in_, reads=(), writes=(), final=False):
        n = self.ndma[q]
        self.ndma[q] += 1
        K = self.NDMA[q]
        key = f"d:{q}:{n % K}"
        deps = self._deps(reads, writes)
        if n // K > 0:
            deps.append((key, 16 * (n // K)))
        waits = self._waits(q, deps)
        token = (key, 16 * (n // K + 1))
        self.ops[q].append((waits, lambda e: e.dma_start(out=out, in_=in_), key, 16))
        self._commit(token, reads, writes)
        if final:
            self.final_tokens.append(token)
        self.nops += 1
        return token

    def barrier(self):
        self.barrier_tokens = list(self.last_tok.items())

    @contextlib.contextmanager
    def phase(self, last=False):
        self.pstack = contextlib.ExitStack()
        self.barrier()
        yield self
        self._emit(last)
        self.pstack.close()
        self.pstack = None

    def _emit(self, last):
        nc = self.nc
        sems = self.sems
        fin = []
        if last:
            fin = self._waits("sp", self.final_tokens)
        ops = self.ops

        def run(engine, lst, tail=()):
            for (waits, fn, key, amt) in lst:
                for (k, v) in waits:
                    engine.wait_ge(sems[k], v)
                inst = fn(engine)
                inst.then_inc(sems[key], amt)
            for (k, v) in tail:
                engine.wait_ge(sems[k], v)

        with nc.Block() as block:
            @block.sync
            def _(e):
                run(e, ops["sp"], fin)

            @block.tensor
            def _(e):
                run(e, ops["pe"])

            @block.scalar
            def _(e):
                run(e, ops["act"])

            @block.vector
            def _(e):
                run(e, ops["dve"])

            @block.gpsimd
            def _(e):
                run(e, ops["pool"])
        self.ops = {e: [] for e in self.ALLENG}

    def close(self):
        self.gstack.close()


def build_nc(NB, L, NBOUT, G=4, stop=None):
    T = NB * 128
    nc = bass.Bass("TRN2", target_bir_lowering=False)
    dt_in = lambda n, s, d=F32: nc.dram_tensor(n, list(s), d, kind="ExternalInput").ap()
    dt_sc = lambda n, s, d=F32: nc.dram_tensor(n, list(s), d, kind="Internal").ap()

    h0 = dt_in("h0", [T, D])
    validT_d = dt_in("validT", [128, NB])
    validB_d = dt_in("validB", [128, T])
    w_in_d = dt_in("w_in", [L, D, NPROJ])
    w_out_d = dt_in("w_out", [L, D, D])
    nwB_d = dt_in("nwB", [128, L, D])
    fnwB_d = dt_in("fnwB", [128, D])
    caw_d = dt_in("caw", [128, L, 4, 3])
    dcw_d = dt_in("dcw", [128, L, 18, 3])
    sinkB_d = dt_in("sinkB", [128, L, 6])
    alogB_d = dt_in("alogB", [128, L, 12])
    dtbB_d = dt_in("dtbB", [128, L, 12])
    dnwB_d = dt_in("dnwB", [128, L, 128])
    c_identb_d = dt_in("c_identb", [128, 128])
    c_U_d = dt_in("c_U", [128, 2, 128])
    c_maskB_d = dt_in("c_maskB", [128, 2, 128])
    c_strict_d = dt_in("c_strict", [128, 2, 128])
    c_AL_d = dt_in("c_AL", [128, 3, 768])
    out_d = nc.dram_tensor("out", [NBOUT * 128, D], F32, kind="ExternalOutput").ap()

    Wb_in = dt_sc("Wb_in", [L, D, NPROJ], BF16)
    Wb_out = dt_sc("Wb_out", [L, D, D], BF16)
    H = dt_sc("H", [T, D])
    ZS = dt_sc("ZS", [D, T], BF16)
    UC = dt_sc("UC", [512, T])
    CB = dt_sc("CB", [512, T])
    QT = dt_sc("QT", [768, T], BF16)
    KT = dt_sc("KT", [256, T], BF16)
    VV = dt_sc("VV", [T, 256], BF16)
    DQKV = dt_sc("DQKV", [2304, T])
    BG = dt_sc("BG", [T, 24])
    KQT = dt_sc("KQT", [NB, 128, 6 * 2 * 128], BF16)
    KN = dt_sc("KN", [NB, 128, 768], BF16)
    VN = dt_sc("VN", [NB, 128, 768], BF16)
    OF = dt_sc("OF", [NB, 128, 768])
    YDN = dt_sc("YDN", [NB, 128, 768], BF16)

    S = Sched(nc)
    identb = S.sb("identb", [128, 128], BF16)
    identf = S.sb("identf", [128, 128], F32)
    Uf = S.sb("Uf", [128, 2, 128], F32)
    maskB = S.sb("maskB", [128, 2, 128], F32)
    strictM = S.sb("strictM", [128, 2, 128], F32)
    AL = S.sb("AL", [128, 3, 768], F32)
    onesf = S.sb("onesf", [128, 128], F32)
    onesb = S.sb("onesb", [128, 128], BF16)
    validT = S.sb("validT", [128, NB], F32)
    kbias = S.sb("kbias", [128, NB], F32)
    caw = S.sb("caw", [128, L, 4, 3], F32)
    dcw = S.sb("dcw", [128, L, 18, 3], F32)
    sinkB = S.sb("sinkB", [128, L, 6], F32)
    esink = S.sb("esink", [128, L, 6], F32)
    alogB = S.sb("alogB", [128, L, 12], F32)
    negA = S.sb("negA", [128, L, 12], F32)
    dtbB = S.sb("dtbB", [128, L, 12], F32)
    dnwB = S.sb("dnwB", [128, L, 128], F32)
    fnwB = S.sb("fnwB", [128, D], F32)
    nw = S.sb("nw", [128, D], F32)
    betaT = S.sb("betaT", [128, NB, 12], F32)
    nbetaT = S.sb("nbetaT", [128, NB, 12], F32)
    gT = S.sb("gT", [128, NB, 12], F32)
    epsc = S.sb("epsc", [128, 1], F32)
    onec = S.sb("onec", [128, 1], F32)

    with S.phase():
        S.dma("pool", identb[:], c_identb_d, writes=["identb"])
        S.dma("sp", identf[:], c_identb_d, writes=["identf"])
        S.dma("sp", Uf[:], c_U_d, writes=["Uf"])
        S.dma("sp", maskB[:], c_maskB_d, writes=["maskB"])
        S.dma("sp", strictM[:], c_strict_d, writes=["strictM"])
        S.dma("sp", AL[:], c_AL_d, writes=["AL"])
        S.dma("sp", validT[:], validT_d, writes=["validT"])
        S.dma("sp", caw[:], caw_d, writes=["caw"])
        S.dma("sp", dcw[:], dcw_d, writes=["dcw"])
        S.dma("sp", sinkB[:], sinkB_d, writes=["sinkB"])
        S.dma("sp", alogB[:], alogB_d, writes=["alogB"])
        S.dma("sp", dtbB[:], dtbB_d, writes=["dtbB"])
        S.dma("sp", dnwB[:], dnwB_d, writes=["dnwB"])
        S.dma("sp", fnwB[:], fnwB_d, writes=["fnwB"])
        S.op("pool", lambda e: e.memset(onesf[:], 1.0), writes=["onesf"])
        S.op("pool", lambda e: e.memset(onesb[:], 1.0), writes=["onesb"])
        S.op("pool", lambda e: e.memset(epsc[:], EPS), writes=["epsc"])
        S.op("pool", lambda e: e.memset(onec[:], 1.0), writes=["onec"])
        S.op("dve", lambda e: e.tensor_scalar(kbias[:], validT[:], -1.0, -NEG, ALU.add, ALU.mult),
             reads=["validT"], writes=["kbias"])
        S.op("act", lambda e: e.activation(esink[:], sinkB[:], AF.Exp), reads=["sinkB"], writes=["esink"])
        S.op("act", lambda e: e.activation(negA[:], alogB[:], AF.Exp), reads=["alogB"], writes=["negA"])
        S.op("dve", lambda e: e.tensor_scalar(negA[:], negA[:], -1.0, None, ALU.mult),
             reads=["negA"], writes=["negA"])
        def convert_weights(lw):
            for kc in range(KC):
                S.dma("pool", Wb_in[lw, kc * 128:(kc + 1) * 128, :], w_in_d[lw, kc * 128:(kc + 1) * 128, :],
                      writes=[f"Wb_in{lw}"])
            for kc in range(0, KC, 4):
                S.dma("pool", Wb_out[lw, kc * 128:(kc + 4) * 128, :], w_out_d[lw, kc * 128:(kc + 4) * 128, :],
                      writes=[f"Wb_out{lw}"])
        convert_weights(0)

    if stop == '0':
        S.close()
        return nc
    for l in range(L):
        Hin = h0 if l == 0 else H
        hin_key = "h0" if l == 0 else "H"
        with S.phase():
            N = G * 128
            ht = [S.sb(f"ht{i}", [128, D], F32) for i in range(2)]
            junk = S.sb("junk", [128, D], BF16)
            xn = S.sb("xn", [128, D], BF16)
            xnT2 = [S.sb(f"xnT{i}", [128, KC, N], BF16) for i in range(2)]
            wt = [S.sb(f"wt{i}", [128, KC, 512], BF16) for i in range(2)]
            wlast = S.sb("wlast", [128, KC, 24], BF16)
            cxs = S.sb("cxs", [128, 4, N], F32)
            ef = [S.sb(f"ef{i}", [128, N], F32) for i in range(3)]
            eb = [S.sb(f"eb{i}", [128, N], BF16) for i in range(3)]
            vtok = S.sb("vtok", [128, 256], BF16)
            bgt = S.sb("bgt", [128, 24], F32)
            ss = S.sb("ss", [128, 2], F32)
            ptr = [S.ps(f"ptr{i}", [128, 8, 128], BF16) for i in range(2)]
            pa = [S.ps(f"pa{i}", [128, 512], F32) for i in range(2)]
            pv_ = S.ps("pv", [128, 512], F32)
            pv = pv_[:, 0:256]
            pbg_ = S.ps("pbg", [128, 512], F32)
            pbg = pbg_[:, 0:24]

            S.dma("sp", nw[:], nwB_d[:, l, :], writes=["nw"])
            S.dma("sp", wlast[:], Wb_in[l, :, 7168:7192].rearrange("(kc p) c -> p kc c", p=128),
                  reads=[f"Wb_in{l}"], writes=["wlast"])
            ngroups = (NB + G - 1) // G
            ecnt = [0]
            ucnt = [0]

            def ukey(name):
                ucnt[0] += 1
                return f"{name}#{l}#{ucnt[0]}"

            def norm_gen(gi):
                gp = gi % 2
                xnT = xnT2[gp]
                b0 = gi * G
                gb = min(G, NB - b0)
                for bi in range(gb):
                    b = b0 + bi
                    hh = ht[b % 2]
                    hk = f"ht{b % 2}"
                    S.dma("sp", hh[:], Hin[b * 128:(b + 1) * 128, :], reads=[hin_key], writes=[hk])
                    S.op("act", lambda e, hh=hh: e.activation(junk[:], hh[:], AF.Square, accum_out=ss[:, 0:1]),
                         reads=[hk], writes=["junk", "ss0"])
                    S.op("act", lambda e: e.activation(ss[:, 1:2], ss[:, 0:1], AF.Sqrt, bias=epsc[:, 0:1], scale=1.0 / D),
                         reads=["ss0", "epsc"], writes=["ss1"])
                    S.op("dve", lambda e: e.reciprocal(ss[:, 1:2], ss[:, 1:2]),
                         reads=["ss1"], writes=["ss1"])
                    S.op("dve", lambda e, hh=hh: e.scalar_tensor_tensor(xn[:], hh[:], ss[:, 1:2], nw[:], ALU.mult, ALU.mult),
                         reads=[hk, "ss1", "nw"], writes=["xn"])
                    yield
                    for q4 in range(4):
                        pt = ptr[q4 % 2]
                        pk = f"ptr{q4 % 2}"

                        def tr(e, pt=pt, q4=q4):
                            r = None
                            for j in range(4):
                                kc = q4 * 4 + j
                                r = e.transpose(pt[:, j, :], xn[:, kc * 128:(kc + 1) * 128], identb[:])
                            return r
                        S.op("pe", tr, reads=["xn", "identb"], writes=[pk])
                        if q4 % 2 == 0:
                            S.op("act", lambda e, pt=pt, q4=q4, bi=bi: e.copy(xnT[:, q4 * 4:(q4 + 1) * 4, bi * 128:(bi + 1) * 128], pt[:, 0:4, :]),
                                 reads=[pk], writes=[f"xnT{gp}_{bi}"])
                        else:
                            S.op("dve", lambda e, pt=pt, q4=q4, bi=bi: e.tensor_copy(xnT[:, q4 * 4:(q4 + 1) * 4, bi * 128:(bi + 1) * 128], pt[:, 0:4, :]),
                                 reads=[pk], writes=[f"xnT{gp}_{bi}"])
                        if q4 % 2 == 1:
                            yield

            def drain(gen, n=None):
                if gen is None:
                    return
                k = 0
                for _ in gen:
                    k += 1
                    if n is not None and k >= n:
                        return

            drain(norm_gen(0))
            for gi in range(ngroups):
                gp = gi % 2
                xnT = xnT2[gp]
                b0 = gi * G
                gb = min(G, NB - b0)
                n = gb * 128
                nxt = norm_gen(gi + 1) if gi + 1 < ngroups else None
                xk = [f"xnT{gp}_{bi}" for bi in range(gb)]
                for u in range(14):
                    w = wt[u % 2]
                    wk = f"wt{u % 2}"
                    S.dma("sp", w[:], Wb_in[l, :, u * 512:(u + 1) * 512].rearrange("(kc p) c -> p kc c", p=128),
                          reads=[f"Wb_in{l}"], writes=[wk])
                    for c4 in range(4):
                        col = u * 512 + c4 * 128
                        ch = col // 128
                        if 3072 <= col < 3328:
                            continue
                        p = pa[ecnt[0] % 2]
                        pk = f"pa{ecnt[0] % 2}"
                        ecnt[0] += 1

                        def mm(e, p=p, w=w, c4=c4, n=n, xnT=xnT):
                            r = None
                            for kc in range(KC):
                                r = e.matmul(p[:, 0:n], w[:, kc, c4 * 128:(c4 + 1) * 128], xnT[:, kc, 0:n],
                                             start=(kc == 0), stop=(kc == KC - 1))
                            return r
                        S.op("pe", mm, reads=[wk] + xk, writes=[pk])
                        tcols = slice(b0 * 128, b0 * 128 + n)
                        i3 = ch % 3
                        if col < 512:
                            S.op("act", lambda e, p=p, ch=ch, n=n: e.copy(cxs[:, ch, 0:n], p[:, 0:n]),
                                 reads=[pk], writes=[f"cxs{ch}"])
                        elif col < 1024:
                            S.op("act", lambda e, p=p, i3=i3, n=n: e.copy(ef[i3][:, 0:n], p[:, 0:n]),
                                 reads=[pk], writes=[f"ef{i3}"])
                            S.dma("pool", CB[col - 512:col - 512 + 128, tcols], ef[i3][:, 0:n], reads=[f"ef{i3}"], writes=[ukey("CB")])
                        elif col < 1536:
                            cxi = (col - 1024) // 128
                            S.op("dve", lambda e, p=p, i3=i3, n=n, cxi=cxi: e.tensor_tensor(ef[i3][:, 0:n], p[:, 0:n], cxs[:, cxi, 0:n], ALU.mult),
                                 reads=[pk, f"cxs{cxi}"], writes=[f"ef{i3}"])
                            S.dma("pool", UC[col - 1024:col - 1024 + 128, tcols], ef[i3][:, 0:n], reads=[f"ef{i3}"], writes=[ukey("UC")])
                        elif col < 2048 or 3328 <= col < 4096 or 6400 <= col < 7168:
                            if col < 2048:
                                zr = col - 1536
                            elif col < 4096:
                                zr = 512 + col - 3328
                            else:
                                zr = 1280 + col - 6400
                            S.op("act", lambda e, p=p, i3=i3, n=n: e.activation(eb[i3][:, 0:n], p[:, 0:n], AF.Silu),
                                 reads=[pk], writes=[f"eb{i3}"])
                            S.dma("pool", ZS[zr:zr + 128, tcols], eb[i3][:, 0:n], reads=[f"eb{i3}"], writes=[ukey("ZS")])
                        elif col < 2816:
                            S.op("act", lambda e, p=p, i3=i3, n=n: e.activation(eb[i3][:, 0:n], p[:, 0:n], AF.Copy, scale=128.0 ** -0.5),
                                 reads=[pk], writes=[f"eb{i3}"])
                            S.dma("pool", QT[col - 2048:col - 2048 + 128, tcols], eb[i3][:, 0:n], reads=[f"eb{i3}"], writes=[ukey("QT")])
                        elif col < 3072:
                            S.op("dve", lambda e, p=p, i3=i3, n=n: e.tensor_copy(eb[i3][:, 0:n], p[:, 0:n]),
                                 reads=[pk], writes=[f"eb{i3}"])
                            S.dma("pool", KT[col - 2816:col - 2816 + 128, tcols], eb[i3][:, 0:n], reads=[f"eb{i3}"], writes=[ukey("KT")])
                        else:
                            r0 = col - 4096
                            S.op("dve", lambda e, p=p, i3=i3, n=n: e.tensor_copy(ef[i3][:, 0:n], p[:, 0:n]),
                                 reads=[pk], writes=[f"ef{i3}"])
                            S.dma("pool", DQKV[r0:r0 + 128, tcols], ef[i3][:, 0:n], reads=[f"ef{i3}"], writes=[ukey("DQKV")])
                    if u == 6:
                        for bi in range(gb):
                            b = b0 + bi

                            def mmv(e, w=w, bi=bi, xnT=xnT):
                                r = None
                                for kc in range(KC):
                                    r = e.matmul(pv, xnT[:, kc, bi * 128:(bi + 1) * 128], w[:, kc, 0:256],
                                                 start=(kc == 0), stop=(kc == KC - 1))
                                return r
                            S.op("pe", mmv, reads=[wk, f"xnT{gp}_{bi}"], writes=["pv"])
                            S.op("act", lambda e: e.copy(vtok[:], pv), reads=["pv"], writes=["vtok"])
                            S.dma("pool", VV[b * 128:(b + 1) * 128, :], vtok[:], reads=["vtok"], writes=[ukey("VV")])
                    if u >= 2:
                        drain(nxt, 1)
                for bi in range(gb):
                    b = b0 + bi

                    def mmb(e, bi=bi, xnT=xnT):
                        r = None
                        for kc in range(KC):
                            r = e.matmul(pbg, xnT[:, kc, bi * 128:(bi + 1) * 128], wlast[:, kc, :],
                                         start=(kc == 0), stop=(kc == KC - 1))
                        return r
                    S.op("pe", mmb, reads=["wlast", f"xnT{gp}_{bi}"], writes=["pbg"])
                    S.op("dve", lambda e: e.tensor_copy(bgt[:], pbg), reads=["pbg"], writes=["bgt"])
                    S.dma("pool", BG[b * 128:(b + 1) * 128, :], bgt[:], reads=["bgt"], writes=[ukey("BG")])
                drain(nxt)

        if stop == 'A':
            S.close()
            return nc
        with S.phase():
            def T2(name, shape, dt):
                return [S.sb(f"{name}{i}", shape, dt) for i in range(2)]
            raw = T2("raw", [128, 18, 130], F32)
            cv = T2("cv", [128, 18, 128], F32)
            sq = T2("sq", [128, 12, 128], BF16)
            rs = T2("rs", [128, 12, 128], F32)
            vB = T2("vB", [128, 128], F32)
            kq = T2("kq", [128, 6, 2, 128], BF16)
            vT = T2("vT", [128, 6, 128], BF16)
            kv_tok = T2("kv_tok", [128, 12, 128], BF16)
            bgt2 = T2("bgt2", [128, 24], F32)
            spt = T2("spt", [128, 12], F32)
            pss = S.ps("pss", [128, 12, 128], F32, nbanks=3)
            ptk_ = S.ps("ptk", [128, 16, 128], BF16, nbanks=2)

            def a2_block(b):
                par = b % 2
                P = str(par)
                raw_, cv_, sq_, rs_, vB_, kq_, vT_, kvt_, bg_, sp_ = (raw[par], cv[par], sq[par], rs[par], vB[par], kq[par],
                                                                     vT[par], kv_tok[par], bgt2[par], spt[par])
                t0 = b * 128
                lo = 1 if b == 0 else 0
                hi = 129 if b == NB - 1 else 130
                if lo == 1:
                    S.op("pool", lambda e: e.memset(raw_[:, :, 0:1], 0.0), writes=["raw" + P])
                if hi == 129:
                    S.op("pool", lambda e: e.memset(raw_[:, :, 129:130], 0.0), writes=["raw" + P])
                S.dma("sp", raw_[:, :, lo:hi],
                      DQKV[:, t0 - 1 + lo:t0 - 1 + hi].rearrange("(c p) t -> p c t", p=128),
                      reads=["DQKV"], writes=["raw" + P])
                S.dma("sp", vB_[:], validB_d[:, t0:t0 + 128], writes=["vB" + P])
                S.dma("sp", bg_[:], BG[t0:t0 + 128, :], reads=["BG"], writes=["bgt2" + P])
                S.op("act", lambda e: e.activation(betaT[:, b, :], bg_[:, 0:12], AF.Sigmoid),
                     reads=["bgt2" + P], writes=["betaT"])
                S.op("dve", lambda e: e.tensor_scalar(nbetaT[:, b, :], betaT[:, b, :], -1.0, None, ALU.mult),
                     reads=["betaT"], writes=["nbetaT"])
                S.op("dve", lambda e: e.tensor_tensor(sp_[:], bg_[:, 12:24], dtbB[:, l, :], ALU.add),
                     reads=["bgt2" + P, "dtbB"], writes=["spt" + P])
                S.op("act", lambda e: e.activation(sp_[:], sp_[:], AF.Exp), reads=["spt" + P], writes=["spt" + P])
                S.op("act", lambda e: e.activation(sp_[:], sp_[:], AF.Ln, bias=onec[:, 0:1]), reads=["spt" + P, "onec"], writes=["spt" + P])
                S.op("dve", lambda e: e.tensor_tensor(gT[:, b, :], sp_[:], negA[:, l, :], ALU.mult),
                     reads=["spt" + P, "negA"], writes=["gT"])
                for c in range(18):
                    ck = f"cv{P}_{c}"
                    S.op("act", lambda e, c=c: e.activation(cv_[:, c, :], raw_[:, c, 1:129], AF.Copy, scale=dcw[:, l, c, 1:2]),
                         reads=["raw" + P, "dcw"], writes=[ck])
                    S.op("dve", lambda e, c=c: e.scalar_tensor_tensor(cv_[:, c, :], raw_[:, c, 0:128], dcw[:, l, c, 0:1], cv_[:, c, :], ALU.mult, ALU.add),
                         reads=["raw" + P, ck], writes=[ck])
                    S.op("dve", lambda e, c=c: e.scalar_tensor_tensor(cv_[:, c, :], raw_[:, c, 2:130], dcw[:, l, c, 2:3], cv_[:, c, :], ALU.mult, ALU.add),
                         reads=["raw" + P, ck], writes=[ck])
                cvk = [f"cv{P}_{c}" for c in range(18)]
                S.op("act", lambda e: e.activation(cv_[:], cv_[:], AF.Silu), reads=cvk, writes=cvk)
                S.op("act", lambda e: e.activation(sq_[:], cv_[:, 0:12, :], AF.Square), reads=cvk, writes=["sq" + P])

                def mss(e):
                    r = None
                    for j in range(3):
                        r = e.matmul(pss[:, j * 4:(j + 1) * 4, :], onesb[:], sq_[:, j * 4:(j + 1) * 4, :], start=True, stop=True)
                    return r
                S.op("pe", mss, reads=["sq" + P, "onesb"], writes=["pss"])
                S.op("act", lambda e: e.activation(rs_[:], pss[:], AF.Ln, bias=epsc[:, 0:1], scale=1.0),
                     reads=["pss", "epsc"], writes=["rs" + P])
                S.op("act", lambda e: e.activation(rs_[:], rs_[:], AF.Exp, scale=-0.5), reads=["rs" + P], writes=["rs" + P])
                S.op("dve", lambda e: e.tensor_tensor(rs_[:, 6:12, :], rs_[:, 6:12, :], vB_[:].unsqueeze(1).broadcast_to([128, 6, 128]), ALU.mult),
                     reads=["rs" + P, "vB" + P], writes=["rs" + P])
                S.op("dve", lambda e: e.scalar_tensor_tensor(kq_[:, :, 1, :], cv_[:, 0:6, :], 128.0 ** -0.5, rs_[:, 0:6, :], ALU.mult, ALU.mult),
                     reads=cvk + ["rs" + P], writes=["kq" + P])
                S.op("dve", lambda e: e.tensor_tensor(kq_[:, :, 0, :], cv_[:, 6:12, :], rs_[:, 6:12, :], ALU.mult),
                     reads=cvk + ["rs" + P], writes=["kq" + P])
                S.op("dve", lambda e: e.tensor_copy(vT_[:], cv_[:, 12:18, :]), reads=cvk, writes=["vT" + P])

                def trk(e):
                    r = None
                    for h in range(6):
                        r = e.transpose(ptk_[:, h, :], kq_[:, h, 0, :], identb[:])
                    for h in range(6):
                        r = e.transpose(ptk_[:, 6 + h, :], vT_[:, h, :], identb[:])
                    return r
                S.op("pe", trk, reads=["kq" + P, "vT" + P, "identb"], writes=["ptk"])
                S.op("act", lambda e: e.copy(kvt_[:], ptk_[:, 0:12, :]), reads=["ptk"], writes=["kv_tok" + P])
                S.dma("sp", KQT[b], kq_[:].rearrange("p h two t -> p (h two t)"), reads=["kq" + P], writes=["KQT"])
                S.dma("sp", KN[b], kvt_[:, 0:6, :].rearrange("p h d -> p (h d)"), reads=["kv_tok" + P], writes=["KN"])
                S.dma("sp", VN[b], kvt_[:, 6:12, :].rearrange("p h d -> p (h d)"), reads=["kv_tok" + P], writes=["VN"])

            if l + 1 < L:
                convert_weights(l + 1)
            for b in range(NB):
                a2_block(b)

        if stop == 'A2':
            S.close()
            return nc
        with S.phase():
            def T2(name, shape, dt):
                return [S.sb(f"{name}{i}", shape, dt) for i in range(2)]
            kqt = T2("kqt", [128, 6, 2, 128], BF16)
            knt = T2("knt", [128, 6, 128], BF16)
            vnt = T2("vnt", [128, 6, 128], BF16)
            gcs = T2("gcs", [128, 12], F32)
            ngc = T2("ngc", [128, 6], F32)
            egs = T2("egs", [128, 18], F32)
            gU = T2("gU", [128, 6, 128], F32)
            GCs = T2("GCs", [128, 6, 128], F32)
            Et = T2("Et", [128, 6, 128], F32)
            DT = T2("DT", [128, 6, 128], F32)
            EG = T2("EG", [128, 6, 128], F32)
            tt = T2("tt", [128, 6, 128], F32)
            qg = T2("qg", [128, 6, 128], BF16)
            MT = T2("MT", [128, 6, 128], BF16)
            X = [[S.sb(f"X{p}{i}", [128, 6, 3, 128], F32R) for i in range(2)] for p in range(2)]
            identr = S.sb("identr", [128, 128], F32R)
            S.op("dve", lambda e: e.tensor_copy(identr[:], identf[:]), reads=["identf"], writes=["identr"])
            Pb = T2("Pb", [128, 6, 128], BF16)
            kg = T2("kg", [128, 6, 128], BF16)
            kt = T2("kt", [128, 6, 128], BF16)
            nWT = S.sb("nWT", [128, 6, 128], BF16)
            vnew = S.sb("vnew", [128, 6, 128], BF16)
            Sf = S.sb("Sf", [128, 6, 128], F32)
            Sb = S.sb("Sb", [128, 6, 128], BF16)
            of = T2("of", [128, 6, 128], F32)
            ofl = T2("ofl", [128, 6, 128], F32)
            osq = S.sb("osq", [128, 6, 128], F32)
            zsd = T2("zsd", [128, 6, 128], BF16)
            ydn = S.sb("ydn", [128, 6, 128], BF16)
            rn = S.sb("rn", [128, 12], F32)
            Hps = [S.ps(f"H{h}", [128, 4, 128], F32) for h in range(6)]
            QA = S.ps("QA", [128, 4, 128], F32)
            QB = S.ps("QB", [128, 4, 128], F32)
            onf = S.sb("onf", [128, 6, 128], F32)

            def part0(dirn, b, par):
                S.dma("sp", kqt[par][:].rearrange("p h two t -> p (h two t)"), KQT[b], reads=["KQT"], writes=[f"kqt{par}"])
                S.dma("sp", knt[par][:].rearrange("p h d -> p (h d)"), KN[b], reads=["KN"], writes=[f"knt{par}"])
                S.dma("sp", vnt[par][:].rearrange("p h d -> p (h d)"), VN[b], reads=["VN"], writes=[f"vnt{par}"])
                if dirn == 1:
                    S.dma("sp", ofl[par][:].rearrange("p h d -> p (h d)"), OF[b], reads=["OF"], writes=[f"ofl{par}"])
                    S.dma("sp", zsd[par][:], ZS[1280:2048, b * 128:(b + 1) * 128].rearrange("(h p) t -> p h t", p=128),
                          reads=["ZS"], writes=[f"zsd{par}"])
                gsl = gT[:, b, dirn * 6:(dirn + 1) * 6]

                def mgc(e):
                    e.matmul(Hps[0][:, 3, 0:6], Uf[:, dirn, :], gsl, start=True, stop=True)
                    return e.matmul(Hps[0][:, 3, 6:12], onesf[:], gsl, start=True, stop=True)
                S.op("pe", mgc, reads=["Uf", "onesf", "gT"], writes=["H0"])
                S.op("dve", lambda e: e.tensor_copy(gcs[par][:], Hps[0][:, 3, 0:12]), reads=["H0"], writes=[f"gcs{par}"])
                S.op("dve", lambda e: e.tensor_scalar(ngc[par][:], gcs[par][:, 0:6], -1.0, None, ALU.mult), reads=[f"gcs{par}"], writes=[f"ngc{par}"])
                S.op("dve", lambda e: e.tensor_tensor(egs[par][:, 6:12], gcs[par][:, 6:12], gcs[par][:, 0:6], ALU.subtract), reads=[f"gcs{par}"], writes=[f"egs1{par}"])
                S.op("act", lambda e: e.activation(egs[par][:, 0:6], gcs[par][:, 0:6], AF.Exp), reads=[f"gcs{par}"], writes=[f"egs0{par}"])
                S.op("act", lambda e: e.activation(egs[par][:, 6:12], egs[par][:, 6:12], AF.Exp), reads=[f"egs1{par}"], writes=[f"egs1{par}"])
                S.op("act", lambda e: e.activation(egs[par][:, 12:18], gcs[par][:, 6:12], AF.Exp), reads=[f"gcs{par}"], writes=[f"egs2{par}"])

            def part1(dirn, b, par, h):
                sfx = f"{par}_{h}"
                Hh = Hps[h]
                hk = f"H{h}"
                kq_, kn_ = kqt[par], knt[par]
                gsl = gT[:, b, dirn * 6:(dirn + 1) * 6]
                nb_ = nbetaT[:, b, dirn * 6:(dirn + 1) * 6]
                Ud = Uf[:, dirn, :]
                Xa, Xb = X[par]
                S.op("act", lambda e: e.activation(gU[par][:, h, :], Ud, AF.Copy, scale=gsl[:, h:h + 1]),
                     reads=["Uf", "gT"], writes=[f"gU{sfx}"])
                S.op("act", lambda e: e.copy(Xa[:, h, 2, :], identf[:]), reads=["identf"], writes=[f"XaP{sfx}"])
                yield

                def m1(e):
                    e.matmul(Hh[:, 3, :], onesf[:], gU[par][:, h, :], start=True, stop=True)
                    return e.matmul(Hh[:, 1:3, :], kq_[:, h, 0, :], kq_[:, h, :, :], start=True, stop=True)
                S.op("pe", m1, reads=[f"gU{sfx}", "onesf", f"kqt{par}"], writes=[hk])
                yield
                S.op("dve", lambda e: e.tensor_tensor(Et[par][:, h, :], Hh[:, 3, :], maskB[:, dirn, :], ALU.add),
                     reads=[hk, "maskB"], writes=[f"Et{sfx}"])
                S.op("act", lambda e: e.activation(EG[par][:, h, :], Hh[:, 3, :], AF.Exp), reads=[hk], writes=[f"EG{sfx}"])
                yield
                S.op("act", lambda e: e.activation(DT[par][:, h, :], Et[par][:, h, :], AF.Exp, bias=ngc[par][:, h:h + 1]),
                     reads=[f"Et{sfx}", f"ngc{par}"], writes=[f"DT{sfx}"])
                S.op("dve", lambda e: e.tensor_tensor(qg[par][:, h, :], kq_[:, h, 1, :], EG[par][:, h, :], ALU.mult),
                     reads=[f"kqt{par}", f"EG{sfx}"], writes=[f"qg{sfx}"])
                S.op("act", lambda e: e.activation(kg[par][:, h, :], kn_[:, h, :], AF.Copy, scale=egs[par][:, h:h + 1]),
                     reads=[f"knt{par}", f"egs0{par}"], writes=[f"kg{sfx}"])
                S.op("act", lambda e: e.activation(kt[par][:, h, :], kn_[:, h, :], AF.Copy, scale=egs[par][:, 6 + h:7 + h]),
                     reads=[f"knt{par}", f"egs1{par}"], writes=[f"kt{sfx}"])
                yield
                S.op("pool", lambda e: e.tensor_tensor(tt[par][:, h, :], DT[par][:, h, :], strictM[:, dirn, :], ALU.mult),
                     reads=[f"DT{sfx}", "strictM"], writes=[f"tt{sfx}"])
                S.op("dve", lambda e: e.tensor_tensor(MT[par][:, h, :], Hh[:, 2, :], DT[par][:, h, :], ALU.mult),
                     reads=[hk, f"DT{sfx}"], writes=[f"MT{sfx}"])
                yield
                S.op("dve", lambda e: e.scalar_tensor_tensor(Xa[:, h, 1, :], Hh[:, 1, :], nb_[:, h:h + 1], tt[par][:, h, :], ALU.mult, ALU.mult),
                     reads=[hk, f"tt{sfx}", "nbetaT"], writes=[f"XaQ{sfx}"])
                yield
                S.op("pe", lambda e: e.matmul(Hh[:, 0, :], Xa[:, h, 1, :], identr[:], start=True, stop=True), reads=[f"XaQ{sfx}", "identr"], writes=[hk])
                yield
                S.op("act", lambda e: e.copy(Xa[:, h, 0, :], Hh[:, 0, :]), reads=[hk], writes=[f"XaT{sfx}"])
                yield
                cur = 0
                for lev in range(7):
                    Xc, Xn_ = (Xa, Xb) if cur == 0 else (Xb, Xa)
                    cc_ = "Xa" if cur == 0 else "Xb"
                    nn_ = "Xb" if cur == 0 else "Xa"

                    def mAB(e, Xc=Xc, lev=lev):
                        if lev < 6:
                            e.matmul(Hh[:, 1:3, :], Xc[:, h, 0, :], Xc[:, h, 1:3, :], start=True, stop=True)
                            return e.matmul(Hh[:, 0, :], Xc[:, h, 1, :], Xc[:, h, 0, :], start=True, stop=True)
                        return e.matmul(Hh[:, 2, :], Xc[:, h, 0, :], Xc[:, h, 2, :], start=True, stop=True)
                    S.op("pe", mAB, reads=[cc_ + "Q" + sfx, cc_ + "T" + sfx, cc_ + "P" + sfx], writes=[hk])
                    yield
                    if lev < 6:
                        S.op("act", lambda e, Xn_=Xn_: e.copy(Xn_[:, h, 0:2, :], Hh[:, 0:2, :]), reads=[hk],
                             writes=[nn_ + "Q" + sfx, nn_ + "T" + sfx])
                        S.op("dve", lambda e, Xn_=Xn_, Xc=Xc: e.tensor_tensor(Xn_[:, h, 2, :], Hh[:, 2, :], Xc[:, h, 2, :], ALU.add),
                             reads=[hk, cc_ + "P" + sfx], writes=[nn_ + "P" + sfx])
                    else:
                        S.op("dve", lambda e, Xc=Xc: e.tensor_tensor(Pb[par][:, h, :], Hh[:, 2, :], Xc[:, h, 2, :], ALU.add),
                             reads=[hk, cc_ + "P" + sfx], writes=[f"Pb{sfx}"])
                    yield
                    cur = 1 - cur

            def part2(dirn, b, par, h):
                sfx = f"{par}_{h}"
                Qp, qk = (QA, "QA") if h % 2 == 0 else (QB, "QB")
                bt_ = betaT[:, b, dirn * 6:(dirn + 1) * 6]
                S.op("pe", lambda e: e.matmul(Qp[:, 0, :], kg[par][:, h, :], Pb[par][:, h, :], start=True, stop=True),
                     reads=[f"kg{sfx}", f"Pb{sfx}", qk], writes=[qk + "W"])
                yield
                S.op("dve", lambda e: e.tensor_scalar(nWT[:, h, :], Qp[:, 0, :], -1.0, None, ALU.mult), reads=[qk + "W", qk], writes=[f"nWT{h}"])
                yield

                def mV(e):
                    e.matmul(Qp[:, 1, :], Pb[par][:, h, :], vnt[par][:, h, :], start=True, stop=False)
                    return e.matmul(Qp[:, 1, :], nWT[:, h, :], Sb[:, h, :], start=False, stop=True)
                S.op("pe", mV, reads=[f"Pb{sfx}", f"vnt{par}", f"nWT{h}", f"Sb{h}", qk], writes=[qk + "V"])
                yield
                S.op("dve", lambda e: e.tensor_scalar(vnew[:, h, :], Qp[:, 1, :], bt_[:, h:h + 1], None, ALU.mult),
                     reads=[qk + "V", qk, "betaT"], writes=[f"vnew{h}"])
                yield

                def mOS(e):
                    e.matmul(Qp[:, 2, :], qg[par][:, h, :], Sb[:, h, :], start=True, stop=False)
                    e.matmul(Qp[:, 2, :], MT[par][:, h, :], vnew[:, h, :], start=False, stop=True)
                    return e.matmul(Qp[:, 3, :], kt[par][:, h, :], vnew[:, h, :], start=True, stop=True)
                S.op("pe", mOS, reads=[f"qg{sfx}", f"Sb{h}", f"MT{sfx}", f"vnew{h}", f"kt{sfx}", qk], writes=[qk + "O", qk + "S"])
                yield
                S.op("dve", lambda e: e.scalar_tensor_tensor(Sf[:, h, :], Sf[:, h, :], egs[par][:, 12 + h:13 + h], Qp[:, 3, :], ALU.mult, ALU.add),
                     reads=[f"Sf{h}", f"egs2{par}", qk + "S", qk], writes=[f"Sf{h}"])
                if dirn == 0:
                    S.op("dve", lambda e: e.tensor_copy(of[par][:, h, :], Qp[:, 2, :]), reads=[qk + "O", qk], writes=[f"of{sfx}"])
                else:
                    S.op("dve", lambda e: e.tensor_tensor(of[par][:, h, :], Qp[:, 2, :], ofl[par][:, h, :], ALU.add),
                         reads=[qk + "O", qk, f"ofl{par}"], writes=[f"of{sfx}"])
                yield
                S.op("act", lambda e: e.copy(Sb[:, h, :], Sf[:, h, :]), reads=[f"Sf{h}"], writes=[f"Sb{h}"])
                yield

            def chain(*gs):
                for g_ in gs:
                    yield from g_

            def part3(dirn, b, par):
                ofk = [f"of{par}_{h}" for h in range(6)]
                if dirn == 0:
                    S.dma("sp", OF[b], of[par][:].rearrange("p h d -> p (h d)"), reads=ofk, writes=["OF"])
                    return
                o_ = of[par]
                S.op("pool", lambda e: e.tensor_tensor(osq[:], o_[:], o_[:], ALU.mult), reads=ofk, writes=["osq"])
                S.op("dve", lambda e: e.tensor_reduce(rn[:, 0:6], osq[:], mybir.AxisListType.X, ALU.add), reads=["osq"], writes=["rn0"])
                S.op("act", lambda e: e.activation(rn[:, 6:12], rn[:, 0:6], AF.Sqrt, bias=epsc[:, 0:1], scale=1.0 / 128), reads=["rn0", "epsc"], writes=["rn1"])
                S.op("dve", lambda e: e.reciprocal(rn[:, 6:12], rn[:, 6:12]), reads=["rn1"], writes=["rn1"])
                for h in range(6):
                    S.op("pool", lambda e, h=h: e.tensor_scalar(osq[:, h, :], o_[:, h, :], rn[:, 6 + h:7 + h], None, ALU.mult),
                         reads=ofk + ["rn1"], writes=["osq"])
                S.op("pool", lambda e: e.tensor_tensor(onf[:], osq[:], dnwB[:, l, :].unsqueeze(1).broadcast_to([128, 6, 128]), ALU.mult),
                     reads=["osq", "dnwB"], writes=["onf"])

                def trO(e):
                    r = None
                    for h in range(3):
                        r = e.transpose(QA[:, h, :], onf[:, h, :], identf[:])
                    for h in range(3):
                        r = e.transpose(QB[:, h, :], onf[:, 3 + h, :], identf[:])
                    return r
                S.op("pe", trO, reads=["onf", "identf", "QA", "QB"], writes=["QAW", "QAV", "QAO", "QBW", "QBV", "QBO"])
                S.op("dve", lambda e: e.tensor_tensor(ydn[:, 0:3, :], QA[:, 0:3, :], zsd[par][:, 0:3, :], ALU.mult),
                     reads=["QAW", "QAV", "QAO", "QA", f"zsd{par}"], writes=["ydn"])
                S.op("dve", lambda e: e.tensor_tensor(ydn[:, 3:6, :], QB[:, 0:3, :], zsd[par][:, 3:6, :], ALU.mult),
                     reads=["QBW", "QBV", "QBO", "QB", f"zsd{par}"], writes=["ydn"])
                S.dma("sp", YDN[b], ydn[:].rearrange("p h t -> p (h t)"), reads=["ydn"], writes=["YDN"])

            rrcnt = [0]

            def rr(gens):
                gens = list(gens)
                while gens:
                    alive = []
                    for gn in gens:
                        if BCUT and rrcnt[0] >= BCUT:
                            return
                        rrcnt[0] += 1
                        try:
                            next(gn)
                            alive.append(gn)
                        except StopIteration:
                            pass
                    gens = alive

            for dirn in range(2):
                S.op("pool", lambda e: e.memset(Sf[:], 0.0), writes=[f"Sf{h}" for h in range(6)])
                S.op("pool", lambda e: e.memset(Sb[:], 0.0), writes=[f"Sb{h}" for h in range(6)])
                blocks = list(range(NB)) if dirn == 0 else list(range(NB - 1, -1, -1))
                for i, b in enumerate(blocks):
                    par = i % 2
                    part0(dirn, b, par)
                    gl = []
                    if i > 0:
                        pb, pp = blocks[i - 1], 1 - par
                        gl.append(chain(*[part2(dirn, pb, pp, h) for h in (0, 2, 4)]))
                        gl.append(chain(*[part2(dirn, pb, pp, h) for h in (1, 3, 5)]))
                    for h in range(6):
                        gl.append(part1(dirn, b, par, h))
                    rr(gl)
                    if i > 0:
                        part3(dirn, blocks[i - 1], 1 - par)
                lp = (len(blocks) - 1) % 2
                rr([chain(*[part2(dirn, blocks[-1], lp, h) for h in (0, 2, 4)]),
                    chain(*[part2(dirn, blocks[-1], lp, h) for h in (1, 3, 5)])])
                part3(dirn, blocks[-1], lp)

        if stop == 'B':
            S.close()
            return nc
        with S.phase(last=(l == L - 1)):
            def T2(name, shape, dt):
                return [S.sb(f"{name}{i}", shape, dt) for i in range(2)]
            wout = S.sb("wout", [128, KC, D], BF16)
            hc2 = T2("hc", [128, D], F32)
            hn2 = T2("hn", [128, D], F32)
            yo = S.sb("yo", [128, D], F32)
            junk2 = S.sb("junk2", [128, D], BF16)
            ss2 = S.sb("ss2", [128, 2], F32)
            yT2 = T2("yT", [128, 16, 128], BF16)
            uc2 = T2("uc", [128, 4, 130], F32)
            cbt2 = T2("cbt", [128, 4, 128], F32)
            cvc2 = T2("cvc", [128, 4, 128], F32)
            zst2 = T2("zst", [128, 10, 128], BF16)
            qt2 = T2("qt", [128, 6, 128], BF16)
            ktl2 = T2("ktl", [128, 2, 384], BF16)
            vl2 = T2("vl", [128, 3, 256], BF16)
            Ea2 = T2("Ea", [128, 384], F32)
            PTa2 = T2("PTa", [128, 384], BF16)
            den = S.sb("den", [128, 384], F32)
            oa = S.sb("oa", [128, 384], F32)
            pst2 = [S.ps(f"pst{i}", [128, 512], F32)[:, 0:384] for i in range(2)]
            pot = [S.ps(f"pot{g}", [128, 512], F32)[:, 0:384] for g in range(2)]
            pden = [S.ps(f"pden{g}", [128, 512], F32)[:, 0:384] for g in range(2)]
            po = [S.ps(f"po{i}", [128, 512], F32) for i in range(2)]

            S.dma("sp", wout[:], Wb_out[l].rearrange("(kc p) c -> p kc c", p=128), reads=[f"Wb_out{l}"], writes=["wout"])
            scnt = [0]

            def att_gen(b):
                par = b % 2
                P = str(par)
                hc, hn, yT, uc, cbt, cvc, zst, qt, ktl, vl = (hc2[par], hn2[par], yT2[par], uc2[par], cbt2[par], cvc2[par],
                                                              zst2[par], qt2[par], ktl2[par], vl2[par])
                t0 = b * 128
                lo = 1 if b == 0 else 0
                hi = 129 if b == NB - 1 else 130
                if lo == 1:
                    S.op("pool", lambda e: e.memset(uc[:, :, 0:1], 0.0), writes=["uc" + P])
                if hi == 129:
                    S.op("pool", lambda e: e.memset(uc[:, :, 129:130], 0.0), writes=["uc" + P])
                S.dma("sp", uc[:, :, lo:hi], UC[:, t0 - 1 + lo:t0 - 1 + hi].rearrange("(c p) t -> p c t", p=128),
                      reads=["UC"], writes=["uc" + P])
                S.dma("sp", cbt[:], CB[:, t0:t0 + 128].rearrange("(c p) t -> p c t", p=128), reads=["CB"], writes=["cbt" + P])
                S.dma("sp", zst[:], ZS[0:1280, t0:t0 + 128].rearrange("(c p) t -> p c t", p=128), reads=["ZS"], writes=["zst" + P])
                S.dma("sp", hc[:], Hin[t0:t0 + 128, :], reads=[hin_key], writes=["hc" + P])
                kbs = [kb for kb in (b - 1, b, b + 1) if 0 <= kb < NB]
                k0, k1 = kbs[0], kbs[-1]
                S.dma("sp", qt[:], QT[:, t0:t0 + 128].rearrange("(h p) t -> p h t", p=128), reads=["QT"], writes=["qt" + P])
                S.dma("sp", ktl[:, :, 0:(k1 - k0 + 1) * 128], KT[:, k0 * 128:(k1 + 1) * 128].rearrange("(g p) t -> p g t", p=128),
                      reads=["KT"], writes=["ktl" + P])
                S.dma("sp", vl[:, 0:(k1 - k0 + 1), :], VV[k0 * 128:(k1 + 1) * 128, :].rearrange("(j p) c -> p j c", p=128),
                      reads=["VV"], writes=["vl" + P])
                S.dma("sp", yT[:, 10:16, :].rearrange("p h t -> p (h t)"), YDN[b], reads=["YDN"], writes=["yT_d" + P])
                for c in range(4):
                    ck = f"cvc{P}_{c}"
                    S.op("act", lambda e, c=c: e.activation(cvc[:, c, :], uc[:, c, 1:129], AF.Copy, scale=caw[:, l, c, 1:2]),
                         reads=["uc" + P, "caw"], writes=[ck])
                    S.op("dve", lambda e, c=c: e.scalar_tensor_tensor(cvc[:, c, :], uc[:, c, 0:128], caw[:, l, c, 0:1], cvc[:, c, :], ALU.mult, ALU.add),
                         reads=["uc" + P, ck], writes=[ck])
                    S.op("dve", lambda e, c=c: e.scalar_tensor_tensor(cvc[:, c, :], uc[:, c, 2:130], caw[:, l, c, 2:3], cvc[:, c, :], ALU.mult, ALU.add),
                         reads=["uc" + P, ck], writes=[ck])
                cvck = [f"cvc{P}_{c}" for c in range(4)]
                S.op("pool", lambda e: e.tensor_tensor(cvc[:], cvc[:], cbt[:], ALU.mult), reads=cvck + ["cbt" + P], writes=cvck)
                S.op("pool", lambda e: e.tensor_tensor(yT[:, 0:4, :], cvc[:], zst[:, 0:4, :], ALU.mult), reads=cvck + ["zst" + P], writes=["yT_c" + P])
                yield
                for g in range(2):
                    for ji, kb in enumerate(kbs):
                        off = kb - b + 1
                        j = kb - k0
                        first = (ji == 0)
                        lastk = (ji == len(kbs) - 1)
                        si = scnt[0] % 2
                        scnt[0] += 1
                        pst, Ea, PTa = pst2[si], Ea2[si], PTa2[si]
                        S.op("pe", lambda e, g=g, j=j, pst=pst: e.matmul(pst, ktl[:, g, j * 128:(j + 1) * 128], qt[:, 3 * g:3 * g + 3, :], start=True, stop=True),
                             reads=["ktl" + P, "qt" + P], writes=[f"pst{si}"])
                        yield
                        S.op("dve", lambda e, g=g, off=off, pst=pst, Ea=Ea: e.tensor_tensor(Ea[:], pst, AL[:, off, g * 384:(g + 1) * 384], ALU.add),
                             reads=[f"pst{si}", "AL"], writes=[f"Ea{si}"])
                        S.op("act", lambda e, kb=kb, Ea=Ea, PTa=PTa: e.activation(PTa[:], Ea[:], AF.Exp, bias=kbias[:, kb:kb + 1]),
                             reads=[f"Ea{si}", "kbias"], writes=[f"PTa{si}"])
                        yield

                        def mpv(e, g=g, j=j, first=first, lastk=lastk, PTa=PTa):
                            e.matmul(pot[g], vl[:, j, g * 128:(g + 1) * 128], PTa[:], start=first, stop=lastk)
                            return e.matmul(pden[g], onesb[:], PTa[:], start=first, stop=lastk)
                        S.op("pe", mpv, reads=["vl" + P, f"PTa{si}", "onesb"], writes=[f"pot{g}", f"pden{g}"])
                        yield
                    for hh in range(3):
                        h = 3 * g + hh
                        S.op("dve", lambda e, g=g, hh=hh, h=h: e.tensor_scalar(den[:, hh * 128:(hh + 1) * 128], pden[g][:, hh * 128:(hh + 1) * 128], esink[:, l, h:h + 1], None, ALU.add),
                             reads=[f"pden{g}", "esink"], writes=["den"])
                    S.op("dve", lambda e: e.reciprocal(den[:], den[:]), reads=["den"], writes=["den"])
                    S.op("dve", lambda e, g=g: e.tensor_tensor(oa[:], pot[g], den[:], ALU.mult), reads=[f"pot{g}", "den"], writes=["oa"])
                    S.op("pool", lambda e, g=g: e.tensor_tensor(yT[:, 4 + 3 * g:7 + 3 * g, :], oa[:].rearrange("p (h t) -> p h t", h=3), zst[:, 4 + 3 * g:7 + 3 * g, :], ALU.mult),
                         reads=["oa", "zst" + P], writes=[f"yT_a{g}" + P])
                    yield

            def proj_gen(b):
                par = b % 2
                P = str(par)
                hc, hn, yT = hc2[par], hn2[par], yT2[par]
                t0 = b * 128
                ykeys = ["yT_c" + P, "yT_a0" + P, "yT_a1" + P, "yT_d" + P]
                for n4 in range(4):
                    p = po[n4 % 2]
                    pk = f"po{n4 % 2}"

                    def mo(e, p=p, n4=n4):
                        r = None
                        for mc in range(16):
                            r = e.matmul(p[:], yT[:, mc, :], wout[:, mc, n4 * 512:(n4 + 1) * 512], start=(mc == 0), stop=(mc == 15))
                        return r
                    S.op("pe", mo, reads=ykeys + ["wout"], writes=[pk])
                    yield
                    S.op("dve", lambda e, p=p, n4=n4: e.scalar_tensor_tensor(hn[:, n4 * 512:(n4 + 1) * 512], p[:], validT[:, b:b + 1], hc[:, n4 * 512:(n4 + 1) * 512], ALU.mult, ALU.add),
                         reads=[pk, "validT", "hc" + P], writes=[f"hn{P}_{n4}"])
                    yield
                hnk = [f"hn{P}_{n4}" for n4 in range(4)]
                if l < L - 1:
                    S.dma("pool", H[t0:t0 + 128, :], hn[:], reads=hnk, writes=["H"])
                elif 1 <= b <= NBOUT:
                    S.op("act", lambda e: e.activation(junk2[:], hn[:], AF.Square, accum_out=ss2[:, 0:1]), reads=hnk, writes=["junk2", "ss2a"])
                    S.op("act", lambda e: e.activation(ss2[:, 1:2], ss2[:, 0:1], AF.Sqrt, bias=epsc[:, 0:1], scale=1.0 / D), reads=["ss2a", "epsc"], writes=["ss2b"])
                    S.op("dve", lambda e: e.reciprocal(ss2[:, 1:2], ss2[:, 1:2]), reads=["ss2b"], writes=["ss2b"])
                    S.op("dve", lambda e: e.scalar_tensor_tensor(yo[:], hn[:], ss2[:, 1:2], fnwB[:], ALU.mult, ALU.mult),
                         reads=hnk + ["ss2b", "fnwB"], writes=["yo"])
                    S.dma("pool", out_d[(b - 1) * 128:b * 128, :], yo[:], reads=["yo"], writes=["out"], final=True)

            def rr2(gens):
                gens = [g_ for g_ in gens if g_ is not None]
                while gens:
                    alive = []
                    for gn in gens:
                        try:
                            next(gn)
                            alive.append(gn)
                        except StopIteration:
                            pass
                    gens = alive

            rr2([att_gen(0)])
            for b in range(NB):
                rr2([proj_gen(b), att_gen(b + 1) if b + 1 < NB else None])
    S.close()
    return nc


def make_consts():
    import ml_dtypes
    i = np.arange(128)
    c = {}
    c["c_identb"] = np.eye(128, dtype=np.float32)
    U = np.zeros((128, 2, 128), np.float32)
    U[:, 0, :] = (i[:, None] <= i[None, :])
    U[:, 1, :] = (i[:, None] >= i[None, :])
    c["c_U"] = U
    mB = np.zeros((128, 2, 128), np.float32)
    mB[:, 0, :] = np.where(i[None, :] < i[:, None], NEG, 0.0)
    mB[:, 1, :] = np.where(i[None, :] > i[:, None], NEG, 0.0)
    c["c_maskB"] = mB
    st = np.zeros((128, 2, 128), np.float32)
    st[:, 0, :] = (i[None, :] > i[:, None])
    st[:, 1, :] = (i[None, :] < i[:, None])
    c["c_strict"] = st
    slopes = np.exp2(-8.0 * np.arange(1, 7, dtype=np.float32) / 6).astype(np.float32)
    AL = np.zeros((128, 3, 6, 128), np.float32)
    s = i[:, None].astype(np.float32)
    q = i[None, :].astype(np.float32)
    for off in range(3):
        dist = np.abs(q - (s + (off - 1) * 128))
        for h in range(6):
            AL[:, off, h, :] = np.where(dist <= 128, -slopes[h] * dist, NEG)
    c["c_AL"] = AL.reshape(128, 3, 768)
    return c


def prep_weights(norm_w, w_in, conv_a_w, attn_sink, dn_conv_w, dn_a_log, dn_dt_bias, dn_norm_w, w_out, final_norm_w):
    L = w_in.shape[0]
    rep = lambda a: np.ascontiguousarray(np.broadcast_to(a[None], (128,) + a.shape)).astype(np.float32)
    m = {}
    m["w_in"] = np.ascontiguousarray(w_in, dtype=np.float32)
    m["w_out"] = np.ascontiguousarray(w_out, dtype=np.float32)
    m["nwB"] = rep(np.asarray(norm_w, np.float32))
    m["fnwB"] = rep(np.asarray(final_norm_w, np.float32))
    m["caw"] = np.ascontiguousarray(np.asarray(conv_a_w, np.float32).reshape(L, 3, 4, 128).transpose(3, 0, 2, 1))
    m["dcw"] = np.ascontiguousarray(np.asarray(dn_conv_w, np.float32).reshape(L, 3, 18, 128).transpose(3, 0, 2, 1))
    m["sinkB"] = rep(np.asarray(attn_sink, np.float32))
    m["alogB"] = rep(np.asarray(dn_a_log, np.float32).reshape(L, 12))
    m["dtbB"] = rep(np.asarray(dn_dt_bias, np.float32).reshape(L, 12))
    m["dnwB"] = rep(np.asarray(dn_norm_w, np.float32))
    return m


def core_inputs(x_seq, meta_tokens, NB):
    T = NB * 128
    h0 = np.zeros((T, D), np.float32)
    valid = np.zeros((T,), np.float32)
    if x_seq is not None:
        Sx = x_seq.shape[0]
        h0[PAD:LEAD] = meta_tokens
        h0[LEAD:LEAD + Sx] = x_seq
        valid[PAD:LEAD + Sx] = 1.0
    return {
        "h0": h0,
        "validT": np.ascontiguousarray(valid.reshape(NB, 128).T),
        "validB": np.ascontiguousarray(np.broadcast_to(valid[None], (128, T))),
    }


def kernel(x_prompt, x_sample, meta_tokens, norm_w, w_in, conv_a_w, attn_sink, dn_conv_w,
           dn_a_log, dn_dt_bias, dn_norm_w, w_out, final_norm_w):
    x_prompt = np.asarray(x_prompt, np.float32)
    x_sample = np.asarray(x_sample, np.float32)
    meta_tokens = np.asarray(meta_tokens, np.float32)
    L = w_in.shape[0]
    Sp = x_prompt.shape[1]
    Ss = x_sample.shape[1]
    NB = (LEAD + max(Sp, Ss)) // 128
    NBOUT = NB - 1
    shared = prep_weights(norm_w, np.asarray(w_in), conv_a_w, attn_sink, dn_conv_w, dn_a_log, dn_dt_bias,
                          dn_norm_w, np.asarray(w_out), final_norm_w)
    shared.update(make_consts())
    seqs = [x_prompt[i] for i in range(x_prompt.shape[0])] + [x_sample[i] for i in range(x_sample.shape[0])]
    assert len(seqs) <= 8
    in_maps = []
    for c in range(8):
        m = dict(shared)
        m.update(core_inputs(seqs[c] if c < len(seqs) else None, meta_tokens, NB))
        in_maps.append(m)
    nc = build_nc(NB, L, NBOUT)
    res = run_bass_kernel_spmd(nc, in_maps, core_ids=list(range(8)))
    outs = [np.asarray(r["out"]) for r in res.results]
    nP = x_prompt.shape[0]
    y_prompt = np.stack([outs[i][:Sp] for i in range(nP)], axis=0).astype(np.float32)
    y_sample = np.stack([outs[nP + i][:Ss] for i in range(x_sample.shape[0])], axis=0).astype(np.float32)
    return (y_prompt, y_sample)
```

```python
import contextlib
import numpy as np
import concourse.bass as bass
import concourse.mybir as mybir
from concourse.bass_utils import run_bass_kernel_spmd

F32 = mybir.dt.float32
BF16 = mybir.dt.bfloat16
F32R = mybir.dt.float32r
AF = mybir.ActivationFunctionType
ALU = mybir.AluOpType

import os as _os
BCUT = int(_os.environ.get('BCUT', '0'))
CCUT = int(_os.environ.get('CCUT', '0'))
D = 2048
KC = 16
NPROJ = 7192
LEAD = 128
PAD = 112
NEG = -1.0e9
EPS = 1e-6


class Sched:
    COMPUTE = ("pe", "act", "dve", "pool")
    ALLENG = ("pe", "act", "dve", "pool", "sp")
    NDMA = {"sp": 8, "pool": 4}

    def __init__(self, nc):
        self.nc = nc
        self.gstack = contextlib.ExitStack()
        self.sems = {}
        for e in self.COMPUTE:
            self.sems["e:" + e] = self.gstack.enter_context(nc.semaphore("s_" + e))
        for q, k in self.NDMA.items():
            for j in range(k):
                self.sems[f"d:{q}:{j}"] = self.gstack.enter_context(nc.semaphore(f"d_{q}{j}"))
        self.seq = {e: 0 for e in self.COMPUTE}
        self.ndma = {q: 0 for q in self.NDMA}
        self.last_tok = {}
        self.known = {e: {} for e in self.ALLENG}
        self.last_w = {}
        self.readers = {}
        self.ops = {e: [] for e in self.ALLENG}
        self.final_tokens = []
        self.barrier_tokens = []
        self.pstack = None
        self.nops = 0
        self.bankmap = {}
        self.bank_last = {}

    def _uid(self):
        self.uid = getattr(self, "uid", 0) + 1
        return f"_{self.uid}"

    def sb(self, name, shape, dtype, glob=False):
        st = self.gstack if (glob or self.pstack is None) else self.pstack
        return st.enter_context(self.nc.sbuf_tensor("s_" + name + self._uid(), list(shape), dtype))

    def ps(self, name, shape, dtype, nbanks=1):
        nbytes = (4 if dtype == F32 else 2)
        for s_ in shape[1:]:
            nbytes *= s_
        assert nbytes == 2048 * nbanks, (name, shape, nbytes)
        self.bankmap[name] = [f"B:{name}:{i}" for i in range(nbanks)]
        return self.pstack.enter_context(self.nc.psum_tensor("p_" + name + self._uid(), list(shape), dtype))

    def _split(self, keys):
        reg, banks = [], []
        for k in keys:
            if k in self.bankmap:
                banks.extend(self.bankmap[k])
            else:
                reg.append(k)
        return reg, banks

    def _bank_deps(self, eng, banks):
        deps = []
        for bk in banks:
            for e2, tok in self.bank_last.get(bk, {}).items():
                if e2 != eng:
                    deps.append(tok)
        return deps

    def _bank_commit(self, eng, banks, token):
        for bk in banks:
            self.bank_last.setdefault(bk, {})[eng] = token

    def _deps(self, reads, writes):
        deps = list(self.barrier_tokens)
        for k in reads:
            t = self.last_w.get(k)
            if t is not None:
                deps.append(t)
        for k in writes:
            t = self.last_w.get(k)
            if t is not None:
                deps.append(t)
            deps.extend(self.readers.get(k, ()))
        return deps

    def _commit(self, token, reads, writes):
        self.last_tok[token[0]] = token[1]
        for k in reads:
            self.readers.setdefault(k, []).append(token)
        for k in writes:
            self.last_w[k] = token
            self.readers[k] = []

    def _waits(self, issuer, deps, skip_key=None):
        kn = self.known[issuer]
        need = {}
        for (key, val) in deps:
            if key == skip_key:
                continue
            if kn.get(key, 0) < val and need.get(key, 0) < val:
                need[key] = val
        for key, val in need.items():
            kn[key] = val
        return list(need.items())

    def op(self, eng, fn, reads=(), writes=()):
        reads, b1 = self._split(reads)
        writes, b2 = self._split(writes)
        banks = set(b1 + b2)
        deps = self._deps(reads, writes) + self._bank_deps(eng, banks)
        key = "e:" + eng
        waits = self._waits(eng, deps, skip_key=key if eng == "pe" else None)
        self.seq[eng] += 1
        token = (key, self.seq[eng])
        self.ops[eng].append((waits, fn, key, 1))
        self._commit(token, reads, writes)
        self._bank_commit(eng, banks, token)
        self.nops += 1
        return token

    def dma(self, q, out, in_, reads=(), writes=(), final=False):
        n = self.ndma[q]
        self.ndma[q] += 1
        K = self.NDMA[q]
        key = f"d:{q}:{n % K}"
        deps = self._deps(reads, writes)
        if n // K > 0:
            deps.append((key, 16 * (n // K)))
        waits = self._waits(q, deps)
        token = (key, 16 * (n // K + 1))
        self.ops[q].append((waits, lambda e: e.dma_start(out=out, in_=in_), key, 16))
        self._commit(token, reads, writes)
        if final:
            self.final_tokens.append(token)
        self.nops += 1
        return token

    def barrier(self):
        self.barrier_tokens = list(self.last_tok.items())

    @contextlib.contextmanager
    def phase(self, last=False):
        self.pstack = contextlib.ExitStack()
        self.barrier()
        yield self
        self._emit(last)
        self.pstack.close()
        self.pstack = None

    def _emit(self, last):
        nc = self.nc
        sems = self.sems
        fin = []
        if last:
            fin = self._waits("sp", self.final_tokens)
        ops = self.ops

        def run(engine, lst, tail=()):
            for (waits, fn, key, amt) in lst:
                for (k, v) in waits:
                    engine.wait_ge(sems[k], v)
                inst = fn(engine)
                inst.then_inc(sems[key], amt)
            for (k, v) in tail:
                engine.wait_ge(sems[k], v)

        with nc.Block() as block:
            @block.sync
            def _(e):
                run(e, ops["sp"], fin)

            @block.tensor
            def _(e):
                run(e, ops["pe"])

            @block.scalar
            def _(e):
                run(e, ops["act"])

            @block.vector
            def _(e):
                run(e, ops["dve"])

            @block.gpsimd
            def _(e):
                run(e, ops["pool"])
        self.ops = {e: [] for e in self.ALLENG}

    def close(self):
        self.gstack.close()


def build_nc(NB, L, NBOUT, G=4, stop=None):
    T = NB * 128
    nc = bass.Bass("TRN2", target_bir_lowering=False)
    dt_in = lambda n, s, d=F32: nc.dram_tensor(n, list(s), d, kind="ExternalInput").ap()
    dt_sc = lambda n, s, d=F32: nc.dram_tensor(n, list(s), d, kind="Internal").ap()

    h0 = dt_in("h0", [T, D])
    validT_d = dt_in("validT", [128, NB])
    validB_d = dt_in("validB", [128, T])
    w_in_d = dt_in("w_in", [L, D, NPROJ])
    w_out_d = dt_in("w_out", [L, D, D])
    nwB_d = dt_in("nwB", [128, L, D])
    fnwB_d = dt_in("fnwB", [128, D])
    caw_d = dt_in("caw", [128, L, 4, 3])
    dcw_d = dt_in("dcw", [128, L, 18, 3])
    sinkB_d = dt_in("sinkB", [128, L, 6])
    alogB_d = dt_in("alogB", [128, L, 12])
    dtbB_d = dt_in("dtbB", [128, L, 12])
    dnwB_d = dt_in("dnwB", [128, L, 128])
    c_identb_d = dt_in("c_identb", [128, 128])
    c_U_d = dt_in("c_U", [128, 2, 128])
    c_maskB_d = dt_in("c_maskB", [128, 2, 128])
    c_strict_d = dt_in("c_strict", [128, 2, 128])
    c_AL_d = dt_in("c_AL", [128, 3, 768])
    out_d = nc.dram_tensor("out", [NBOUT * 128, D], F32, kind="ExternalOutput").ap()

    Wb_in = dt_sc("Wb_in", [L, D, NPROJ], BF16)
    Wb_out = dt_sc("Wb_out", [L, D, D], BF16)
    H = dt_sc("H", [T, D])
    ZS = dt_sc("ZS", [D, T], BF16)
    UC = dt_sc("UC", [512, T])
    CB = dt_sc("CB", [512, T])
    QT = dt_sc("QT", [768, T], BF16)
    KT = dt_sc("KT", [256, T], BF16)
    VV = dt_sc("VV", [T, 256], BF16)
    DQKV = dt_sc("DQKV", [2304, T])
    BG = dt_sc("BG", [T, 24])
    KQT = dt_sc("KQT", [NB, 128, 6 * 2 * 128], BF16)
    KN = dt_sc("KN", [NB, 128, 768], BF16)
    VN = dt_sc("VN", [NB, 128, 768], BF16)
    OF = dt_sc("OF", [NB, 128, 768])
    YDN = dt_sc("YDN", [NB, 128, 768], BF16)

    S = Sched(nc)
    identb = S.sb("identb", [128, 128], BF16)
    identf = S.sb("identf", [128, 128], F32)
    Uf = S.sb("Uf", [128, 2, 128], F32)
    maskB = S.sb("maskB", [128, 2, 128], F32)
    strictM = S.sb("strictM", [128, 2, 128], F32)
    AL = S.sb("AL", [128, 3, 768], F32)
    onesf = S.sb("onesf", [128, 128], F32)
    onesb = S.sb("onesb", [128, 128], BF16)
    validT = S.sb("validT", [128, NB], F32)
    kbias = S.sb("kbias", [128, NB], F32)
    caw = S.sb("caw", [128, L, 4, 3], F32)
    dcw = S.sb("dcw", [128, L, 18, 3], F32)
    sinkB = S.sb("sinkB", [128, L, 6], F32)
    esink = S.sb("esink", [128, L, 6], F32)
    alogB = S.sb("alogB", [128, L, 12], F32)
    negA = S.sb("negA", [128, L, 12], F32)
    dtbB = S.sb("dtbB", [128, L, 12], F32)
    dnwB = S.sb("dnwB", [128, L, 128], F32)
    fnwB = S.sb("fnwB", [128, D], F32)
    nw = S.sb("nw", [128, D], F32)
    betaT = S.sb("betaT", [128, NB, 12], F32)
    nbetaT = S.sb("nbetaT", [128, NB, 12], F32)
    gT = S.sb("gT", [128, NB, 12], F32)
    epsc = S.sb("epsc", [128, 1], F32)
    onec = S.sb("onec", [128, 1], F32)

    with S.phase():
        S.dma("pool", identb[:], c_identb_d, writes=["identb"])
        S.dma("sp", identf[:], c_identb_d, writes=["identf"])
        S.dma("sp", Uf[:], c_U_d, writes=["Uf"])
        S.dma("sp", maskB[:], c_maskB_d, writes=["maskB"])
        S.dma("sp", strictM[:], c_strict_d, writes=["strictM"])
        S.dma("sp", AL[:], c_AL_d, writes=["AL"])
        S.dma("sp", validT[:], validT_d, writes=["validT"])
        S.dma("sp", caw[:], caw_d, writes=["caw"])
        S.dma("sp", dcw[:], dcw_d, writes=["dcw"])
        S.dma("sp", sinkB[:], sinkB_d, writes=["sinkB"])
        S.dma("sp", alogB[:], alogB_d, writes=["alogB"])
        S.dma("sp", dtbB[:], dtbB_d, writes=["dtbB"])
        S.dma("sp", dnwB[:], dnwB_d, writes=["dnwB"])
        S.dma("sp", fnwB[:], fnwB_d, writes=["fnwB"])
        S.op("pool", lambda e: e.memset(onesf[:], 1.0), writes=["onesf"])
        S.op("pool", lambda e: e.memset(onesb[:], 1.0), writes=["onesb"])
        S.op("pool", lambda e: e.memset(epsc[:], EPS), writes=["epsc"])
        S.op("pool", lambda e: e.memset(onec[:], 1.0), writes=["onec"])
        S.op("dve", lambda e: e.tensor_scalar(kbias[:], validT[:], -1.0, -NEG, ALU.add, ALU.mult),
             reads=["validT"], writes=["kbias"])
        S.op("act", lambda e: e.activation(esink[:], sinkB[:], AF.Exp), reads=["sinkB"], writes=["esink"])
        S.op("act", lambda e: e.activation(negA[:], alogB[:], AF.Exp), reads=["alogB"], writes=["negA"])
        S.op("dve", lambda e: e.tensor_scalar(negA[:], negA[:], -1.0, None, ALU.mult),
             reads=["negA"], writes=["negA"])
        def convert_weights(lw):
            for kc in range(KC):
                S.dma("pool", Wb_in[lw, kc * 128:(kc + 1) * 128, :], w_in_d[lw, kc * 128:(kc + 1) * 128, :],
                      writes=[f"Wb_in{lw}"])
            for kc in range(0, KC, 4):
                S.dma("pool", Wb_out[lw, kc * 128:(kc + 4) * 128, :], w_out_d[lw, kc * 128:(kc + 4) * 128, :],
                      writes=[f"Wb_out{lw}"])
        convert_weights(0)

    if stop == '0':
        S.close()
        return nc
    for l in range(L):
        Hin = h0 if l == 0 else H
        hin_key = "h0" if l == 0 else "H"
        with S.phase():
            N = G * 128
            ht = [S.sb(f"ht{i}", [128, D], F32) for i in range(2)]
            junk = S.sb("junk", [128, D], BF16)
            xn = S.sb("xn", [128, D], BF16)
            xnT2 = [S.sb(f"xnT{i}", [128, KC, N], BF16) for i in range(2)]
            wt = [S.sb(f"wt{i}", [128, KC, 512], BF16) for i in range(2)]
            wlast = S.sb("wlast", [128, KC, 24], BF16)
            cxs = S.sb("cxs", [128, 4, N], F32)
            ef = [S.sb(f"ef{i}", [128, N], F32) for i in range(3)]
            eb = [S.sb(f"eb{i}", [128, N], BF16) for i in range(3)]
            vtok = S.sb("vtok", [128, 256], BF16)
            bgt = S.sb("bgt", [128, 24], F32)
            ss = S.sb("ss", [128, 2], F32)
            ptr = [S.ps(f"ptr{i}", [128, 8, 128], BF16) for i in range(2)]
            pa = [S.ps(f"pa{i}", [128, 512], F32) for i in range(2)]
            pv_ = S.ps("pv", [128, 512], F32)
            pv = pv_[:, 0:256]
            pbg_ = S.ps("pbg", [128, 512], F32)
            pbg = pbg_[:, 0:24]

            S.dma("sp", nw[:], nwB_d[:, l, :], writes=["nw"])
            S.dma("sp", wlast[:], Wb_in[l, :, 7168:7192].rearrange("(kc p) c -> p kc c", p=128),
                  reads=[f"Wb_in{l}"], writes=["wlast"])
            ngroups = (NB + G - 1) // G
            ecnt = [0]
            ucnt = [0]

            def ukey(name):
                ucnt[0] += 1
                return f"{name}#{l}#{ucnt[0]}"

            def norm_gen(gi):
                gp = gi % 2
                xnT = xnT2[gp]
                b0 = gi * G
                gb = min(G, NB - b0)
                for bi in range(gb):
                    b = b0 + bi
                    hh = ht[b % 2]
                    hk = f"ht{b % 2}"
                    S.dma("sp", hh[:], Hin[b * 128:(b + 1) * 128, :], reads=[hin_key], writes=[hk])
                    S.op("act", lambda e, hh=hh: e.activation(junk[:], hh[:], AF.Square, accum_out=ss[:, 0:1]),
                         reads=[hk], writes=["junk", "ss0"])
                    S.op("act", lambda e: e.activation(ss[:, 1:2], ss[:, 0:1], AF.Sqrt, bias=epsc[:, 0:1], scale=1.0 / D),
                         reads=["ss0", "epsc"], writes=["ss1"])
                    S.op("dve", lambda e: e.reciprocal(ss[:, 1:2], ss[:, 1:2]),
                         reads=["ss1"], writes=["ss1"])
                    S.op("dve", lambda e, hh=hh: e.scalar_tensor_tensor(xn[:], hh[:], ss[:, 1:2], nw[:], ALU.mult, ALU.mult),
                         reads=[hk, "ss1", "nw"], writes=["xn"])
                    yield
                    for q4 in range(4):
                        pt = ptr[q4 % 2]
                        pk = f"ptr{q4 % 2}"

                        def tr(e, pt=pt, q4=q4):
                            r = None
                            for j in range(4):
                                kc = q4 * 4 + j
                                r = e.transpose(pt[:, j, :], xn[:, kc * 128:(kc + 1) * 128], identb[:])
                            return r
                        S.op("pe", tr, reads=["xn", "identb"], writes=[pk])
                        if q4 % 2 == 0:
                            S.op("act", lambda e, pt=pt, q4=q4, bi=bi: e.copy(xnT[:, q4 * 4:(q4 + 1) * 4, bi * 128:(bi + 1) * 128], pt[:, 0:4, :]),
                                 reads=[pk], writes=[f"xnT{gp}_{bi}"])
                        else:
                            S.op("dve", lambda e, pt=pt, q4=q4, bi=bi: e.tensor_copy(xnT[:, q4 * 4:(q4 + 1) * 4, bi * 128:(bi + 1) * 128], pt[:, 0:4, :]),
                                 reads=[pk], writes=[f"xnT{gp}_{bi}"])
                        if q4 % 2 == 1:
                            yield

            def drain(gen, n=None):
                if gen is None:
                    return
                k = 0
                for _ in gen:
                    k += 1
                    if n is not None and k >= n:
                        return

            drain(norm_gen(0))
            for gi in range(ngroups):
                gp = gi % 2
                xnT = xnT2[gp]
                b0 = gi * G
                gb = min(G, NB - b0)
                n = gb * 128
                nxt = norm_gen(gi + 1) if gi + 1 < ngroups else None
                xk = [f"xnT{gp}_{bi}" for bi in range(gb)]
                for u in range(14):
                    w = wt[u % 2]
                    wk = f"wt{u % 2}"
                    S.dma("sp", w[:], Wb_in[l, :, u * 512:(u + 1) * 512].rearrange("(kc p) c -> p kc c", p=128),
                          reads=[f"Wb_in{l}"], writes=[wk])
                    for c4 in range(4):
                        col = u * 512 + c4 * 128
                        ch = col // 128
                        if 3072 <= col < 3328:
                            continue
                        p = pa[ecnt[0] % 2]
                        pk = f"pa{ecnt[0] % 2}"
                        ecnt[0] += 1

                        def mm(e, p=p, w=w, c4=c4, n=n, xnT=xnT):
                            r = None
                            for kc in range(KC):
                                r = e.matmul(p[:, 0:n], w[:, kc, c4 * 128:(c4 + 1) * 128], xnT[:, kc, 0:n],
                                             start=(kc == 0), stop=(kc == KC - 1))
                            return r
                        S.op("pe", mm, reads=[wk] + xk, writes=[pk])
                        tcols = slice(b0 * 128, b0 * 128 + n)
                        i3 = ch % 3
                        if col < 512:
                            S.op("act", lambda e, p=p, ch=ch, n=n: e.copy(cxs[:, ch, 0:n], p[:, 0:n]),
                                 reads=[pk], writes=[f"cxs{ch}"])
                        elif col < 1024:
                            S.op("act", lambda e, p=p, i3=i3, n=n: e.copy(ef[i3][:, 0:n], p[:, 0:n]),
                                 reads=[pk], writes=[f"ef{i3}"])
                            S.dma("pool", CB[col - 512:col - 512 + 128, tcols], ef[i3][:, 0:n], reads=[f"ef{i3}"], writes=[ukey("CB")])
                        elif col < 1536:
                            cxi = (col - 1024) // 128
                            S.op("dve", lambda e, p=p, i3=i3, n=n, cxi=cxi: e.tensor_tensor(ef[i3][:, 0:n], p[:, 0:n], cxs[:, cxi, 0:n], ALU.mult),
                                 reads=[pk, f"cxs{cxi}"], writes=[f"ef{i3}"])
                            S.dma("pool", UC[col - 1024:col - 1024 + 128, tcols], ef[i3][:, 0:n], reads=[f"ef{i3}"], writes=[ukey("UC")])
                        elif col < 2048 or 3328 <= col < 4096 or 6400 <= col < 7168:
                            if col < 2048:
                                zr = col - 1536
                            elif col < 4096:
                                zr = 512 + col - 3328
                            else:
                                zr = 1280 + col - 6400
                            S.op("act", lambda e, p=p, i3=i3, n=n: e.activation(eb[i3][:, 0:n], p[:, 0:n], AF.Silu),
                                 reads=[pk], writes=[f"eb{i3}"])
                            S.dma("pool", ZS[zr:zr + 128, tcols], eb[i3][:, 0:n], reads=[f"eb{i3}"], writes=[ukey("ZS")])
                        elif col < 2816:
                            S.op("act", lambda e, p=p, i3=i3, n=n: e.activation(eb[i3][:, 0:n], p[:, 0:n], AF.Copy, scale=128.0 ** -0.5),
                                 reads=[pk], writes=[f"eb{i3}"])
                            S.dma("pool", QT[col - 2048:col - 2048 + 128, tcols], eb[i3][:, 0:n], reads=[f"eb{i3}"], writes=[ukey("QT")])
                        elif col < 3072:
                            S.op("dve", lambda e, p=p, i3=i3, n=n: e.tensor_copy(eb[i3][:, 0:n], p[:, 0:n]),
                                 reads=[pk], writes=[f"eb{i3}"])
                            S.dma("pool", KT[col - 2816:col - 2816 + 128, tcols], eb[i3][:, 0:n], reads=[f"eb{i3}"], writes=[ukey("KT")])
                        else:
                            r0 = col - 4096
                            S.op("dve", lambda e, p=p, i3=i3, n=n: e.tensor_copy(ef[i3][:, 0:n], p[:, 0:n]),
                                 reads=[pk], writes=[f"ef{i3}"])
                            S.dma("pool", DQKV[r0:r0 + 128, tcols], ef[i3][:, 0:n], reads=[f"ef{i3}"], writes=[ukey("DQKV")])
                    if u == 6:
                        for bi in range(gb):
                            b = b0 + bi

                            def mmv(e, w=w, bi=bi, xnT=xnT):
                                r = None
                                for kc in range(KC):
                                    r = e.matmul(pv, xnT[:, kc, bi * 128:(bi + 1) * 128], w[:, kc, 0:256],
                                                 start=(kc == 0), stop=(kc == KC - 1))
                                return r
                            S.op("pe", mmv, reads=[wk, f"xnT{gp}_{bi}"], writes=["pv"])
                            S.op("act", lambda e: e.copy(vtok[:], pv), reads=["pv"], writes=["vtok"])
                            S.dma("pool", VV[b * 128:(b + 1) * 128, :], vtok[:], reads=["vtok"], writes=[ukey("VV")])
                    if u >= 2:
                        drain(nxt, 1)
                for bi in range(gb):
                    b = b0 + bi

                    def mmb(e, bi=bi, xnT=xnT):
                        r = None
                        for kc in range(KC):
                            r = e.matmul(pbg, xnT[:, kc, bi * 128:(bi + 1) * 128], wlast[:, kc, :],
                                         start=(kc == 0), stop=(kc == KC - 1))
                        return r
                    S.op("pe", mmb, reads=["wlast", f"xnT{gp}_{bi}"], writes=["pbg"])
                    S.op("dve", lambda e: e.tensor_copy(bgt[:], pbg), reads=["pbg"], writes=["bgt"])
                    S.dma("pool", BG[b * 128:(b + 1) * 128, :], bgt[:], reads=["bgt"], writes=[ukey("BG")])
                drain(nxt)

        if stop == 'A':
            S.close()
            return nc
        with S.phase():
            def T2(name, shape, dt):
                return [S.sb(f"{name}{i}", shape, dt) for i in range(2)]
            raw = T2("raw", [128, 18, 130], F32)
            cv = T2("cv", [128, 18, 128], F32)
            sq = T2("sq", [128, 12, 128], BF16)
            rs = T2("rs", [128, 12, 128], F32)
            vB = T2("vB", [128, 128], F32)
            kq = T2("kq", [128, 6, 2, 128], BF16)
            vT = T2("vT", [128, 6, 128], BF16)
            kv_tok = T2("kv_tok", [128, 12, 128], BF16)
            bgt2 = T2("bgt2", [128, 24], F32)
            spt = T2("spt", [128, 12], F32)
            pss = S.ps("pss", [128, 12, 128], F32, nbanks=3)
            ptk_ = S.ps("ptk", [128, 16, 128], BF16, nbanks=2)

            def a2_block(b):
                par = b % 2
                P = str(par)
                raw_, cv_, sq_, rs_, vB_, kq_, vT_, kvt_, bg_, sp_ = (raw[par], cv[par], sq[par], rs[par], vB[par], kq[par],
                                                                     vT[par], kv_tok[par], bgt2[par], spt[par])
                t0 = b * 128
                lo = 1 if b == 0 else 0
                hi = 129 if b == NB - 1 else 130
                if lo == 1:
                    S.op("pool", lambda e: e.memset(raw_[:, :, 0:1], 0.0), writes=["raw" + P])
                if hi == 129:
                    S.op("pool", lambda e: e.memset(raw_[:, :, 129:130], 0.0), writes=["raw" + P])
                S.dma("sp", raw_[:, :, lo:hi],
                      DQKV[:, t0 - 1 + lo:t0 - 1 + hi].rearrange("(c p) t -> p c t", p=128),
                      reads=["DQKV"], writes=["raw" + P])
                S.dma("sp", vB_[:], validB_d[:, t0:t0 + 128], writes=["vB" + P])
                S.dma("sp", bg_[:], BG[t0:t0 + 128, :], reads=["BG"], writes=["bgt2" + P])
                S.op("act", lambda e: e.activation(betaT[:, b, :], bg_[:, 0:12], AF.Sigmoid),
                     reads=["bgt2" + P], writes=["betaT"])
                S.op("dve", lambda e: e.tensor_scalar(nbetaT[:, b, :], betaT[:, b, :], -1.0, None, ALU.mult),
                     reads=["betaT"], writes=["nbetaT"])
                S.op("dve", lambda e: e.tensor_tensor(sp_[:], bg_[:, 12:24], dtbB[:, l, :], ALU.add),
                     reads=["bgt2" + P, "dtbB"], writes=["spt" + P])
                S.op("act", lambda e: e.activation(sp_[:], sp_[:], AF.Exp), reads=["spt" + P], writes=["spt" + P])
                S.op("act", lambda e: e.activation(sp_[:], sp_[:], AF.Ln, bias=onec[:, 0:1]), reads=["spt" + P, "onec"], writes=["spt" + P])
                S.op("dve", lambda e: e.tensor_tensor(gT[:, b, :], sp_[:], negA[:, l, :], ALU.mult),
                     reads=["spt" + P, "negA"], writes=["gT"])
                for c in range(18):
                    ck = f"cv{P}_{c}"
                    S.op("act", lambda e, c=c: e.activation(cv_[:, c, :], raw_[:, c, 1:129], AF.Copy, scale=dcw[:, l, c, 1:2]),
                         reads=["raw" + P, "dcw"], writes=[ck])
                    S.op("dve", lambda e, c=c: e.scalar_tensor_tensor(cv_[:, c, :], raw_[:, c, 0:128], dcw[:, l, c, 0:1], cv_[:, c, :], ALU.mult, ALU.add),
                         reads=["raw" + P, ck], writes=[ck])
                    S.op("dve", lambda e, c=c: e.scalar_tensor_tensor(cv_[:, c, :], raw_[:, c, 2:130], dcw[:, l, c, 2:3], cv_[:, c, :], ALU.mult, ALU.add),
                         reads=["raw" + P, ck], writes=[ck])
                cvk = [f"cv{P}_{c}" for c in range(18)]
                S.op("act", lambda e: e.activation(cv_[:], cv_[:], AF.Silu), reads=cvk, writes=cvk)
                S.op("act", lambda e: e.activation(sq_[:], cv_[:, 0:12, :], AF.Square), reads=cvk, writes=["sq" + P])

                def mss(e):
                    r = None
                    for j in range(3):
                        r = e.matmul(pss[:, j * 4:(j + 1) * 4, :], onesb[:], sq_[:, j * 4:(j + 1) * 4, :], start=True, stop=True)
                    return r
                S.op("pe", mss, reads=["sq" + P, "onesb"], writes=["pss"])
                S.op("act", lambda e: e.activation(rs_[:], pss[:], AF.Ln, bias=epsc[:, 0:1], scale=1.0),
                     reads=["pss", "epsc"], writes=["rs" + P])
                S.op("act", lambda e: e.activation(rs_[:], rs_[:], AF.Exp, scale=-0.5), reads=["rs" + P], writes=["rs" + P])
                S.op("dve", lambda e: e.tensor_tensor(rs_[:, 6:12, :], rs_[:, 6:12, :], vB_[:].unsqueeze(1).broadcast_to([128, 6, 128]), ALU.mult),
                     reads=["rs" + P, "vB" + P], writes=["rs" + P])
                S.op("dve", lambda e: e.scalar_tensor_tensor(kq_[:, :, 1, :], cv_[:, 0:6, :], 128.0 ** -0.5, rs_[:, 0:6, :], ALU.mult, ALU.mult),
                     reads=cvk + ["rs" + P], writes=["kq" + P])
                S.op("dve", lambda e: e.tensor_tensor(kq_[:, :, 0, :], cv_[:, 6:12, :], rs_[:, 6:12, :], ALU.mult),
                     reads=cvk + ["rs" + P], writes=["kq" + P])
                S.op("dve", lambda e: e.tensor_copy(vT_[:], cv_[:, 12:18, :]), reads=cvk, writes=["vT" + P])

                def trk(e):
                    r = None
                    for h in range(6):
                        r = e.transpose(ptk_[:, h, :], kq_[:, h, 0, :], identb[:])
                    for h in range(6):
                        r = e.transpose(ptk_[:, 6 + h, :], vT_[:, h, :], identb[:])
                    return r
                S.op("pe", trk, reads=["kq" + P, "vT" + P, "identb"], writes=["ptk"])
                S.op("act", lambda e: e.copy(kvt_[:], ptk_[:, 0:12, :]), reads=["ptk"], writes=["kv_tok" + P])
                S.dma("sp", KQT[b], kq_[:].rearrange("p h two t -> p (h two t)"), reads=["kq" + P], writes=["KQT"])
                S.dma("sp", KN[b], kvt_[:, 0:6, :].rearrange("p h d -> p (h d)"), reads=["kv_tok" + P], writes=["KN"])
                S.dma("sp", VN[b], kvt_[:, 6:12, :].rearrange("p h d -> p (h d)"), reads=["kv_tok" + P], writes=["VN"])

            for b in range(NB):
                a2_block(b)
                if b == min(1, NB - 1) and l + 1 < L:
                    convert_weights(l + 1)

        if stop == 'A2':
            S.close()
            return nc
        with S.phase():
            def T2(name, shape, dt):
                return [S.sb(f"{name}{i}", shape, dt) for i in range(2)]
            kqt = T2("kqt", [128, 6, 2, 128], BF16)
            knt = T2("knt", [128, 6, 128], BF16)
            vnt = T2("vnt", [128, 6, 128], BF16)
            gcs = T2("gcs", [128, 12], F32)
            ngc = T2("ngc", [128, 6], F32)
            egs = T2("egs", [128, 18], F32)
            gU = T2("gU", [128, 6, 128], F32)
            GCs = T2("GCs", [128, 6, 128], F32)
            Et = T2("Et", [128, 6, 128], F32)
            DT = T2("DT", [128, 6, 128], F32)
            EG = T2("EG", [128, 6, 128], F32)
            tt = T2("tt", [128, 6, 128], F32)
            qg = T2("qg", [128, 6, 128], BF16)
            MT = T2("MT", [128, 6, 128], BF16)
            X = [[S.sb(f"X{p}{i}", [128, 6, 3, 128], F32R) for i in range(2)] for p in range(2)]
            identr = S.sb("identr", [128, 128], F32R)
            S.op("dve", lambda e: e.tensor_copy(identr[:], identf[:]), reads=["identf"], writes=["identr"])
            Pb = T2("Pb", [128, 6, 128], BF16)
            kg = T2("kg", [128, 6, 128], BF16)
            kt = T2("kt", [128, 6, 128], BF16)
            nWT = S.sb("nWT", [128, 6, 128], BF16)
            vnew = S.sb("vnew", [128, 6, 128], BF16)
            Sf = S.sb("Sf", [128, 6, 128], F32)
            Sb = S.sb("Sb", [128, 6, 128], BF16)
            of = T2("of", [128, 6, 128], F32)
            ofl = T2("ofl", [128, 6, 128], F32)
            osq = S.sb("osq", [128, 6, 128], F32)
            zsd = T2("zsd", [128, 6, 128], BF16)
            ydn = S.sb("ydn", [128, 6, 128], BF16)
            rn = S.sb("rn", [128, 12], F32)
            Hps = [S.ps(f"H{h}", [128, 4, 128], F32) for h in range(6)]
            QA = S.ps("QA", [128, 4, 128], F32)
            QB = S.ps("QB", [128, 4, 128], F32)
            onf = S.sb("onf", [128, 6, 128], F32)

            def part0(dirn, b, par):
                S.dma("sp", kqt[par][:].rearrange("p h two t -> p (h two t)"), KQT[b], reads=["KQT"], writes=[f"kqt{par}"])
                S.dma("sp", knt[par][:].rearrange("p h d -> p (h d)"), KN[b], reads=["KN"], writes=[f"knt{par}"])
                S.dma("sp", vnt[par][:].rearrange("p h d -> p (h d)"), VN[b], reads=["VN"], writes=[f"vnt{par}"])
                if dirn == 1:
                    S.dma("sp", ofl[par][:].rearrange("p h d -> p (h d)"), OF[b], reads=["OF"], writes=[f"ofl{par}"])
                    S.dma("sp", zsd[par][:], ZS[1280:2048, b * 128:(b + 1) * 128].rearrange("(h p) t -> p h t", p=128),
                          reads=["ZS"], writes=[f"zsd{par}"])
                gsl = gT[:, b, dirn * 6:(dirn + 1) * 6]

                def mgc(e):
                    e.matmul(Hps[0][:, 3, 0:6], Uf[:, dirn, :], gsl, start=True, stop=True)
                    return e.matmul(Hps[0][:, 3, 6:12], onesf[:], gsl, start=True, stop=True)
                S.op("pe", mgc, reads=["Uf", "onesf", "gT"], writes=["H0"])
                S.op("dve", lambda e: e.tensor_copy(gcs[par][:], Hps[0][:, 3, 0:12]), reads=["H0"], writes=[f"gcs{par}"])
                S.op("dve", lambda e: e.tensor_scalar(ngc[par][:], gcs[par][:, 0:6], -1.0, None, ALU.mult), reads=[f"gcs{par}"], writes=[f"ngc{par}"])
                S.op("dve", lambda e: e.tensor_tensor(egs[par][:, 6:12], gcs[par][:, 6:12], gcs[par][:, 0:6], ALU.subtract), reads=[f"gcs{par}"], writes=[f"egs1{par}"])
                S.op("act", lambda e: e.activation(egs[par][:, 0:6], gcs[par][:, 0:6], AF.Exp), reads=[f"gcs{par}"], writes=[f"egs0{par}"])
                S.op("act", lambda e: e.activation(egs[par][:, 6:12], egs[par][:, 6:12], AF.Exp), reads=[f"egs1{par}"], writes=[f"egs1{par}"])
                S.op("act", lambda e: e.activation(egs[par][:, 12:18], gcs[par][:, 6:12], AF.Exp), reads=[f"gcs{par}"], writes=[f"egs2{par}"])

            def part1(dirn, b, par, h):
                sfx = f"{par}_{h}"
                Hh = Hps[h]
                hk = f"H{h}"
                kq_, kn_ = kqt[par], knt[par]
                gsl = gT[:, b, dirn * 6:(dirn + 1) * 6]
                nb_ = nbetaT[:, b, dirn * 6:(dirn + 1) * 6]
                Ud = Uf[:, dirn, :]
                Xa, Xb = X[par]
                S.op("act", lambda e: e.activation(gU[par][:, h, :], Ud, AF.Copy, scale=gsl[:, h:h + 1]),
                     reads=["Uf", "gT"], writes=[f"gU{sfx}"])
                S.op("act", lambda e: e.copy(Xa[:, h, 2, :], identf[:]), reads=["identf"], writes=[f"XaP{sfx}"])
                yield

                def m1(e):
                    e.matmul(Hh[:, 3, :], onesf[:], gU[par][:, h, :], start=True, stop=True)
                    return e.matmul(Hh[:, 1:3, :], kq_[:, h, 0, :], kq_[:, h, :, :], start=True, stop=True)
                S.op("pe", m1, reads=[f"gU{sfx}", "onesf", f"kqt{par}"], writes=[hk])
                yield
                S.op("dve", lambda e: e.tensor_tensor(Et[par][:, h, :], Hh[:, 3, :], maskB[:, dirn, :], ALU.add),
                     reads=[hk, "maskB"], writes=[f"Et{sfx}"])
                S.op("act", lambda e: e.activation(EG[par][:, h, :], Hh[:, 3, :], AF.Exp), reads=[hk], writes=[f"EG{sfx}"])
                yield
                S.op("act", lambda e: e.activation(DT[par][:, h, :], Et[par][:, h, :], AF.Exp, bias=ngc[par][:, h:h + 1]),
                     reads=[f"Et{sfx}", f"ngc{par}"], writes=[f"DT{sfx}"])
                S.op("dve", lambda e: e.tensor_tensor(qg[par][:, h, :], kq_[:, h, 1, :], EG[par][:, h, :], ALU.mult),
                     reads=[f"kqt{par}", f"EG{sfx}"], writes=[f"qg{sfx}"])
                S.op("act", lambda e: e.activation(kg[par][:, h, :], kn_[:, h, :], AF.Copy, scale=egs[par][:, h:h + 1]),
                     reads=[f"knt{par}", f"egs0{par}"], writes=[f"kg{sfx}"])
                S.op("act", lambda e: e.activation(kt[par][:, h, :], kn_[:, h, :], AF.Copy, scale=egs[par][:, 6 + h:7 + h]),
                     reads=[f"knt{par}", f"egs1{par}"], writes=[f"kt{sfx}"])
                yield
                S.op("pool", lambda e: e.tensor_tensor(tt[par][:, h, :], DT[par][:, h, :], strictM[:, dirn, :], ALU.mult),
                     reads=[f"DT{sfx}", "strictM"], writes=[f"tt{sfx}"])
                S.op("dve", lambda e: e.tensor_tensor(MT[par][:, h, :], Hh[:, 2, :], DT[par][:, h, :], ALU.mult),
                     reads=[hk, f"DT{sfx}"], writes=[f"MT{sfx}"])
                yield
                S.op("dve", lambda e: e.scalar_tensor_tensor(Xa[:, h, 1, :], Hh[:, 1, :], nb_[:, h:h + 1], tt[par][:, h, :], ALU.mult, ALU.mult),
                     reads=[hk, f"tt{sfx}", "nbetaT"], writes=[f"XaQ{sfx}"])
                yield
                S.op("pe", lambda e: e.matmul(Hh[:, 0, :], Xa[:, h, 1, :], identr[:], start=True, stop=True), reads=[f"XaQ{sfx}", "identr"], writes=[hk])
                yield
                S.op("act", lambda e: e.copy(Xa[:, h, 0, :], Hh[:, 0, :]), reads=[hk], writes=[f"XaT{sfx}"])
                yield
                cur = 0
                for lev in range(7):
                    Xc, Xn_ = (Xa, Xb) if cur == 0 else (Xb, Xa)
                    cc_ = "Xa" if cur == 0 else "Xb"
                    nn_ = "Xb" if cur == 0 else "Xa"

                    def mAB(e, Xc=Xc, lev=lev):
                        if lev < 6:
                            e.matmul(Hh[:, 1:3, :], Xc[:, h, 0, :], Xc[:, h, 1:3, :], start=True, stop=True)
                            return e.matmul(Hh[:, 0, :], Xc[:, h, 1, :], Xc[:, h, 0, :], start=True, stop=True)
                        return e.matmul(Hh[:, 2, :], Xc[:, h, 0, :], Xc[:, h, 2, :], start=True, stop=True)
                    S.op("pe", mAB, reads=[cc_ + "Q" + sfx, cc_ + "T" + sfx, cc_ + "P" + sfx], writes=[hk])
                    yield
                    if lev < 6:
                        S.op("act", lambda e, Xn_=Xn_: e.copy(Xn_[:, h, 0:2, :], Hh[:, 0:2, :]), reads=[hk],
                             writes=[nn_ + "Q" + sfx, nn_ + "T" + sfx])
                        S.op("dve", lambda e, Xn_=Xn_, Xc=Xc: e.tensor_tensor(Xn_[:, h, 2, :], Hh[:, 2, :], Xc[:, h, 2, :], ALU.add),
                             reads=[hk, cc_ + "P" + sfx], writes=[nn_ + "P" + sfx])
                    else:
                        S.op("dve", lambda e, Xc=Xc: e.tensor_tensor(Pb[par][:, h, :], Hh[:, 2, :], Xc[:, h, 2, :], ALU.add),
                             reads=[hk, cc_ + "P" + sfx], writes=[f"Pb{sfx}"])
                    yield
                    cur = 1 - cur

            def part2(dirn, b, par, h):
                sfx = f"{par}_{h}"
                Qp, qk = (QA, "QA") if h % 2 == 0 else (QB, "QB")
                bt_ = betaT[:, b, dirn * 6:(dirn + 1) * 6]
                S.op("pe", lambda e: e.matmul(Qp[:, 0, :], kg[par][:, h, :], Pb[par][:, h, :], start=True, stop=True),
                     reads=[f"kg{sfx}", f"Pb{sfx}", qk], writes=[qk + "W"])
                yield
                S.op("dve", lambda e: e.tensor_scalar(nWT[:, h, :], Qp[:, 0, :], -1.0, None, ALU.mult), reads=[qk + "W", qk], writes=[f"nWT{h}"])
                yield

                def mV(e):
                    e.matmul(Qp[:, 1, :], Pb[par][:, h, :], vnt[par][:, h, :], start=True, stop=False)
                    return e.matmul(Qp[:, 1, :], nWT[:, h, :], Sb[:, h, :], start=False, stop=True)
                S.op("pe", mV, reads=[f"Pb{sfx}", f"vnt{par}", f"nWT{h}", f"Sb{h}", qk], writes=[qk + "V"])
                yield
                S.op("dve", lambda e: e.tensor_scalar(vnew[:, h, :], Qp[:, 1, :], bt_[:, h:h + 1], None, ALU.mult),
                     reads=[qk + "V", qk, "betaT"], writes=[f"vnew{h}"])
                yield

                def mOS(e):
                    e.matmul(Qp[:, 2, :], qg[par][:, h, :], Sb[:, h, :], start=True, stop=False)
                    e.matmul(Qp[:, 2, :], MT[par][:, h, :], vnew[:, h, :], start=False, stop=True)
                    return e.matmul(Qp[:, 3, :], kt[par][:, h, :], vnew[:, h, :], start=True, stop=True)
                S.op("pe", mOS, reads=[f"qg{sfx}", f"Sb{h}", f"MT{sfx}", f"vnew{h}", f"kt{sfx}", qk], writes=[qk + "O", qk + "S"])
                yield
                S.op("dve", lambda e: e.scalar_tensor_tensor(Sf[:, h, :], Sf[:, h, :], egs[par][:, 12 + h:13 + h], Qp[:, 3, :], ALU.mult, ALU.add),
                     reads=[f"Sf{h}", f"egs2{par}", qk + "S", qk], writes=[f"Sf{h}"])
                if dirn == 0:
                    S.op("dve", lambda e: e.tensor_copy(of[par][:, h, :], Qp[:, 2, :]), reads=[qk + "O", qk], writes=[f"of{sfx}"])
                else:
                    S.op("dve", lambda e: e.tensor_tensor(of[par][:, h, :], Qp[:, 2, :], ofl[par][:, h, :], ALU.add),
                         reads=[qk + "O", qk, f"ofl{par}"], writes=[f"of{sfx}"])
                yield
                S.op("act", lambda e: e.copy(Sb[:, h, :], Sf[:, h, :]), reads=[f"Sf{h}"], writes=[f"Sb{h}"])
                yield

            def chain(*gs):
                for g_ in gs:
                    yield from g_

            def part3(dirn, b, par):
                ofk = [f"of{par}_{h}" for h in range(6)]
                if dirn == 0:
                    S.dma("sp", OF[b], of[par][:].rearrange("p h d -> p (h d)"), reads=ofk, writes=["OF"])
                    return
                o_ = of[par]
                S.op("pool", lambda e: e.tensor_tensor(osq[:], o_[:], o_[:], ALU.mult), reads=ofk, writes=["osq"])
                S.op("dve", lambda e: e.tensor_reduce(rn[:, 0:6], osq[:], mybir.AxisListType.X, ALU.add), reads=["osq"], writes=["rn0"])
                S.op("act", lambda e: e.activation(rn[:, 6:12], rn[:, 0:6], AF.Sqrt, bias=epsc[:, 0:1], scale=1.0 / 128), reads=["rn0", "epsc"], writes=["rn1"])
                S.op("dve", lambda e: e.reciprocal(rn[:, 6:12], rn[:, 6:12]), reads=["rn1"], writes=["rn1"])
                for h in range(6):
                    S.op("pool", lambda e, h=h: e.tensor_scalar(osq[:, h, :], o_[:, h, :], rn[:, 6 + h:7 + h], None, ALU.mult),
                         reads=ofk + ["rn1"], writes=["osq"])
                S.op("pool", lambda e: e.tensor_tensor(onf[:], osq[:], dnwB[:, l, :].unsqueeze(1).broadcast_to([128, 6, 128]), ALU.mult),
                     reads=["osq", "dnwB"], writes=["onf"])

                def trO(e):
                    r = None
                    for h in range(3):
                        r = e.transpose(QA[:, h, :], onf[:, h, :], identf[:])
                    for h in range(3):
                        r = e.transpose(QB[:, h, :], onf[:, 3 + h, :], identf[:])
                    return r
                S.op("pe", trO, reads=["onf", "identf", "QA", "QB"], writes=["QAW", "QAV", "QAO", "QBW", "QBV", "QBO"])
                S.op("dve", lambda e: e.tensor_tensor(ydn[:, 0:3, :], QA[:, 0:3, :], zsd[par][:, 0:3, :], ALU.mult),
                     reads=["QAW", "QAV", "QAO", "QA", f"zsd{par}"], writes=["ydn"])
                S.op("dve", lambda e: e.tensor_tensor(ydn[:, 3:6, :], QB[:, 0:3, :], zsd[par][:, 3:6, :], ALU.mult),
                     reads=["QBW", "QBV", "QBO", "QB", f"zsd{par}"], writes=["ydn"])
                S.dma("sp", YDN[b], ydn[:].rearrange("p h t -> p (h t)"), reads=["ydn"], writes=["YDN"])

            rrcnt = [0]

            def rr(gens):
                gens = list(gens)
                while gens:
                    alive = []
                    for gn in gens:
                        if BCUT and rrcnt[0] >= BCUT:
                            return
                        rrcnt[0] += 1
                        try:
                            next(gn)
                            alive.append(gn)
                        except StopIteration:
                            pass
                    gens = alive

            for dirn in range(2):
                S.op("pool", lambda e: e.memset(Sf[:], 0.0), writes=[f"Sf{h}" for h in range(6)])
                S.op("pool", lambda e: e.memset(Sb[:], 0.0), writes=[f"Sb{h}" for h in range(6)])
                blocks = list(range(NB)) if dirn == 0 else list(range(NB - 1, -1, -1))
                for i, b in enumerate(blocks):
                    par = i % 2
                    part0(dirn, b, par)
                    gl = []
                    if i > 0:
                        pb, pp = blocks[i - 1], 1 - par
                        gl.append(chain(*[part2(dirn, pb, pp, h) for h in (0, 2, 4)]))
                        gl.append(chain(*[part2(dirn, pb, pp, h) for h in (1, 3, 5)]))
                    for h in range(6):
                        gl.append(part1(dirn, b, par, h))
                    rr(gl)
                    if i > 0:
                        part3(dirn, blocks[i - 1], 1 - par)
                lp = (len(blocks) - 1) % 2
                rr([chain(*[part2(dirn, blocks[-1], lp, h) for h in (0, 2, 4)]),
                    chain(*[part2(dirn, blocks[-1], lp, h) for h in (1, 3, 5)])])
                part3(dirn, blocks[-1], lp)

        if stop == 'B':
            S.close()
            return nc
        with S.phase(last=(l == L - 1)):
            def T2(name, shape, dt):
                return [S.sb(f"{name}{i}", shape, dt) for i in range(2)]
            wout = S.sb("wout", [128, KC, D], BF16)
            hc2 = T2("hc", [128, D], F32)
            hn2 = T2("hn", [128, D], F32)
            yo = S.sb("yo", [128, D], F32)
            junk2 = S.sb("junk2", [128, D], BF16)
            ss2 = S.sb("ss2", [128, 2], F32)
            yT2 = T2("yT", [128, 16, 128], BF16)
            uc2 = T2("uc", [128, 4, 130], F32)
            cbt2 = T2("cbt", [128, 4, 128], F32)
            cvc2 = T2("cvc", [128, 4, 128], F32)
            zst2 = T2("zst", [128, 10, 128], BF16)
            qt2 = T2("qt", [128, 6, 128], BF16)
            ktl2 = T2("ktl", [128, 2, 384], BF16)
            vl2 = T2("vl", [128, 3, 256], BF16)
            Ea2 = T2("Ea", [128, 384], F32)
            PTa2 = T2("PTa", [128, 384], BF16)
            den = S.sb("den", [128, 384], F32)
            oa = S.sb("oa", [128, 384], F32)
            pst2 = [S.ps(f"pst{i}", [128, 512], F32)[:, 0:384] for i in range(2)]
            pot = [S.ps(f"pot{g}", [128, 512], F32)[:, 0:384] for g in range(2)]
            pden = [S.ps(f"pden{g}", [128, 512], F32)[:, 0:384] for g in range(2)]
            po = [S.ps(f"po{i}", [128, 512], F32) for i in range(2)]

            S.dma("sp", wout[:], Wb_out[l].rearrange("(kc p) c -> p kc c", p=128), reads=[f"Wb_out{l}"], writes=["wout"])
            scnt = [0]

            def att_gen(b):
                par = b % 2
                P = str(par)
                hc, hn, yT, uc, cbt, cvc, zst, qt, ktl, vl = (hc2[par], hn2[par], yT2[par], uc2[par], cbt2[par], cvc2[par],
                                                              zst2[par], qt2[par], ktl2[par], vl2[par])
                t0 = b * 128
                lo = 1 if b == 0 else 0
                hi = 129 if b == NB - 1 else 130
                if lo == 1:
                    S.op("pool", lambda e: e.memset(uc[:, :, 0:1], 0.0), writes=["uc" + P])
                if hi == 129:
                    S.op("pool", lambda e: e.memset(uc[:, :, 129:130], 0.0), writes=["uc" + P])
                S.dma("sp", uc[:, :, lo:hi], UC[:, t0 - 1 + lo:t0 - 1 + hi].rearrange("(c p) t -> p c t", p=128),
                      reads=["UC"], writes=["uc" + P])
                S.dma("sp", cbt[:], CB[:, t0:t0 + 128].rearrange("(c p) t -> p c t", p=128), reads=["CB"], writes=["cbt" + P])
                S.dma("sp", zst[:], ZS[0:1280, t0:t0 + 128].rearrange("(c p) t -> p c t", p=128), reads=["ZS"], writes=["zst" + P])
                S.dma("sp", hc[:], Hin[t0:t0 + 128, :], reads=[hin_key], writes=["hc" + P])
                kbs = [kb for kb in (b - 1, b, b + 1) if 0 <= kb < NB]
                k0, k1 = kbs[0], kbs[-1]
                S.dma("sp", qt[:], QT[:, t0:t0 + 128].rearrange("(h p) t -> p h t", p=128), reads=["QT"], writes=["qt" + P])
                S.dma("sp", ktl[:, :, 0:(k1 - k0 + 1) * 128], KT[:, k0 * 128:(k1 + 1) * 128].rearrange("(g p) t -> p g t", p=128),
                      reads=["KT"], writes=["ktl" + P])
                S.dma("sp", vl[:, 0:(k1 - k0 + 1), :], VV[k0 * 128:(k1 + 1) * 128, :].rearrange("(j p) c -> p j c", p=128),
                      reads=["VV"], writes=["vl" + P])
                S.dma("sp", yT[:, 10:16, :].rearrange("p h t -> p (h t)"), YDN[b], reads=["YDN"], writes=["yT_d" + P])
                for c in range(4):
                    ck = f"cvc{P}_{c}"
                    S.op("act", lambda e, c=c: e.activation(cvc[:, c, :], uc[:, c, 1:129], AF.Copy, scale=caw[:, l, c, 1:2]),
                         reads=["uc" + P, "caw"], writes=[ck])
                    S.op("dve", lambda e, c=c: e.scalar_tensor_tensor(cvc[:, c, :], uc[:, c, 0:128], caw[:, l, c, 0:1], cvc[:, c, :], ALU.mult, ALU.add),
                         reads=["uc" + P, ck], writes=[ck])
                    S.op("dve", lambda e, c=c: e.scalar_tensor_tensor(cvc[:, c, :], uc[:, c, 2:130], caw[:, l, c, 2:3], cvc[:, c, :], ALU.mult, ALU.add),
                         reads=["uc" + P, ck], writes=[ck])
                cvck = [f"cvc{P}_{c}" for c in range(4)]
                S.op("pool", lambda e: e.tensor_tensor(cvc[:], cvc[:], cbt[:], ALU.mult), reads=cvck + ["cbt" + P], writes=cvck)
                S.op("pool", lambda e: e.tensor_tensor(yT[:, 0:4, :], cvc[:], zst[:, 0:4, :], ALU.mult), reads=cvck + ["zst" + P], writes=["yT_c" + P])
                yield
                for g in range(2):
                    for ji, kb in enumerate(kbs):
                        off = kb - b + 1
                        j = kb - k0
                        first = (ji == 0)
                        lastk = (ji == len(kbs) - 1)
                        si = scnt[0] % 2
                        scnt[0] += 1
                        pst, Ea, PTa = pst2[si], Ea2[si], PTa2[si]
                        S.op("pe", lambda e, g=g, j=j, pst=pst: e.matmul(pst, ktl[:, g, j * 128:(j + 1) * 128], qt[:, 3 * g:3 * g + 3, :], start=True, stop=True),
                             reads=["ktl" + P, "qt" + P], writes=[f"pst{si}"])
                        yield
                        S.op("dve", lambda e, g=g, off=off, pst=pst, Ea=Ea: e.tensor_tensor(Ea[:], pst, AL[:, off, g * 384:(g + 1) * 384], ALU.add),
                             reads=[f"pst{si}", "AL"], writes=[f"Ea{si}"])
                        S.op("act", lambda e, kb=kb, Ea=Ea, PTa=PTa: e.activation(PTa[:], Ea[:], AF.Exp, bias=kbias[:, kb:kb + 1]),
                             reads=[f"Ea{si}", "kbias"], writes=[f"PTa{si}"])
                        yield

                        def mpv(e, g=g, j=j, first=first, lastk=lastk, PTa=PTa):
                            e.matmul(pot[g], vl[:, j, g * 128:(g + 1) * 128], PTa[:], start=first, stop=lastk)
                            return e.matmul(pden[g], onesb[:], PTa[:], start=first, stop=lastk)
                        S.op("pe", mpv, reads=["vl" + P, f"PTa{si}", "onesb"], writes=[f"pot{g}", f"pden{g}"])
                        yield
                    for hh in range(3):
                        h = 3 * g + hh
                        S.op("dve", lambda e, g=g, hh=hh, h=h: e.tensor_scalar(den[:, hh * 128:(hh + 1) * 128], pden[g][:, hh * 128:(hh + 1) * 128], esink[:, l, h:h + 1], None, ALU.add),
                             reads=[f"pden{g}", "esink"], writes=["den"])
                    S.op("dve", lambda e: e.reciprocal(den[:], den[:]), reads=["den"], writes=["den"])
                    S.op("dve", lambda e, g=g: e.tensor_tensor(oa[:], pot[g], den[:], ALU.mult), reads=[f"pot{g}", "den"], writes=["oa"])
                    S.op("pool", lambda e, g=g: e.tensor_tensor(yT[:, 4 + 3 * g:7 + 3 * g, :], oa[:].rearrange("p (h t) -> p h t", h=3), zst[:, 4 + 3 * g:7 + 3 * g, :], ALU.mult),
                         reads=["oa", "zst" + P], writes=[f"yT_a{g}" + P])
                    yield

            def proj_gen(b):
                par = b % 2
                P = str(par)
                hc, hn, yT = hc2[par], hn2[par], yT2[par]
                t0 = b * 128
                ykeys = ["yT_c" + P, "yT_a0" + P, "yT_a1" + P, "yT_d" + P]
                for n4 in range(4):
                    p = po[n4 % 2]
                    pk = f"po{n4 % 2}"

                    def mo(e, p=p, n4=n4):
                        r = None
                        for mc in range(16):
                            r = e.matmul(p[:], yT[:, mc, :], wout[:, mc, n4 * 512:(n4 + 1) * 512], start=(mc == 0), stop=(mc == 15))
                        return r
                    S.op("pe", mo, reads=ykeys + ["wout"], writes=[pk])
                    yield
                    S.op("dve", lambda e, p=p, n4=n4: e.scalar_tensor_tensor(hn[:, n4 * 512:(n4 + 1) * 512], p[:], validT[:, b:b + 1], hc[:, n4 * 512:(n4 + 1) * 512], ALU.mult, ALU.add),
                         reads=[pk, "validT", "hc" + P], writes=[f"hn{P}_{n4}"])
                    yield
                hnk = [f"hn{P}_{n4}" for n4 in range(4)]
                if l < L - 1:
                    S.dma("pool", H[t0:t0 + 128, :], hn[:], reads=hnk, writes=["H"])
                elif 1 <= b <= NBOUT:
                    S.op("act", lambda e: e.activation(junk2[:], hn[:], AF.Square, accum_out=ss2[:, 0:1]), reads=hnk, writes=["junk2", "ss2a"])
                    S.op("act", lambda e: e.activation(ss2[:, 1:2], ss2[:, 0:1], AF.Sqrt, bias=epsc[:, 0:1], scale=1.0 / D), reads=["ss2a", "epsc"], writes=["ss2b"])
                    S.op("dve", lambda e: e.reciprocal(ss2[:, 1:2], ss2[:, 1:2]), reads=["ss2b"], writes=["ss2b"])
                    S.op("dve", lambda e: e.scalar_tensor_tensor(yo[:], hn[:], ss2[:, 1:2], fnwB[:], ALU.mult, ALU.mult),
                         reads=hnk + ["ss2b", "fnwB"], writes=["yo"])
                    S.dma("pool", out_d[(b - 1) * 128:b * 128, :], yo[:], reads=["yo"], writes=["out"], final=True)

            def rr2(gens):
                gens = [g_ for g_ in gens if g_ is not None]
                while gens:
                    alive = []
                    for gn in gens:
                        try:
                            next(gn)
                            alive.append(gn)
                        except StopIteration:
                            pass
                    gens = alive

            rr2([att_gen(0)])
            for b in range(NB):
                rr2([proj_gen(b), att_gen(b + 1) if b + 1 < NB else None])
    S.close()
    return nc


def make_consts():
    import ml_dtypes
    i = np.arange(128)
    c = {}
    c["c_identb"] = np.eye(128, dtype=np.float32)
    U = np.zeros((128, 2, 128), np.float32)
    U[:, 0, :] = (i[:, None] <= i[None, :])
    U[:, 1, :] = (i[:, None] >= i[None, :])
    c["c_U"] = U
    mB = np.zeros((128, 2, 128), np.float32)
    mB[:, 0, :] = np.where(i[None, :] < i[:, None], NEG, 0.0)
    mB[:, 1, :] = np.where(i[None, :] > i[:, None], NEG, 0.0)
    c["c_maskB"] = mB
    st = np.zeros((128, 2, 128), np.float32)
    st[:, 0, :] = (i[None, :] > i[:, None])
    st[:, 1, :] = (i[None, :] < i[:, None])
    c["c_strict"] = st
    slopes = np.exp2(-8.0 * np.arange(1, 7, dtype=np.float32) / 6).astype(np.float32)
    AL = np.zeros((128, 3, 6, 128), np.float32)
    s = i[:, None].astype(np.float32)
    q = i[None, :].astype(np.float32)
    for off in range(3):
        dist = np.abs(q - (s + (off - 1) * 128))
        for h in range(6):
            AL[:, off, h, :] = np.where(dist <= 128, -slopes[h] * dist, NEG)
    c["c_AL"] = AL.reshape(128, 3, 768)
    return c


def prep_weights(norm_w, w_in, conv_a_w, attn_sink, dn_conv_w, dn_a_log, dn_dt_bias, dn_norm_w, w_out, final_norm_w):
    L = w_in.shape[0]
    rep = lambda a: np.ascontiguousarray(np.broadcast_to(a[None], (128,) + a.shape)).astype(np.float32)
    m = {}
    m["w_in"] = np.ascontiguousarray(w_in, dtype=np.float32)
    m["w_out"] = np.ascontiguousarray(w_out, dtype=np.float32)
    m["nwB"] = rep(np.asarray(norm_w, np.float32))
    m["fnwB"] = rep(np.asarray(final_norm_w, np.float32))
    m["caw"] = np.ascontiguousarray(np.asarray(conv_a_w, np.float32).reshape(L, 3, 4, 128).transpose(3, 0, 2, 1))
    m["dcw"] = np.ascontiguousarray(np.asarray(dn_conv_w, np.float32).reshape(L, 3, 18, 128).transpose(3, 0, 2, 1))
    m["sinkB"] = rep(np.asarray(attn_sink, np.float32))
    m["alogB"] = rep(np.asarray(dn_a_log, np.float32).reshape(L, 12))
    m["dtbB"] = rep(np.asarray(dn_dt_bias, np.float32).reshape(L, 12))
    m["dnwB"] = rep(np.asarray(dn_norm_w, np.float32))
    return m


def core_inputs(x_seq, meta_tokens, NB):
    T = NB * 128
    h0 = np.zeros((T, D), np.float32)
    valid = np.zeros((T,), np.float32)
    if x_seq is not None:
        Sx = x_seq.shape[0]
        h0[PAD:LEAD] = meta_tokens
        h0[LEAD:LEAD + Sx] = x_seq
        valid[PAD:LEAD + Sx] = 1.0
    return {
        "h0": h0,
        "validT": np.ascontiguousarray(valid.reshape(NB, 128).T),
        "validB": np.ascontiguousarray(np.broadcast_to(valid[None], (128, T))),
    }


def kernel(x_prompt, x_sample, meta_tokens, norm_w, w_in, conv_a_w, attn_sink, dn_conv_w,
           dn_a_log, dn_dt_bias, dn_norm_w, w_out, final_norm_w):
    x_prompt = np.asarray(x_prompt, np.float32)
    x_sample = np.asarray(x_sample, np.float32)
    meta_tokens = np.asarray(meta_tokens, np.float32)
    L = w_in.shape[0]
    Sp = x_prompt.shape[1]
    Ss = x_sample.shape[1]
    NB = (LEAD + max(Sp, Ss)) // 128
    NBOUT = NB - 1
    shared = prep_weights(norm_w, np.asarray(w_in), conv_a_w, attn_sink, dn_conv_w, dn_a_log, dn_dt_bias,
                          dn_norm_w, np.asarray(w_out), final_norm_w)
    shared.update(make_consts())
    seqs = [x_prompt[i] for i in range(x_prompt.shape[0])] + [x_sample[i] for i in range(x_sample.shape[0])]
    assert len(seqs) <= 8
    in_maps = []
    for c in range(8):
        m = dict(shared)
        m.update(core_inputs(seqs[c] if c < len(seqs) else None, meta_tokens, NB))
        in_maps.append(m)
    nc = build_nc(NB, L, NBOUT)
    res = run_bass_kernel_spmd(nc, in_maps, core_ids=list(range(8)))
    outs = [np.asarray(r["out"]) for r in res.results]
    nP = x_prompt.shape[0]
    y_prompt = np.stack([outs[i][:Sp] for i in range(nP)], axis=0).astype(np.float32)
    y_sample = np.stack([outs[nP + i][:Ss] for i in range(x_sample.shape[0])], axis=0).astype(np.float32)
    return (y_prompt, y_sample)
```
